# Optimizing a Trainium2 kernel written in Bass

```python
import jax
import jax.numpy as jnp
from jax import lax
import numpy as np

D_MODEL = 2048
BATCH = 8
SEQ = 4096
DEPTH = 4

CTX_LEN = 256
GRID_W = 64
HEAD_DIM = 128
A_HEADS = D_MODEL // 2 // HEAD_DIM
A_KV_HEADS = 2
B_HEADS = D_MODEL // 4 // HEAD_DIM
B_KV_HEADS = 2
F_GROUPS = 4
F_GROUP_DIM = D_MODEL // 4 // F_GROUPS
WINDOW = 128
Q_BLOCK = 128
D_FF = 5632
CONV_W = 3
ROPE_BASE = 10000.0
EPS = 1e-6
NEG_INF = -1e30
DEEPNORM_ALPHA = (2.0 * DEPTH) ** 0.25
DEEPNORM_BETA = (8.0 * DEPTH) ** -0.25

QA_W = A_HEADS * HEAD_DIM
QB_W = B_HEADS * HEAD_DIM
KA_W = A_KV_HEADS * HEAD_DIM
KB_W = B_KV_HEADS * HEAD_DIM
F_WIDTH = F_GROUPS * F_GROUP_DIM
MIX_WIDTH = QA_W + QB_W + F_WIDTH
IN_SIZES = (QA_W, QB_W, KA_W, KA_W, KB_W, KB_W, F_WIDTH)
KV_SIZES = (KA_W, KA_W, KB_W, KB_W)
IN_WIDTH = QA_W + QB_W + 2 * KA_W + 2 * KB_W + F_WIDTH
KV_START = QA_W + QB_W
KV_END = KV_START + 2 * KA_W + 2 * KB_W

kernel_name = 'hybrid_flow_backbone_parallel_heads'


def _split(t, sizes):
    out, start = [], 0
    for s in sizes:
        out.append(t[..., start:start + s])
        start += s
    return out


def _heads(t, n_heads):
    return t.reshape(t.shape[:-1] + (n_heads, HEAD_DIM))


def layer_norm(t, gain=None, bias=None):
    tf = t.astype(jnp.float32)
    mu = jnp.mean(tf, -1, keepdims=True)
    var = jnp.mean(jnp.square(tf - mu), -1, keepdims=True)
    y = (tf - mu) * lax.rsqrt(var + EPS)
    if gain is not None:
        y = y * gain.astype(jnp.float32) + bias.astype(jnp.float32)
    return y.astype(t.dtype)


def rms_norm(t, gain):
    tf = t.astype(jnp.float32)
    y = tf * lax.rsqrt(jnp.mean(tf * tf, -1, keepdims=True) + EPS) * gain.astype(jnp.float32)
    return y.astype(t.dtype)


def modulate(t, shift, scale):
    return layer_norm(t) * (1.0 + scale) + shift


def axial_rope_tables(n_tokens):
    rows = n_tokens // GRID_W
    row = jnp.repeat(jnp.arange(rows), GRID_W).astype(jnp.float32)
    col = jnp.tile(jnp.arange(GRID_W), rows).astype(jnp.float32)
    n_freq = HEAD_DIM // 4
    inv_freq = ROPE_BASE ** (-jnp.arange(n_freq, dtype=jnp.float32) / n_freq)
    ang = jnp.stack([row[:, None] * inv_freq, col[:, None] * inv_freq], axis=1)
    return jnp.cos(ang), jnp.sin(ang)


def apply_rope(t, cos, sin):
    n_freq = HEAD_DIM // 4
    tf = t.astype(jnp.float32).reshape(t.shape[:-1] + (2, 2, n_freq))
    a, b = tf[..., 0, :], tf[..., 1, :]
    c = cos[None, :, None]
    s = sin[None, :, None]
    out = jnp.stack([a * c - b * s, a * s + b * c], axis=-2)
    return out.reshape(t.shape).astype(t.dtype)


def softmax_with_sink(s, sink):
    m = jnp.maximum(jnp.max(s, -1, keepdims=True), sink)
    e = jnp.exp(s - m)
    return e / (jnp.sum(e, -1, keepdims=True) + jnp.exp(sink - m))


def context_attention(q, k, v, sink=None):
    bsz, n_ctx, n_heads, d = q.shape
    n_kv = k.shape[2]
    qg = q.reshape(bsz, n_ctx, n_kv, n_heads // n_kv, d)
    s = jnp.einsum('bqkgd,bskd->bkgqs', qg, k).astype(jnp.float32) * (d ** -0.5)
    if sink is None:
        p = jax.nn.softmax(s, -1)
    else:
        p = softmax_with_sink(s, sink.astype(jnp.float32).reshape(n_kv, -1)[None, :, :, None, None])
    o = jnp.einsum('bkgqs,bskd->bqkgd', p.astype(v.dtype), v)
    return o.reshape(bsz, n_ctx, n_heads * d)


def global_attention_latent(q, k, v, k_ctx, v_ctx):
    bsz, n_tok, n_heads, d = q.shape
    n_grp = n_heads // A_KV_HEADS
    n_blk = n_tok // Q_BLOCK
    kk = jnp.concatenate([k, k_ctx], axis=1)
    vv = jnp.concatenate([v, v_ctx], axis=1)
    qb = q.reshape(bsz, n_blk, Q_BLOCK, A_KV_HEADS, n_grp, d).transpose(1, 0, 2, 3, 4, 5)
    scale = d ** -0.5

    def block(qi):
        s = jnp.einsum('bqkgd,bskd->bkgqs', qi, kk).astype(jnp.float32) * scale
        p = jax.nn.softmax(s, -1).astype(vv.dtype)
        return jnp.einsum('bkgqs,bskd->bqkgd', p, vv)

    o = lax.map(block, qb)
    return o.transpose(1, 0, 2, 3, 4, 5).reshape(bsz, n_tok, n_heads * d)


def window_attention_latent(q, k, v, k_ctx, v_ctx, sink):
    bsz, n_tok, n_heads, d = q.shape
    n_grp = n_heads // B_KV_HEADS
    n_blk = n_tok // Q_BLOCK
    qb = q.reshape(bsz, n_blk, Q_BLOCK, B_KV_HEADS, n_grp, d)

    def band(t):
        tb = t.reshape(bsz, n_blk, Q_BLOCK, B_KV_HEADS, d)
        tp = jnp.pad(tb, ((0, 0), (1, 1), (0, 0), (0, 0), (0, 0)))
        return jnp.concatenate([tp[:, :-2], tp[:, 1:-1], tp[:, 2:]], axis=2)

    kb, vb = band(k), band(v)
    qpos = jnp.arange(n_tok).reshape(n_blk, Q_BLOCK)
    kpos = (jnp.arange(n_blk)[:, None] - 1) * Q_BLOCK + jnp.arange(3 * Q_BLOCK)[None, :]
    allowed = ((jnp.abs(qpos[:, :, None] - kpos[:, None, :]) <= WINDOW)
               & (kpos[:, None, :] >= 0) & (kpos[:, None, :] < n_tok))
    scale = d ** -0.5
    s_loc = jnp.einsum('bnqkgd,bnskd->bnkgqs', qb, kb).astype(jnp.float32) * scale
    s_loc = jnp.where(allowed[None, :, None, None], s_loc, NEG_INF)
    s_ctx = jnp.einsum('bnqkgd,bskd->bnkgqs', qb, k_ctx).astype(jnp.float32) * scale
    s = jnp.concatenate([s_loc, s_ctx], axis=-1)
    sink_b = sink.astype(jnp.float32).reshape(B_KV_HEADS, n_grp)[None, None, :, :, None, None]
    p = softmax_with_sink(s, sink_b).astype(v.dtype)
    o = (jnp.einsum('bnkgqs,bnskd->bnqkgd', p[..., :3 * Q_BLOCK], vb)
         + jnp.einsum('bnkgqs,bskd->bnqkgd', p[..., 3 * Q_BLOCK:], v_ctx))
    return o.reshape(bsz, n_tok, n_heads * d)


def fourier_mix(u, w_f):
    bsz, n_tok, _ = u.shape
    ug = u.astype(jnp.float32).reshape(bsz, n_tok, F_GROUPS, F_GROUP_DIM)
    z = jnp.fft.fft2(ug, axes=(1, 3), norm='ortho').real.astype(u.dtype)
    return jnp.einsum('bngc,gce->bnge', z, w_f).reshape(bsz, n_tok, F_WIDTH)


def conv_ffn(h, w_up, w_gate, conv_w, conv_b, w_down):
    n_tok = h.shape[1]
    u = h @ w_up
    g = h @ w_gate
    half = CONV_W // 2
    gp = jnp.pad(g, ((0, 0), (half, half), (0, 0)))
    g = conv_b + sum(gp[:, j:j + n_tok] * conv_w[j] for j in range(CONV_W))
    return (jax.nn.silu(g) * u) @ w_down


def setup_inputs(seed: int = 0) -> dict:
    key = jax.random.key(seed)
    ks = jax.random.split(key, 24)
    D = D_MODEL

    def nrm(k, shape, s):
        return jax.random.normal(k, shape, jnp.float32) * s

    return {
        'x': nrm(ks[0], (BATCH, SEQ, D), 1.0),
        'c': nrm(ks[1], (BATCH, D), 1.0),
        'ctx': nrm(ks[2], (BATCH, CTX_LEN, D), 1.0),
        'c_ctx': nrm(ks[3], (D,), 1.0),
        'w_mod': nrm(ks[4], (DEPTH, D, 6 * D), 0.5 * D ** -0.5),
        'b_mod': nrm(ks[5], (DEPTH, 6 * D), 0.02),
        'w_in': nrm(ks[6], (DEPTH, D, IN_WIDTH), D ** -0.5),
        'q_gain_a': 1.0 + nrm(ks[7], (DEPTH, HEAD_DIM), 0.02),
        'k_gain_a': 1.0 + nrm(ks[8], (DEPTH, HEAD_DIM), 0.02),
        'sink_b': nrm(ks[9], (DEPTH, B_HEADS), 0.5),
        'w_fourier': nrm(ks[10], (DEPTH, F_GROUPS, F_GROUP_DIM, F_GROUP_DIM), F_GROUP_DIM ** -0.5),
        'w_out': nrm(ks[11], (DEPTH, MIX_WIDTH, D), DEEPNORM_BETA * MIX_WIDTH ** -0.5),
        'ln1_g': 1.0 + nrm(ks[12], (DEPTH, D), 0.02),
        'ln1_b': nrm(ks[13], (DEPTH, D), 0.02),
        'w_up': nrm(ks[14], (DEPTH, D, D_FF), D ** -0.5),
        'w_gate': nrm(ks[15], (DEPTH, D, D_FF), D ** -0.5),
        'conv_w': nrm(ks[16], (DEPTH, CONV_W, D_FF), CONV_W ** -0.5),
        'conv_b': nrm(ks[17], (DEPTH, D_FF), 0.02),
        'w_down': nrm(ks[18], (DEPTH, D_FF, D), DEEPNORM_BETA * D_FF ** -0.5),
        'ln2_g': 1.0 + nrm(ks[19], (DEPTH, D), 0.02),
        'ln2_b': nrm(ks[20], (DEPTH, D), 0.02),
    }


def reference(x, c, ctx, c_ctx, w_mod, b_mod, w_in, q_gain_a, k_gain_a, sink_b, w_fourier,
              w_out, ln1_g, ln1_b, w_up, w_gate, conv_w, conv_b, w_down, ln2_g, ln2_b):
    n_tok = x.shape[1]
    cos, sin = axial_rope_tables(n_tok)
    for l in range(DEPTH):
        last = l == DEPTH - 1
        mod = (jax.nn.silu(c) @ w_mod[l] + b_mod[l])[:, None, :]
        sh1, sc1, g1, sh2, sc2, g2 = jnp.split(mod, 6, axis=-1)

        if last:
            mod_c = jax.nn.silu(c_ctx) @ w_mod[l][:, :2 * D_MODEL] + b_mod[l][:2 * D_MODEL]
            csh1, csc1 = jnp.split(mod_c, 2)
            hc = modulate(ctx, csh1, csc1)
            pc_kv = hc @ w_in[l][:, KV_START:KV_END]
        else:
            mod_c = jax.nn.silu(c_ctx) @ w_mod[l] + b_mod[l]
            csh1, csc1, cg1, csh2, csc2, cg2 = jnp.split(mod_c, 6)
            hc = modulate(ctx, csh1, csc1)
            pc = hc @ w_in[l]
            pc_kv = pc[..., KV_START:KV_END]
        kA_c, vA_c, kB_c, vB_c = _split(pc_kv, KV_SIZES)
        kA_c = rms_norm(_heads(kA_c, A_KV_HEADS), k_gain_a[l])
        vA_c = _heads(vA_c, A_KV_HEADS)
        kB_c = _heads(kB_c, B_KV_HEADS)
        vB_c = _heads(vB_c, B_KV_HEADS)

        h = modulate(x, sh1, sc1)
        qA, qB, kA, vA, kB, vB, uF = _split(h @ w_in[l], IN_SIZES)
        qA = apply_rope(rms_norm(_heads(qA, A_HEADS), q_gain_a[l]), cos, sin)
        kA = apply_rope(rms_norm(_heads(kA, A_KV_HEADS), k_gain_a[l]), cos, sin)
        qB = apply_rope(_heads(qB, B_HEADS), cos, sin)
        kB = apply_rope(_heads(kB, B_KV_HEADS), cos, sin)
        oA = global_attention_latent(qA, kA, _heads(vA, A_KV_HEADS), kA_c, vA_c)
        oB = window_attention_latent(qB, kB, _heads(vB, B_KV_HEADS), kB_c, vB_c, sink_b[l])
        oF = fourier_mix(uF, w_fourier[l])
        y = jnp.concatenate([oA, oB, oF], axis=-1) @ w_out[l]
        x = layer_norm(DEEPNORM_ALPHA * x + g1 * y, ln1_g[l], ln1_b[l])
        f = conv_ffn(modulate(x, sh2, sc2), w_up[l], w_gate[l], conv_w[l], conv_b[l], w_down[l])
        x = layer_norm(DEEPNORM_ALPHA * x + g2 * f, ln2_g[l], ln2_b[l])

        if not last:
            qA_c, qB_c, _, _, _, _, uF_c = _split(pc, IN_SIZES)
            oA_c = context_attention(rms_norm(_heads(qA_c, A_HEADS), q_gain_a[l]), kA_c, vA_c)
            oB_c = context_attention(_heads(qB_c, B_HEADS), kB_c, vB_c, sink_b[l])
            oF_c = fourier_mix(uF_c, w_fourier[l])
            yc = jnp.concatenate([oA_c, oB_c, oF_c], axis=-1) @ w_out[l]
            ctx1 = layer_norm(DEEPNORM_ALPHA * ctx + cg1 * yc, ln1_g[l], ln1_b[l])
            fc = conv_ffn(modulate(ctx1, csh2, csc2), w_up[l], w_gate[l], conv_w[l], conv_b[l], w_down[l])
            ctx = layer_norm(DEEPNORM_ALPHA * ctx1 + cg2 * fc, ln2_g[l], ln2_b[l])
    return x
```

```python
from contextlib import ExitStack
import numpy as np
import ml_dtypes
import concourse.bass as bass
import concourse.mybir as mybir
from concourse.bass_utils import run_bass_kernel_spmd

F32 = mybir.dt.float32
BF16 = mybir.dt.bfloat16
AF = mybir.ActivationFunctionType
ALU = mybir.AluOpType
AX = mybir.AxisListType

D = 2048
KC = 16
L = 256
HD = 128
FF = 5632
FC = 44
INW = 3072
EPS = 1e-6
ALPHA = (2.0 * 4) ** 0.25
QSCALE = 128.0 ** -0.5
NEG = -30000.0
NSL = 4
FCS = FC // NSL


class Buf:
    __slots__ = ("w", "r")

    def __init__(self):
        self.w = {}
        self.r = {}


class Sem:
    __slots__ = ("h", "cnt", "step", "sid")

    def __init__(self, h, step, sid):
        self.h = h
        self.cnt = 0
        self.step = step
        self.sid = sid


class Seq:
    def __init__(self, eng, name, inorder=False):
        self.eng = eng
        self.name = name
        self.seen = {}
        self.csem = None
        self.dsems = []
        self.rr = 0
        self.inorder = inorder


class T:
    def __init__(self, h, b=None):
        self.h = h
        self.b = b if b is not None else Buf()

    def __getitem__(self, key):
        return self.h[key]


class Ring:
    def __init__(self, tiles):
        self.tiles = tiles
        self.i = 0

    def next(self):
        t = self.tiles[self.i % len(self.tiles)]
        self.i += 1
        return t


class KB:
    def __init__(self, nc):
        self.nc = nc
        self.stack = ExitStack()
        self.sems = []
        self.pe = Seq(nc.tensor, "pe", inorder=True)
        self.act = Seq(nc.scalar, "act")
        self.dve = Seq(nc.vector, "dve")
        self.pool = Seq(nc.gpsimd, "pool")
        self.sp = Seq(nc.sync, "sp")
        self.seqs = [self.pe, self.act, self.dve, self.pool, self.sp]
        for s in (self.pe, self.act, self.dve, self.pool):
            s.csem = self._sem("c_" + s.name, 1)
        for s, n in ((self.sp, 10), (self.pool, 8), (self.act, 4)):
            s.dsems = [self._sem("d_%s%d" % (s.name, i), 16) for i in range(n)]

    def _sem(self, name, step):
        h = self.stack.enter_context(self.nc.semaphore(name))
        s = Sem(h, step, len(self.sems))
        self.sems.append(s)
        return s

    def emit(self, seq, fn, r=(), w=(), dma=False, sig=True):
        need = {}

        def nd(d):
            for sid, v in d.items():
                if v > need.get(sid, 0):
                    need[sid] = v

        for b in r:
            nd(b.b.w if isinstance(b, T) else b.w)
        for b in w:
            bb = b.b if isinstance(b, T) else b
            nd(bb.r)
            nd(bb.w)
        if dma:
            sem = seq.dsems[seq.rr % len(seq.dsems)]
            seq.rr += 1
            if sem.cnt > 0:
                if sem.cnt > need.get(sem.sid, 0):
                    need[sem.sid] = sem.cnt
        else:
            sem = seq.csem
        for sid, v in need.items():
            if seq.inorder and not dma and sid == seq.csem.sid:
                continue
            if seq.seen.get(sid, 0) >= v:
                continue
            s = self.sems[sid]
            assert v <= s.cnt, "waiting on a pending ticket (%s sid=%d v=%d cnt=%d)" % (seq.name, sid, v, s.cnt)
            seq.eng.wait_ge(s.h, v)
            seq.seen[sid] = v
        ins = fn()
        if sig:
            sem.cnt += sem.step
            ins.then_inc(sem.h, sem.step)
            tk = sem.cnt
        else:
            tk = sem.cnt + sem.step
        for b in r:
            bb = b.b if isinstance(b, T) else b
            if tk > bb.r.get(sem.sid, 0):
                bb.r[sem.sid] = tk
        for b in w:
            bb = b.b if isinstance(b, T) else b
            bb.w = {sem.sid: tk}
            bb.r = {}
        return ins

    def barrier(self):
        for seq in self.seqs:
            for s in self.sems:
                if s.cnt > seq.seen.get(s.sid, 0):
                    seq.eng.wait_ge(s.h, s.cnt)
                    seq.seen[s.sid] = s.cnt

    def dma(self, seq, out, in_, r=(), w=(), **kw):
        return self.emit(seq, lambda: seq.eng.dma_start(out=out, in_=in_, **kw), r=r, w=w, dma=True)

    def mm(self, out, lhsT, rhs, start, stop, r=(), w=(), sig=None):
        if sig is None:
            sig = stop
        return self.emit(self.pe, lambda: self.nc.tensor.matmul(out, lhsT, rhs, start=start, stop=stop),
                         r=r, w=w, sig=sig)

    def tr(self, out, in_, ident, r=(), w=(), sig=True):
        return self.emit(self.pe, lambda: self.nc.tensor.transpose(out=out, in_=in_, identity=ident),
                         r=r, w=w, sig=sig)

    def actf(self, out, in_, func, r=(), w=(), **kw):
        return self.emit(self.act, lambda: self.nc.scalar.activation(out=out, in_=in_, func=func, **kw), r=r, w=w)

    def tt(self, seq, out, in0, in1, op, r=(), w=()):
        return self.emit(seq, lambda: seq.eng.tensor_tensor(out=out, in0=in0, in1=in1, op=op), r=r, w=w)

    def ts(self, seq, out, in0, s1, s2, op0, op1=None, r=(), w=()):
        if op1 is None:
            return self.emit(seq, lambda: seq.eng.tensor_scalar(out=out, in0=in0, scalar1=s1, scalar2=None, op0=op0),
                             r=r, w=w)
        return self.emit(seq, lambda: seq.eng.tensor_scalar(out=out, in0=in0, scalar1=s1, scalar2=s2, op0=op0, op1=op1),
                         r=r, w=w)

    def stt(self, seq, out, in0, scalar, in1, op0, op1, r=(), w=()):
        return self.emit(seq, lambda: seq.eng.scalar_tensor_tensor(out=out, in0=in0, scalar=scalar, in1=in1,
                                                                   op0=op0, op1=op1), r=r, w=w)


class Phase:
    def __init__(self, k, name):
        self.k = k
        self.nc = k.nc
        self.name = name
        self.es = ExitStack()
        self.n = 0

    def __enter__(self):
        self.es.__enter__()
        return self

    def __exit__(self, *a):
        self.k.barrier()
        return self.es.__exit__(*a)

    def sb(self, shape, dt, nm="t"):
        self.n += 1
        h = self.es.enter_context(self.nc.sbuf_tensor("%s_%s%d" % (self.name, nm, self.n), list(shape), dt))
        return T(h)

    def ps(self, shape, dt, nm="p"):
        self.n += 1
        h = self.es.enter_context(self.nc.psum_tensor("%s_%s%d" % (self.name, nm, self.n), list(shape), dt))
        return T(h)

    def const(self, val):
        if not hasattr(self, "_consts"):
            self._consts = {}
        if val not in self._consts:
            t = self.sb([128, 1], F32, "c")
            self.k.emit(self.k.dve, lambda: self.nc.vector.memset(t[:], val), w=[t])
            self._consts[val] = t
        return self._consts[val]

    def sbring(self, n, shape, dt, nm="r"):
        return Ring([self.sb(shape, dt, nm) for _ in range(n)])

    def psring(self, n, shape, dt, nm="pr"):
        return Ring([self.ps(shape, dt, nm) for _ in range(n)])


def layer_norm_rows(k, ph, x, xn, stat_ring, eps=EPS):
    nc = k.nc
    st = stat_ring.next()
    for j in range(4):
        k.emit(k.dve, lambda: nc.vector.bn_stats(out=st[:, j * 6:(j + 1) * 6], in_=x[:, j * 512:(j + 1) * 512]),
               r=[x], w=[st])
    k.emit(k.dve, lambda: nc.vector.bn_aggr(out=st[:, 24:26], in_=st[:, 0:24]), r=[st], w=[st])
    epsT = ph.const(eps)
    k.actf(st[:, 26:27], st[:, 25:26], AF.Sqrt, r=[st, epsT], w=[st], bias=epsT[:, 0:1], scale=1.0)
    k.emit(k.dve, lambda: nc.vector.reciprocal(out=st[:, 26:27], in_=st[:, 26:27]), r=[st], w=[st])
    k.stt(k.dve, st[:, 27:28], st[:, 24:25], -1.0, st[:, 26:27], ALU.mult, ALU.mult, r=[st], w=[st])
    k.actf(xn[:, :], x[:, :], AF.Identity, r=[x, st], w=[xn], bias=st[:, 27:28], scale=st[:, 26:27])


def layer_norm_gen(k, ph, x, xn, stat_ring, eps):
    nc = k.nc
    st = stat_ring.next()
    for j in range(4):
        k.emit(k.dve, lambda: nc.vector.bn_stats(out=st[:, j * 6:(j + 1) * 6], in_=x[:, j * 512:(j + 1) * 512]),
               r=[x], w=[st])
        yield None
    k.emit(k.dve, lambda: nc.vector.bn_aggr(out=st[:, 24:26], in_=st[:, 0:24]), r=[st], w=[st])
    yield None
    epsT = ph.const(eps)
    k.actf(st[:, 26:27], st[:, 25:26], AF.Sqrt, r=[st, epsT], w=[st], bias=epsT[:, 0:1], scale=1.0)
    yield None
    k.emit(k.dve, lambda: nc.vector.reciprocal(out=st[:, 26:27], in_=st[:, 26:27]), r=[st], w=[st])
    yield None
    k.stt(k.dve, st[:, 27:28], st[:, 24:25], -1.0, st[:, 26:27], ALU.mult, ALU.mult, r=[st], w=[st])
    yield None
    k.actf(xn[:, :], x[:, :], AF.Identity, r=[x, st], w=[xn], bias=st[:, 27:28], scale=st[:, 26:27])
    yield None


def build(S, depth, dbg=False):
    assert S % 512 == 0
    NTL = S // 128
    NT = NTL + 2
    NTOK = S + L
    NQC = S // 512
    nc = bass.Bass("TRN2", target_bir_lowering=False)

    def din(name, shape, dt=F32):
        return nc.dram_tensor(name, list(shape), dt, kind="ExternalInput").ap()

    def dscr(name, shape, dt, out=False):
        return nc.dram_tensor(name, list(shape), dt, kind="ExternalOutput" if (out or dbg) else "Internal").ap()

    x_in = din("x", [S, D])
    ctx_in = din("ctx", [L, D])
    ccT = din("ccT", [128, KC, 2])
    w_mod = din("w_mod", [depth, D, 6 * D])
    b_mod = din("b_mod", [depth, 6 * D])
    w_in = din("w_in", [depth, D, INW])
    q_gain = din("q_gain_a", [depth, HD])
    k_gain = din("k_gain_a", [depth, HD])
    sink_b = din("sink_b", [depth, 4])
    w_f = din("w_fourier", [depth, 4, 128, 128])
    w_out = din("w_out", [depth, D, D])
    ln1_g = din("ln1_g", [depth, D])
    ln1_b = din("ln1_b", [depth, D])
    w_up = din("w_up", [depth, D, FF])
    w_gate = din("w_gate", [depth, D, FF])
    convp = din("convp", [depth, 128, 4, FC])
    w_down = din("w_down", [depth, FF, D])
    ln2_g = din("ln2_g", [depth, D])
    ln2_b = din("ln2_b", [depth, D])
    ropec = din("ropec", [S, 64])
    ropes = din("ropes", [S, 64])
    dftc = din("dftc", [S, S], BF16)
    dfts = din("dfts", [S, S], BF16)
    dftc_c = din("dftc_c", [L, L], BF16)
    dfts_c = din("dfts_c", [L, L], BF16)
    c128 = din("c128", [128, 128])
    ns128 = din("ns128", [128, 128])
    maskb_in = din("maskb", [128, 6, 512], BF16)
    identb_in = din("identb", [128, 128], BF16)
    identf_in = din("identf", [128, 128])

    y_out = nc.dram_tensor("y", [S, D], F32, kind="ExternalOutput").ap()

    xres = dscr("xres", [NTOK, D], F32)
    x1s = dscr("x1s", [NTOK, D], F32)
    fs = dscr("fs", [NTOK, D], F32)
    modv = dscr("modv", [depth, 2, 6 * D], F32)
    modT = dscr("modT", [depth, 128, 96, 2], F32)
    qT = dscr("qT", [12, 128, NTOK], BF16)
    kT = dscr("kT", [4, 128, NTOK], BF16)
    vS = dscr("vS", [NTOK, 512], BF16)
    uT = dscr("uT", [4, 128, NTOK], BF16)
    oT = dscr("oT", [KC, 128, NTOK], BF16)
    h2T = dscr("h2T", [KC, 128, NTOK], BF16)
    aT = dscr("aT", [FC, 128, NTOK], BF16)

    k = KB(nc)
    with k.stack:
        with Phase(k, "pro") as ph:
            k.dma(k.sp, xres[0:S, :], x_in[:, :])
            k.dma(k.sp, xres[S:NTOK, :], ctx_in[:, :])

        for l in range(depth):
            last = (l == depth - 1)
            NTa = NTL if last else NT
            with Phase(k, "p0_%d" % l) as ph:
                sc = ph.sb([128, KC, 2], F32)
                k.dma(k.sp, sc[:], ccT[:, :, :], w=[sc])
                scs = ph.sb([128, KC, 2], F32)
                k.actf(scs[:], sc[:], AF.Silu, r=[sc], w=[scs])
                bm = ph.sb([2, 6 * D], F32)
                k.dma(k.sp, bm[:], b_mod[l:l + 1, :].partition_broadcast(2), w=[bm])
                mo = ph.sb([2, 6 * D], F32)
                wring = ph.sbring(12, [128, 4, 512], F32, "wm")
                pring = ph.psring(2, [2, 512], F32)
                wv = w_mod[l].rearrange("(kc p) n -> p kc n", p=128)
                for cb in range(24):
                    wq = []
                    for q4 in range(4):
                        wt = wring.next()
                        k.dma(k.sp, wt[:, :, :], wv[:, q4 * 4:(q4 + 1) * 4, cb * 512:(cb + 1) * 512], w=[wt])
                        wq.append(wt)
                    p = pring.next()
                    for kc in range(KC):
                        wt = wq[kc // 4]
                        k.mm(p[:, :], scs[:, kc, :], wt[:, kc % 4, :], kc == 0, kc == KC - 1, r=[scs, wt], w=[p])
                    k.tt(k.dve, mo[:, cb * 512:(cb + 1) * 512], p[:, :], bm[:, cb * 512:(cb + 1) * 512], ALU.add,
                         r=[p, bm], w=[mo])
                for a in (1, 4):
                    k.ts(k.dve, mo[:, a * D:(a + 1) * D], mo[:, a * D:(a + 1) * D], 1.0, None, ALU.add, r=[mo], w=[mo])
                k.dma(k.sp, modv[l], mo[:], r=[mo])
                idf = ph.sb([2, 2], F32)
                k.dma(k.sp, idf[:], identf_in[0:2, 0:2], w=[idf])
                pT = ph.ps([128, 96 * 2], F32, "pT")
                for j in range(96):
                    k.tr(pT[:, 2 * j:2 * j + 2], mo[:, j * 128:(j + 1) * 128], idf[:], r=[mo, idf], w=[pT], sig=(j == 95))
                moT = ph.sb([128, 96 * 2], F32)
                k.actf(moT[:], pT[:], AF.Copy, r=[pT], w=[moT])
                k.dma(k.sp, modT[l].rearrange("p a b -> p (a b)"), moT[:], r=[moT])

            def modrow(idx, r):
                return modv[l, r:r + 1, idx * D:(idx + 1) * D].partition_broadcast(128)

            with Phase(k, "p1_%d" % l) as ph:
                wi = ph.sb([128, KC, INW], BF16, "wi")
                wiv = w_in[l].rearrange("(kc p) n -> p kc n", p=128)
                wiq = [{dc: T(wi.h) for dc in (0, 1536, 1792, 2048, 2304)} for _ in range(KC)]
                colmap = ((0, 0, 1536), (1536, 1536, 256), (1792, 2048, 256), (2048, 1792, 256), (2304, 2304, 768))
                for kc in range(KC):
                    for (dc, sc_, n_) in colmap:
                        hh = 0 if dc < 1536 else 1
                        k.dma(k.pool, wi[:, kc, dc:dc + n_], wiv[:, kc, sc_:sc_ + n_], w=[wiq[kc][dc]])
                identb = ph.sb([128, 128], BF16)
                k.dma(k.sp, identb[:], identb_in[:, :], w=[identb])
                mT = ph.sb([128, 96 * 2], F32, "mT")
                k.dma(k.sp, mT[:], modT[l].rearrange("p a b -> p (a b)"), w=[mT])
                qg = ph.sb([128, HD], F32)
                kg = ph.sb([128, HD], F32)
                k.dma(k.sp, qg[:], q_gain[l:l + 1, :].partition_broadcast(128), w=[qg])
                k.dma(k.sp, kg[:], k_gain[l:l + 1, :].partition_broadcast(128), w=[kg])
                k.ts(k.dve, qg[:], qg[:], QSCALE, None, ALU.mult, r=[qg], w=[qg])
                rcring = ph.sbring(2, [128, 64], F32, "rc")
                rsring = ph.sbring(2, [128, 64], F32, "rs")
                xring = ph.sbring(2, [128, D], F32, "x")
                string = ph.sbring(2, [128, 32], F32, "st")
                hbring = ph.sbring(2, [128, D], BF16, "hb")
                hTring = ph.sbring(2, [128, KC, 128], BF16, "hT")
                tp = [ph.ps([128, 1024], BF16, "tp") for _ in range(2)]
                pb = [ph.ps([128, 512], F32, "pb") for _ in range(6)]
                psb = ph.sb([128, INW], F32, "psb")
                psbq = T(psb.h)
                psbv = T(psb.h)
                sqring = ph.sbring(2, [128, 512], F32, "sq")
                ssring = ph.sbring(2, [128, 16], F32, "ss")
                qkrring = ph.sbring(2, [128, D], BF16, "qkr")
                tmpA = ph.sbring(1, [128, 16, 2, 32], F32, "ta")
                tmpB = ph.sbring(1, [128, 16, 2, 32], F32, "tb")
                vtring = ph.sbring(2, [128, 512], BF16, "vt")
                utring = ph.sbring(2, [128, 512], BF16, "ut")
                qTring = ph.sbring(2, [128, 16, 128], BF16, "qT")
                uTring = ph.sbring(2, [128, 4, 128], BF16, "uT")
                state = {"mod": None}

                def stageA1(t):
                    x = xring.next()
                    k.dma(k.sp, x[:], xres[t * 128:(t + 1) * 128, :], w=[x])
                    hb = hbring.next()
                    layer_norm_rows(k, ph, x, hb, string)
                    return hb

                def stageA2(t, hb):
                    isctx = t >= NTL
                    mr = 1 if isctx else 0
                    hT = hTring.next()
                    for half in range(2):
                        for j in range(8):
                            kc = half * 8 + j
                            k.tr(tp[half][:, j * 128:(j + 1) * 128], hb[:, kc * 128:(kc + 1) * 128], identb[:],
                                 r=[hb, identb], w=[tp[half]], sig=(j == 7))
                        for j in range(8):
                            kc = half * 8 + j
                            k.actf(hT[:, kc, :], tp[half][:, j * 128:(j + 1) * 128], AF.Identity, r=[tp[half], mT], w=[hT],
                                   scale=mT[:, (16 + kc) * 2 + mr:(16 + kc) * 2 + mr + 1],
                                   bias=mT[:, kc * 2 + mr:kc * 2 + mr + 1])
                    return hT

                def stageB(t, hT, half):
                    if True:
                        for kc in range(KC):
                            for cb in range(half * 3, half * 3 + 3):
                                k.mm(pb[cb][:, :], hT[:, kc, :], wi[:, kc, cb * 512:(cb + 1) * 512], kc == 0, kc == KC - 1,
                                     r=[hT] + ([wiq[kc][0]] if half == 0 else [wiq[kc][dc] for dc in (1536, 1792, 2048, 2304)]), w=[pb[cb]])

                def stageC0(t, lo, hi):
                    for cb in range(lo, hi):
                        dstb = psbq if cb < 4 else psbv
                        if cb % 2 == 0:
                            k.actf(psb[:, cb * 512:(cb + 1) * 512], pb[cb][:, :], AF.Copy, r=[pb[cb]], w=[dstb])
                        else:
                            k.emit(k.dve, lambda: nc.vector.tensor_copy(out=psb[:, cb * 512:(cb + 1) * 512], in_=pb[cb][:, :]),
                                   r=[pb[cb]], w=[dstb])

                def stageC(t):
                    isctx = t >= NTL
                    ss = ssring.next()
                    for (c0, nh, ofs) in ((0, 4, 0), (512, 4, 4), (1536, 2, 8)):
                        sq = sqring.next()
                        k.actf(sq[:, 0:nh * 128], psb[:, c0:c0 + nh * 128], AF.Square, r=[psbq], w=[sq])
                        k.emit(k.dve, lambda: nc.vector.tensor_reduce(
                            out=ss[:, ofs:ofs + nh], in_=sq[:, 0:nh * 128].rearrange("p (h d) -> p h d", d=128),
                            axis=AX.X, op=ALU.add), r=[sq], w=[ss])
                    epsT = ph.const(EPS)
                    k.actf(ss[:, 0:10], ss[:, 0:10], AF.Sqrt, r=[ss, epsT], w=[ss], bias=epsT[:, 0:1], scale=1.0 / 128.0)
                    k.emit(k.dve, lambda: nc.vector.reciprocal(out=ss[:, 0:10], in_=ss[:, 0:10]), r=[ss], w=[ss])
                    for h in range(8):
                        k.stt(k.dve, psb[:, h * 128:(h + 1) * 128], psb[:, h * 128:(h + 1) * 128],
                              ss[:, h:h + 1], qg[:, :], ALU.mult, ALU.mult, r=[psbq, ss, qg], w=[psbq])
                    k.actf(psb[:, 1024:1536], psb[:, 1024:1536], AF.Copy, r=[psbq], w=[psbq], scale=QSCALE)
                    for h in range(2):
                        c0 = 1536 + h * 128
                        k.stt(k.dve, psb[:, c0:c0 + 128], psb[:, c0:c0 + 128],
                              ss[:, 8 + h:9 + h], kg[:, :], ALU.mult, ALU.mult, r=[psbq, ss, kg], w=[psbq])
                    vt = vtring.next()
                    k.actf(vt[:, :], psb[:, 2048:2560], AF.Copy, r=[psbv], w=[vt])
                    ut = utring.next()
                    k.actf(ut[:, :], psb[:, 2560:3072], AF.Copy, r=[psbv], w=[ut])
                    qkr = qkrring.next()
                    if isctx:
                        k.actf(qkr[:, :], psb[:, 0:2048], AF.Copy, r=[psbq], w=[qkr])
                    else:
                        q5 = psb[:, 0:2048].rearrange("p (h x y f) -> p h x y f", h=16, x=2, y=2)
                        o5 = qkr[:, :].rearrange("p (h x y f) -> p h x y f", h=16, x=2, y=2)
                        a_ = q5[:, :, :, 0, :]
                        b_ = q5[:, :, :, 1, :]
                        rc = rcring.next()
                        rs_ = rsring.next()
                        k.dma(k.sp, rc[:], ropec[t * 128:(t + 1) * 128, :], w=[rc])
                        k.dma(k.sp, rs_[:], ropes[t * 128:(t + 1) * 128, :], w=[rs_])
                        cc = rc[:, :].rearrange("p (x f) -> p x f", x=2).unsqueeze(1).to_broadcast([128, 16, 2, 32])
                        sn = rs_[:, :].rearrange("p (x f) -> p x f", x=2).unsqueeze(1).to_broadcast([128, 16, 2, 32])
                        ta = tmpA.next()
                        tb = tmpB.next()
                        k.tt(k.dve, ta[:], a_, cc, ALU.mult, r=[psbq, rc], w=[ta])
                        k.tt(k.pool, tb[:], b_, sn, ALU.mult, r=[psbq, rs_], w=[tb])
                        k.tt(k.dve, o5[:, :, :, 0, :], ta[:], tb[:], ALU.subtract, r=[ta, tb], w=[qkr])
                        ta = tmpA.next()
                        tb = tmpB.next()
                        k.tt(k.pool, ta[:], a_, sn, ALU.mult, r=[psbq, rs_], w=[ta])
                        k.tt(k.dve, tb[:], b_, cc, ALU.mult, r=[psbq, rc], w=[tb])
                        k.tt(k.pool, o5[:, :, :, 1, :], ta[:], tb[:], ALU.add, r=[ta, tb], w=[qkr])
                    return (qkr, ut, vt)

                def stageD(t, qkr, ut, vt):
                    qTt = qTring.next()
                    for half in range(2):
                        for j in range(8):
                            c = half * 8 + j
                            k.tr(tp[half][:, j * 128:(j + 1) * 128], qkr[:, c * 128:(c + 1) * 128], identb[:],
                                 r=[qkr, identb], w=[tp[half]], sig=(j == 7))
                        k.emit(k.dve, lambda: nc.vector.tensor_copy(
                            out=qTt[:, half * 8:(half + 1) * 8, :],
                            in_=tp[half][:, :].rearrange("p (a b) -> p a b", b=128)), r=[tp[half]], w=[qTt])
                    uTt = uTring.next()
                    for j in range(4):
                        k.tr(tp[0][:, j * 128:(j + 1) * 128], ut[:, j * 128:(j + 1) * 128], identb[:],
                             r=[ut, identb], w=[tp[0]], sig=(j == 3))
                    k.emit(k.dve, lambda: nc.vector.tensor_copy(
                        out=uTt[:, :, :], in_=tp[0][:, 0:512].rearrange("p (a b) -> p a b", b=128)), r=[tp[0]], w=[uTt])
                    tsl = slice(t * 128, (t + 1) * 128)
                    k.dma(k.sp, qT[:, :, tsl].rearrange("h p n -> p h n"), qTt[:, 0:12, :], r=[qTt])
                    k.dma(k.sp, kT[:, :, tsl].rearrange("h p n -> p h n"), qTt[:, 12:16, :], r=[qTt])
                    k.dma(k.sp, uT[:, :, tsl].rearrange("h p n -> p h n"), uTt[:, :, :], r=[uTt])
                    k.dma(k.sp, vS[tsl, :], vt[:, :], r=[vt])

                hbs = {0: stageA1(0)}
                if NT > 1:
                    hbs[1] = stageA1(1)
                hT_cur = stageA2(0, hbs.pop(0))
                prevC = None
                for t in range(NT):
                    stageB(t, hT_cur, 0)
                    hT_nxt = stageA2(t + 1, hbs.pop(t + 1)) if t + 1 < NT else None
                    stageC0(t, 0, 3)
                    stageB(t, hT_cur, 1)
                    if prevC is not None:
                        stageD(t - 1, *prevC)
                    stageC0(t, 3, 6)
                    if t + 2 < NT:
                        hbs[t + 2] = stageA1(t + 2)
                    prevC = stageC(t)
                    hT_cur = hT_nxt
                stageD(NT - 1, *prevC)

            with Phase(k, "p2_%d" % l) as ph:
                kTs = ph.sb([128, 4, NTOK], BF16, "kTs")
                kTb = [T(kTs.h) for _ in range(4)]
                for h in range(4):
                    k.dma(k.sp, kTs[:, h, :], kT[h], w=[kTb[h]])
                vs = ph.sb([128, NT, 512], BF16, "vs")
                k.dma(k.sp, vs[:], vS.rearrange("(t p) c -> p t c", p=128), w=[vs])
                ones = ph.sb([128, 128], F32)
                k.emit(k.dve, lambda: nc.vector.memset(ones[:], 1.0), w=[ones])
                onesb = ph.sb([128, 128], BF16)
                k.emit(k.dve, lambda: nc.vector.memset(onesb[:], 1.0), w=[onesb])
                accsets = [[ph.sb([128, 512], F32, "acc") for _ in range(5)] for _ in range(2)]
                chunk_i = [0]
                identb = ph.sb([128, 128], BF16)
                k.dma(k.sp, identb[:], identb_in[:, :], w=[identb])
                maskb = ph.sb([128, 6, 512], BF16)
                k.dma(k.sp, maskb[:], maskb_in[:, :, :], w=[maskb])
                es = ph.sb([128, 4], F32)
                k.dma(k.sp, es[:], sink_b[l:l + 1, :].partition_broadcast(128), w=[es])
                k.actf(es[:], es[:], AF.Exp, r=[es], w=[es])
                qring = ph.sbring(2, [128, NTOK], BF16, "q")
                pring = ph.sbring(4, [128, 512], BF16, "pt")
                oring = ph.sbring(2, [128, 512], BF16, "ot")
                rdring = ph.sbring(2, [128, 512], F32, "rd")
                ps_s = ph.psring(3, [128, 512], F32, "s")
                ps_o = ph.psring(2, [128, 512], F32, "o")
                ps_d = ph.psring(2, [128, 512], F32, "d")
                for hq in range(12):
                    isB = hq >= 8
                    kv = (hq // 4) if not isB else (2 + (hq - 8) // 2)
                    q = qring.next()
                    k.dma(k.sp, q[:], qT[hq], w=[q])
                    chunks = [(qc * 512, 512, False) for qc in range(NQC)] + ([] if last else [(S, 256, True)])
                    for (q0, qn, isctx) in chunks:
                        if isctx:
                            tiles = [(NTL, None), (NTL + 1, None)]
                        elif not isB:
                            tiles = [(t, None) for t in range(NT)]
                        else:
                            n0 = q0 // 128
                            tiles = [(n0 + r, r + 1) for r in range(-1, 5) if 0 <= n0 + r < NTL]
                            tiles += [(NTL, None), (NTL + 1, None)]
                        po = ps_o.next()
                        pd = ps_d.next()
                        n = len(tiles)

                        def qk(i):
                            st, mi = tiles[i]
                            p = ps_s.next()
                            k.mm(p[:, 0:qn], kTs[:, kv, st * 128:(st + 1) * 128], q[:, q0:q0 + qn], True, mi is None,
                                 r=[kTb[kv], q], w=[p])
                            if mi is not None:
                                k.mm(p[:, 0:qn], identb[:], maskb[:, mi, 0:qn], False, True, r=[identb, maskb], w=[p])
                            return p

                        LA = 2
                        pend = [qk(i) for i in range(min(LA, n))]
                        accs = accsets[chunk_i[0] % 2]
                        chunk_i[0] += 1
                        used = [False] * 5
                        pd_started = [False]
                        plan = ((0, k.dve), (None, None), (3, k.pool), (1, k.dve), (None, None), (2, k.dve), (4, k.pool),
                                (None, None))
                        for i in range(n):
                            p = pend.pop(0)
                            pt = pring.next()
                            k.actf(pt[:, 0:qn], p[:, 0:qn], AF.Exp, r=[p], w=[pt])
                            if i + LA < n:
                                pend.append(qk(i + LA))
                            st = tiles[i][0]
                            k.mm(po[:, 0:qn], vs[:, st, kv * 128:(kv + 1) * 128], pt[:, 0:qn], i == 0, i == n - 1,
                                 r=[vs, pt], w=[po])
                            ai, seq_ = plan[i % 8]
                            if ai is None:
                                k.mm(pd[:, 0:qn], onesb[:], pt[:, 0:qn], not pd_started[0], False, r=[onesb, pt], w=[pd],
                                     sig=True)
                                pd_started[0] = True
                            else:
                                acc = accs[ai]
                                if not used[ai]:
                                    k.emit(seq_, lambda: seq_.eng.tensor_copy(out=acc[:, 0:qn], in_=pt[:, 0:qn]),
                                           r=[pt], w=[acc])
                                    used[ai] = True
                                else:
                                    k.tt(seq_, acc[:, 0:qn], acc[:, 0:qn], pt[:, 0:qn], ALU.add, r=[acc, pt], w=[acc])
                        ua = [a for a, u in zip(accs, used) if u]
                        for j, acc in enumerate(ua):
                            k.mm(pd[:, 0:qn], ones[:], acc[:, 0:qn], not pd_started[0], j == len(ua) - 1, r=[ones, acc], w=[pd])
                            pd_started[0] = True
                        rd = rdring.next()
                        if isB:
                            k.ts(k.dve, rd[:, 0:qn], pd[:, 0:qn], es[:, hq - 8:hq - 7], None, ALU.add, r=[pd, es], w=[rd])
                            k.emit(k.dve, lambda: nc.vector.reciprocal(out=rd[:, 0:qn], in_=rd[:, 0:qn]), r=[rd], w=[rd])
                        else:
                            k.emit(k.dve, lambda: nc.vector.reciprocal(out=rd[:, 0:qn], in_=pd[:, 0:qn]), r=[pd], w=[rd])
                        ot = oring.next()
                        k.tt(k.dve, ot[:, 0:qn], po[:, 0:qn], rd[:, 0:qn], ALU.mult, r=[po, rd], w=[ot])
                        k.dma(k.pool, oT[hq, :, q0:q0 + qn], ot[:, 0:qn], r=[ot])

            with Phase(k, "p3_%d" % l) as ph:
                c128t = ph.sb([128, 128], F32)
                ns128t = ph.sb([128, 128], F32)
                k.dma(k.sp, c128t[:], c128[:, :], w=[c128t])
                k.dma(k.sp, ns128t[:], ns128[:, :], w=[ns128t])
                wf = ph.sb([128, 4, 128], F32)
                k.dma(k.sp, wf[:], w_f[l].rearrange("g c e -> c g e"), w=[wf])
                AB = ph.sb([128, 4, 256], BF16)
                pw = ph.psring(2, [128, 512], F32, "pw")
                pacc = ph.psring(6, [128, 512], F32, "pa")
                for g in range(4):
                    p = pw.next()
                    k.mm(p[:, 0:128], c128t[:], wf[:, g, :], True, True, r=[c128t, wf], w=[p])
                    k.mm(p[:, 128:256], ns128t[:], wf[:, g, :], True, True, r=[ns128t, wf], w=[p])
                    k.actf(AB[:, g, :], p[:, 0:256], AF.Copy, r=[p], w=[AB])
                cring = ph.sbring(3, [128, 8, 512], BF16, "dc")
                sring = ph.sbring(3, [128, 8, 512], BF16, "ds")
                foring = ph.sbring(3, [128, 512], BF16, "fo")
                segs = [(S, 0, dftc, dfts, 512)] + ([] if last else [(L, S, dftc_c, dfts_c, 256)])
                for (N, off, Ct, St, CH) in segs:
                    MT = N // 128
                    uts = ph.sb([128, 4, N], BF16, "uts")
                    utb = [T(uts.h) for _ in range(4)]
                    for g in range(4):
                        k.dma(k.sp, uts[:, g, :], uT[g, :, off:off + N], w=[utb[g]])
                    UAB = ph.sb([128, MT, 4, 256], BF16, "uab")
                    uabb = [T(UAB.h) for _ in range(MT)]
                    for mt in range(MT):
                        for gp in range(2):
                            p = pw.next()
                            for gg in range(2):
                                g = gp * 2 + gg
                                k.mm(p[:, gg * 256:(gg + 1) * 256], uts[:, g, mt * 128:(mt + 1) * 128], AB[:, g, :], True, True,
                                     r=[utb[g], AB], w=[p])
                            if gp == 0:
                                k.actf(UAB[:, mt, 0:2, :], p[:, :].rearrange("p (a b) -> p a b", b=256), AF.Copy,
                                       r=[p], w=[uabb[mt]])
                            else:
                                k.emit(k.dve, lambda: nc.vector.tensor_copy(
                                    out=UAB[:, mt, 2:4, :], in_=p[:, :].rearrange("p (a b) -> p a b", b=256)),
                                    r=[p], w=[uabb[mt]])
                    for nci in range(N // CH):
                        banks = [pacc.next() for _ in range(4)]
                        for mg in range(0, MT, 8):
                            mcount = min(8, MT - mg)
                            ct = cring.next()
                            st_ = sring.next()
                            k.dma(k.sp, ct[:, 0:mcount, 0:CH],
                                  Ct[mg * 128:(mg + mcount) * 128, nci * CH:(nci + 1) * CH].rearrange("(m p) n -> p m n", p=128),
                                  w=[ct])
                            k.dma(k.sp, st_[:, 0:mcount, 0:CH],
                                  St[mg * 128:(mg + mcount) * 128, nci * CH:(nci + 1) * CH].rearrange("(m p) n -> p m n", p=128),
                                  w=[st_])
                            for mi in range(mcount):
                                mt = mg + mi
                                for g in range(4):
                                    k.mm(banks[g][:, 0:CH], UAB[:, mt, g, 0:128], ct[:, mi, 0:CH], mt == 0, False,
                                         r=[uabb[mt], ct], w=[banks[g]])
                                for g in range(4):
                                    k.mm(banks[g][:, 0:CH], UAB[:, mt, g, 128:256], st_[:, mi, 0:CH], False, mt == MT - 1,
                                         r=[uabb[mt], st_], w=[banks[g]],
                                         sig=(mt == MT - 1) or (mi == mcount - 1 and g == 3))
                        for g in range(4):
                            fo = foring.next()
                            k.actf(fo[:, 0:CH], banks[g][:, 0:CH], AF.Copy, r=[banks[g]], w=[fo])
                            k.dma(k.pool, oT[12 + g, :, off + nci * CH:off + (nci + 1) * CH], fo[:, 0:CH], r=[fo])

            with Phase(k, "p4_%d" % l) as ph:
                wo = ph.sb([128, KC, D], BF16, "wo")
                wob = [T(wo.h) for _ in range(KC)]
                wov = w_out[l].rearrange("(kc p) n -> p kc n", p=128)
                for kc in range(KC):
                    k.dma(k.pool, wo[:, kc, :], wov[:, kc, :], w=[wob[kc]])
                identb = ph.sb([128, 128], BF16)
                k.dma(k.sp, identb[:], identb_in[:, :], w=[identb])
                G1 = ph.sb([128, D], F32)
                LG = ph.sb([128, D], F32)
                LB = ph.sb([128, D], F32)
                mT = ph.sb([128, 96 * 2], F32, "mT")
                k.dma(k.sp, mT[:], modT[l].rearrange("p a b -> p (a b)"), w=[mT])
                k.dma(k.sp, LG[:], ln1_g[l:l + 1, :].partition_broadcast(128), w=[LG])
                k.dma(k.sp, LB[:], ln1_b[l:l + 1, :].partition_broadcast(128), w=[LB])
                ocring = ph.sbring(2, [128, KC, 128], BF16, "oc")
                xring = ph.sbring(3, [128, D], F32, "x")
                vring = ph.sbring(2, [128, D], F32, "v")
                xnring = ph.sbring(1, [128, D], F32, "xn")
                x1ring = ph.sbring(3, [128, D], F32, "x1")
                string = ph.sbring(4, [128, 32], F32, "st")
                hbring = ph.sbring(4, [128, D], BF16, "hb")
                hTring = ph.sbring(2, [128, KC, 128], BF16, "hT")
                tp = [ph.ps([128, 1024], BF16, "tp") for _ in range(2)]
                pb = [ph.ps([128, 512], F32, "pb") for _ in range(4)]
                state = {"mod": None}

                def stageA(t):
                    tsl = slice(t * 128, (t + 1) * 128)
                    oc = ocring.next()
                    k.dma(k.sp, oc[:], oT[:, :, tsl].rearrange("c p n -> p c n"), w=[oc])
                    x = xring.next()
                    k.dma(k.sp, x[:], xres[tsl, :], w=[x])
                    return (oc, x)

                def stageB(t, oc, half):
                    if True:
                        for kc in range(KC):
                            for cb in (2 * half, 2 * half + 1):
                                k.mm(pb[cb][:, :], oc[:, kc, :], wo[:, kc, cb * 512:(cb + 1) * 512], kc == 0, kc == KC - 1,
                                     r=[oc, wob[kc]], w=[pb[cb]])

                def stageC0(t):
                    mr = 1 if t >= NTL else 0
                    if state["mod"] != mr:
                        k.dma(k.sp, G1[:], modrow(2, mr), w=[G1])
                        k.ts(k.dve, G1[:, :], G1[:, :], 1.0 / ALPHA, None, ALU.mult, r=[G1], w=[G1])
                        state["mod"] = mr
                    tsl = slice(t * 128, (t + 1) * 128)
                    v = vring.next()
                    for cb in range(0, 2):
                        k.tt(k.dve, v[:, cb * 512:(cb + 1) * 512], pb[cb][:, :], G1[:, cb * 512:(cb + 1) * 512], ALU.mult,
                             r=[pb[cb], G1], w=[v])
                    return v

                def stageCa(t, x, v):
                    tsl = slice(t * 128, (t + 1) * 128)
                    for cb in range(2, 4):
                        k.tt(k.dve, v[:, cb * 512:(cb + 1) * 512], pb[cb][:, :], G1[:, cb * 512:(cb + 1) * 512], ALU.mult,
                             r=[pb[cb], G1], w=[v])
                        yield None
                    k.tt(k.pool, v[:, :], x[:, :], v[:, :], ALU.add, r=[x, v], w=[v])
                    yield None
                    xn = xnring.next()
                    for _ in layer_norm_gen(k, ph, v, xn, string, EPS / (ALPHA * ALPHA)):
                        yield None
                    k.tt(k.dve, xn[:, :], xn[:, :], LG[:, :], ALU.mult, r=[xn, LG], w=[xn])
                    yield None
                    x1 = x1ring.next()
                    k.tt(k.pool, x1[:, :], xn[:, :], LB[:, :], ALU.add, r=[xn, LB], w=[x1])
                    k.dma(k.pool, x1s[tsl, :], x1[:, :], r=[x1])
                    yield x1

                def stageCb(t, x1):
                    hb = hbring.next()
                    for _ in layer_norm_gen(k, ph, x1, hb, string, EPS):
                        yield None
                    yield hb

                def run_interleaved(ga, gb):
                    ra = rb = None
                    da = ga is None
                    db = gb is None
                    while not (da and db):
                        if not da:
                            try:
                                r_ = next(ga)
                                if r_ is not None:
                                    ra = r_
                            except StopIteration:
                                da = True
                        if not db:
                            try:
                                r_ = next(gb)
                                if r_ is not None:
                                    rb = r_
                            except StopIteration:
                                db = True
                    return ra, rb

                def stageD(t, hb):
                    mr = 1 if t >= NTL else 0
                    tsl = slice(t * 128, (t + 1) * 128)
                    hT = hTring.next()
                    for half in range(2):
                        for j in range(8):
                            kc = half * 8 + j
                            k.tr(tp[half][:, j * 128:(j + 1) * 128], hb[:, kc * 128:(kc + 1) * 128], identb[:],
                                 r=[hb, identb], w=[tp[half]], sig=(j == 7))
                        for j in range(8):
                            kc = half * 8 + j
                            k.actf(hT[:, kc, :], tp[half][:, j * 128:(j + 1) * 128], AF.Identity, r=[tp[half], mT], w=[hT],
                                   scale=mT[:, (64 + kc) * 2 + mr:(64 + kc) * 2 + mr + 1],
                                   bias=mT[:, (48 + kc) * 2 + mr:(48 + kc) * 2 + mr + 1])
                    k.dma(k.act, h2T[:, :, tsl].rearrange("c p n -> p c n"), hT[:, :, :], r=[hT])

                curA = stageA(0)
                hbs = {}
                x1prev = None
                for t in range(NTa):
                    stageB(t, curA[0], 0)
                    nxtA = stageA(t + 1) if t + 1 < NTa else None
                    v = stageC0(t)
                    stageB(t, curA[0], 1)
                    if t - 3 in hbs:
                        stageD(t - 3, hbs.pop(t - 3))
                    ga = stageCa(t, curA[1], v)
                    gb = stageCb(t - 1, x1prev) if x1prev is not None else None
                    x1cur, hbp = run_interleaved(ga, gb)
                    if hbp is not None:
                        hbs[t - 1] = hbp
                    x1prev = x1cur
                    curA = nxtA
                _, hbp = run_interleaved(None, stageCb(NTa - 1, x1prev))
                hbs[NTa - 1] = hbp
                for t in sorted(hbs):
                    stageD(t, hbs[t])

            tchunks = [(cq * 512, 512) for cq in range(NQC)] + ([] if last else [(S, 256)])
            NS5 = 11
            FS5 = FC // NS5
            with Phase(k, "p5a_%d" % l) as ph:
                wslots = []
                for i in range(2):
                    wu = ph.sb([128, KC, FS5 * 128], BF16, "wu")
                    wg = ph.sb([128, KC, FS5 * 128], BF16, "wg")
                    wslots.append((wu, wg, [T(wu.h) for _ in range(KC)], [T(wg.h) for _ in range(KC)]))
                wuv = w_up[l].rearrange("(kc p) n -> p kc n", p=128)
                wgv = w_gate[l].rearrange("(kc p) n -> p kc n", p=128)
                cp = ph.sb([128, 4, FC], F32)
                k.dma(k.sp, cp[:], convp[l], w=[cp])
                hring = ph.sbring(2, [128, KC, 514], BF16, "hc")
                gsring = ph.sbring(2, [128, 514], F32, "gs")
                accring = ph.sbring(2, [128, 512], F32, "acc")
                sgring = ph.sbring(2, [128, 512], F32, "sg")
                aring = ph.sbring(2, [128, FS5, 512], BF16, "at")
                pu = ph.psring(2, [128, 512], F32, "pu")
                pg = ph.psring(2, [128, 512], F32, "pg")
                phl = ph.psring(2, [128, 2], F32, "ph")

                def loadw(sl):
                    wu, wg, wub, wgb = wslots[sl % 2]
                    cs0 = sl * FS5 * 128
                    for kc in range(KC):
                        k.dma(k.pool, wg[:, kc, :], wgv[:, kc, cs0:cs0 + FS5 * 128], w=[wgb[kc]])
                        k.dma(k.pool, wu[:, kc, :], wuv[:, kc, cs0:cs0 + FS5 * 128], w=[wub[kc]])

                loadw(0)
                for sl in range(NS5):
                    if sl + 1 < NS5:
                        loadw(sl + 1)
                    wu, wg, wub, wgb = wslots[sl % 2]
                    for (t0, tn) in tchunks:
                        hc = hring.next()
                        lo_valid = (t0 > 0 and t0 < S)
                        hi_valid = (t0 + tn < S)
                        a0 = t0 - (1 if lo_valid else 0)
                        a1 = t0 + tn + (1 if hi_valid else 0)
                        d0 = 0 if lo_valid else 1
                        k.dma(k.sp, hc[:, :, d0:d0 + (a1 - a0)], h2T[:, :, a0:a1].rearrange("c p n -> p c n"), w=[hc])
                        if not lo_valid:
                            k.emit(k.dve, lambda: nc.vector.memset(hc[:, :, 0:1], 0.0), w=[hc])
                        if not hi_valid:
                            k.emit(k.dve, lambda: nc.vector.memset(hc[:, :, tn + 1:tn + 2], 0.0), w=[hc])
                        at = aring.next()
                        for fi in range(FS5):
                            fc = sl * FS5 + fi
                            pU = pu.next()
                            pG = pg.next()
                            pH = phl.next()
                            wsl = slice(fi * 128, (fi + 1) * 128)
                            for kc in range(KC):
                                k.mm(pG[:, 0:tn], wg[:, kc, wsl], hc[:, kc, 1:1 + tn], kc == 0, kc == KC - 1,
                                     r=[wgb[kc], hc], w=[pG])
                            for kc in range(KC):
                                k.mm(pH[:, 0:2], wg[:, kc, wsl], hc[:, kc, 0:tn + 2:tn + 1], kc == 0, kc == KC - 1,
                                     r=[wgb[kc], hc], w=[pH])
                            for kc in range(KC):
                                k.mm(pU[:, 0:tn], wu[:, kc, wsl], hc[:, kc, 1:1 + tn], kc == 0, kc == KC - 1,
                                     r=[wub[kc], hc], w=[pU])
                            gs = gsring.next()
                            k.actf(gs[:, 1:1 + tn], pG[:, 0:tn], AF.Copy, r=[pG], w=[gs])
                            k.actf(gs[:, 0:tn + 2:tn + 1], pH[:, 0:2], AF.Copy, r=[pH], w=[gs])
                            acc = accring.next()
                            k.ts(k.dve, acc[:, 0:tn], gs[:, 1:1 + tn], cp[:, 1, fc:fc + 1], cp[:, 3, fc:fc + 1],
                                 ALU.mult, ALU.add, r=[gs, cp], w=[acc])
                            k.stt(k.dve, acc[:, 0:tn], gs[:, 0:tn], cp[:, 0, fc:fc + 1], acc[:, 0:tn], ALU.mult, ALU.add,
                                  r=[gs, cp, acc], w=[acc])
                            k.stt(k.dve, acc[:, 0:tn], gs[:, 2:2 + tn], cp[:, 2, fc:fc + 1], acc[:, 0:tn], ALU.mult, ALU.add,
                                  r=[gs, cp, acc], w=[acc])
                            sg = sgring.next()
                            k.actf(sg[:, 0:tn], acc[:, 0:tn], AF.Silu, r=[acc], w=[sg])
                            k.tt(k.dve, at[:, fi, 0:tn], sg[:, 0:tn], pU[:, 0:tn], ALU.mult, r=[sg, pU], w=[at])
                        k.dma(k.pool, aT[sl * FS5:(sl + 1) * FS5, :, t0:t0 + tn].rearrange("f p n -> p f n"), at[:, :, 0:tn],
                              r=[at])

            with Phase(k, "p5b_%d" % l) as ph:
                wdh = [ph.sb([128, FC, 512], BF16, "wd") for _ in range(2)]
                wdq = [[T(h.h) for _ in range(4)] for h in wdh]
                ach = [ph.sb([128, FC, 512], BF16, "ac") for _ in range(2)]
                acq = [[T(h.h) for _ in range(4)] for h in ach]
                fring = ph.sbring(3, [128, 512], F32, "f")
                pf = ph.psring(4, [128, 512], F32, "pf")
                wdv = w_down[l].rearrange("(fc p) n -> p fc n", p=128)
                aci = 0
                for cs in range(4):
                    wd = wdh[cs % 2]
                    for q4 in range(4):
                        k.dma(k.pool, wd[:, q4 * 11:(q4 + 1) * 11, :], wdv[:, q4 * 11:(q4 + 1) * 11, cs * 512:(cs + 1) * 512],
                              w=[wdq[cs % 2][q4]])
                    for (t0, tn) in tchunks:
                        ac = ach[aci % 2]
                        aq = acq[aci % 2]
                        aci += 1
                        for q4 in range(4):
                            k.dma(k.sp, ac[:, q4 * 11:(q4 + 1) * 11, 0:tn],
                                  aT[q4 * 11:(q4 + 1) * 11, :, t0:t0 + tn].rearrange("f p n -> p f n"), w=[aq[q4]])
                        for ti in range(tn // 128):
                            p = pf.next()
                            for fc in range(FC):
                                k.mm(p[:, :], ac[:, fc, ti * 128:(ti + 1) * 128], wd[:, fc, :], fc == 0, fc == FC - 1,
                                     r=[aq[fc // 11], wdq[cs % 2][fc // 11]], w=[p])
                            f = fring.next()
                            k.actf(f[:, :], p[:, :], AF.Copy, r=[p], w=[f])
                            r0 = t0 + ti * 128
                            k.dma(k.act, fs[r0:r0 + 128, cs * 512:(cs + 1) * 512], f[:, :], r=[f])

            with Phase(k, "p5c_%d" % l) as ph:
                G2 = ph.sb([128, D], F32)
                LG = ph.sb([128, D], F32)
                LB = ph.sb([128, D], F32)
                k.dma(k.sp, LG[:], ln2_g[l:l + 1, :].partition_broadcast(128), w=[LG])
                k.dma(k.sp, LB[:], ln2_b[l:l + 1, :].partition_broadcast(128), w=[LB])
                x1ring = ph.sbring(2, [128, D], F32, "x1")
                fring = ph.sbring(2, [128, D], F32, "f")
                xnring = ph.sbring(2, [128, D], F32, "xn")
                oring = ph.sbring(2, [128, D], F32, "o")
                string = ph.sbring(2, [128, 32], F32, "st")
                cur_mod = None
                for t in range(NTa):
                    mr = 1 if t >= NTL else 0
                    if cur_mod != mr:
                        k.dma(k.sp, G2[:], modrow(5, mr), w=[G2])
                        cur_mod = mr
                    tsl = slice(t * 128, (t + 1) * 128)
                    x1 = x1ring.next()
                    f = fring.next()
                    k.dma(k.sp, x1[:], x1s[tsl, :], w=[x1])
                    k.dma(k.sp, f[:], fs[tsl, :], w=[f])
                    k.tt(k.dve, f[:, :], f[:, :], G2[:, :], ALU.mult, r=[f, G2], w=[f])
                    k.stt(k.dve, f[:, :], x1[:, :], ALPHA, f[:, :], ALU.mult, ALU.add, r=[x1, f], w=[f])
                    xn = xnring.next()
                    layer_norm_rows(k, ph, f, xn, string)
                    k.tt(k.dve, xn[:, :], xn[:, :], LG[:, :], ALU.mult, r=[xn, LG], w=[xn])
                    o = oring.next()
                    k.tt(k.pool, o[:, :], xn[:, :], LB[:, :], ALU.add, r=[xn, LB], w=[o])
                    dst = y_out[tsl, :] if last else xres[tsl, :]
                    k.dma(k.pool, dst, o[:, :], r=[o])
    return nc


_CACHE = {}


def _consts(S):
    if S in _CACHE:
        return _CACHE[S]
    bf = ml_dtypes.bfloat16
    t = np.arange(S)
    row = (t // 64).astype(np.float32)
    col = (t % 64).astype(np.float32)
    inv = (10000.0 ** (-np.arange(32, dtype=np.float32) / 32)).astype(np.float32)
    ang = np.concatenate([row[:, None] * inv[None, :], col[:, None] * inv[None, :]], axis=1).astype(np.float32)
    ropec = np.cos(ang).astype(np.float32)
    ropes = np.sin(ang).astype(np.float32)

    def dft(N, scale):
        i = np.arange(N, dtype=np.int64)
        m = (i[:, None] * i[None, :]) % N
        a = (2.0 * np.pi / N) * m.astype(np.float64)
        return (np.cos(a) * scale), (np.sin(a) * scale)

    c, s = dft(S, 1.0 / np.sqrt(S))
    dftc, dfts = c.astype(np.float32).astype(bf), s.astype(np.float32).astype(bf)
    c, s = dft(L, 1.0 / np.sqrt(L))
    dftc_c, dfts_c = c.astype(np.float32).astype(bf), s.astype(np.float32).astype(bf)
    c, s = dft(128, 1.0 / np.sqrt(128.0))
    c128 = c.astype(np.float32)
    ns128 = (-s).astype(np.float32)
    maskb = np.full((128, 6, 512), NEG, np.float32)
    sj = np.arange(128)[:, None]
    qi = np.arange(128)[None, :]
    for r in range(-1, 5):
        for cblk in range(4):
            d = r - cblk
            if d == -1:
                m = np.where(qi <= sj, 0.0, NEG)
            elif d == 0:
                m = np.zeros((128, 128))
            elif d == 1:
                m = np.where(sj <= qi, 0.0, NEG)
            else:
                continue
            maskb[:, r + 1, cblk * 128:(cblk + 1) * 128] = m
    out = dict(ropec=ropec, ropes=ropes, dftc=dftc, dfts=dfts, dftc_c=dftc_c, dfts_c=dfts_c, c128=c128, ns128=ns128,
               maskb=maskb.astype(bf), identb=np.eye(128, dtype=np.float32).astype(bf),
               identf=np.eye(128, dtype=np.float32))
    _CACHE[S] = out
    return out


_NC = {}


def kernel(x, c, ctx, c_ctx, w_mod, b_mod, w_in, q_gain_a, k_gain_a, sink_b, w_fourier, w_out, ln1_g, ln1_b,
           w_up, w_gate, conv_w, conv_b, w_down, ln2_g, ln2_b, _dbg=False):
    f = lambda a: np.ascontiguousarray(np.asarray(a, dtype=np.float32))
    x = f(x)
    B, S, _ = x.shape
    depth = w_mod.shape[0]
    key = (S, depth, _dbg)
    if key not in _NC:
        _NC[key] = build(S, depth, dbg=_dbg)
    nc = _NC[key]
    cs = _consts(S)
    c = f(c)
    ctx = f(ctx)
    c_ctx = f(c_ctx)
    conv_w = f(conv_w)
    conv_b = f(conv_b)
    cp = np.concatenate([conv_w, conv_b[:, None, :]], axis=1)
    convp = np.ascontiguousarray(cp.reshape(depth, 4, FC, 128).transpose(0, 3, 1, 2))
    shared = dict(w_mod=f(w_mod), b_mod=f(b_mod), w_in=f(w_in), q_gain_a=f(q_gain_a), k_gain_a=f(k_gain_a),
                  sink_b=f(sink_b), w_fourier=f(w_fourier), w_out=f(w_out), ln1_g=f(ln1_g), ln1_b=f(ln1_b),
                  w_up=f(w_up), w_gate=f(w_gate), convp=convp, w_down=f(w_down), ln2_g=f(ln2_g), ln2_b=f(ln2_b))
    shared.update(cs)
    in_maps = []
    for b in range(B):
        cc = np.stack([c[b], c_ctx], axis=0)
        ccT = np.ascontiguousarray(cc.reshape(2, KC, 128).transpose(2, 1, 0))
        m = dict(shared)
        m.update(x=x[b], ctx=ctx[b], ccT=ccT)
        in_maps.append(m)
    res = run_bass_kernel_spmd(nc, in_maps, core_ids=list(range(B)))
    if _dbg:
        return res
    return np.stack([np.asarray(r["y"], dtype=np.float32) for r in res.results], axis=0)
```

```python
from contextlib import ExitStack
import numpy as np
import ml_dtypes
import concourse.bass as bass
import concourse.mybir as mybir
from concourse.bass_utils import run_bass_kernel_spmd

F32 = mybir.dt.float32
BF16 = mybir.dt.bfloat16
AF = mybir.ActivationFunctionType
ALU = mybir.AluOpType
AX = mybir.AxisListType

D = 2048
KC = 16
L = 256
HD = 128
FF = 5632
FC = 44
INW = 3072
EPS = 1e-6
ALPHA = (2.0 * 4) ** 0.25
QSCALE = 128.0 ** -0.5
NEG = -30000.0
NSL = 4
FCS = FC // NSL


class Buf:
    __slots__ = ("w", "r")

    def __init__(self):
        self.w = {}
        self.r = {}


class Sem:
    __slots__ = ("h", "cnt", "step", "sid")

    def __init__(self, h, step, sid):
        self.h = h
        self.cnt = 0
        self.step = step
        self.sid = sid


class Seq:
    def __init__(self, eng, name, inorder=False):
        self.eng = eng
        self.name = name
        self.seen = {}
        self.csem = None
        self.dsems = []
        self.rr = 0
        self.inorder = inorder


class T:
    def __init__(self, h, b=None):
        self.h = h
        self.b = b if b is not None else Buf()

    def __getitem__(self, key):
        return self.h[key]


class Ring:
    def __init__(self, tiles):
        self.tiles = tiles
        self.i = 0

    def next(self):
        t = self.tiles[self.i % len(self.tiles)]
        self.i += 1
        return t


class KB:
    def __init__(self, nc):
        self.nc = nc
        self.stack = ExitStack()
        self.sems = []
        self.pe = Seq(nc.tensor, "pe", inorder=True)
        self.act = Seq(nc.scalar, "act")
        self.dve = Seq(nc.vector, "dve")
        self.pool = Seq(nc.gpsimd, "pool")
        self.sp = Seq(nc.sync, "sp")
        self.seqs = [self.pe, self.act, self.dve, self.pool, self.sp]
        for s in (self.pe, self.act, self.dve, self.pool):
            s.csem = self._sem("c_" + s.name, 1)
        for s, n in ((self.sp, 10), (self.pool, 8), (self.act, 4)):
            s.dsems = [self._sem("d_%s%d" % (s.name, i), 16) for i in range(n)]

    def _sem(self, name, step):
        h = self.stack.enter_context(self.nc.semaphore(name))
        s = Sem(h, step, len(self.sems))
        self.sems.append(s)
        return s

    def emit(self, seq, fn, r=(), w=(), dma=False, sig=True):
        need = {}

        def nd(d):
            for sid, v in d.items():
                if v > need.get(sid, 0):
                    need[sid] = v

        for b in r:
            nd(b.b.w if isinstance(b, T) else b.w)
        for b in w:
            bb = b.b if isinstance(b, T) else b
            nd(bb.r)
            nd(bb.w)
        if dma:
            sem = seq.dsems[seq.rr % len(seq.dsems)]
            seq.rr += 1
            if sem.cnt > 0:
                if sem.cnt > need.get(sem.sid, 0):
                    need[sem.sid] = sem.cnt
        else:
            sem = seq.csem
        for sid, v in need.items():
            if seq.inorder and not dma and sid == seq.csem.sid:
                continue
            if seq.seen.get(sid, 0) >= v:
                continue
            s = self.sems[sid]
            assert v <= s.cnt, "waiting on a pending ticket (%s sid=%d v=%d cnt=%d)" % (seq.name, sid, v, s.cnt)
            seq.eng.wait_ge(s.h, v)
            seq.seen[sid] = v
        ins = fn()
        if sig:
            sem.cnt += sem.step
            ins.then_inc(sem.h, sem.step)
            tk = sem.cnt
        else:
            tk = sem.cnt + sem.step
        for b in r:
            bb = b.b if isinstance(b, T) else b
            if tk > bb.r.get(sem.sid, 0):
                bb.r[sem.sid] = tk
        for b in w:
            bb = b.b if isinstance(b, T) else b
            bb.w = {sem.sid: tk}
            bb.r = {}
        return ins

    def barrier(self):
        for seq in self.seqs:
            for s in self.sems:
                if s.cnt > seq.seen.get(s.sid, 0):
                    seq.eng.wait_ge(s.h, s.cnt)
                    seq.seen[s.sid] = s.cnt

    def dma(self, seq, out, in_, r=(), w=(), **kw):
        return self.emit(seq, lambda: seq.eng.dma_start(out=out, in_=in_, **kw), r=r, w=w, dma=True)

    def mm(self, out, lhsT, rhs, start, stop, r=(), w=(), sig=None):
        if sig is None:
            sig = stop
        return self.emit(self.pe, lambda: self.nc.tensor.matmul(out, lhsT, rhs, start=start, stop=stop),
                         r=r, w=w, sig=sig)

    def tr(self, out, in_, ident, r=(), w=(), sig=True):
        return self.emit(self.pe, lambda: self.nc.tensor.transpose(out=out, in_=in_, identity=ident),
                         r=r, w=w, sig=sig)

    def actf(self, out, in_, func, r=(), w=(), **kw):
        return self.emit(self.act, lambda: self.nc.scalar.activation(out=out, in_=in_, func=func, **kw), r=r, w=w)

    def tt(self, seq, out, in0, in1, op, r=(), w=()):
        return self.emit(seq, lambda: seq.eng.tensor_tensor(out=out, in0=in0, in1=in1, op=op), r=r, w=w)

    def ts(self, seq, out, in0, s1, s2, op0, op1=None, r=(), w=()):
        if op1 is None:
            return self.emit(seq, lambda: seq.eng.tensor_scalar(out=out, in0=in0, scalar1=s1, scalar2=None, op0=op0),
                             r=r, w=w)
        return self.emit(seq, lambda: seq.eng.tensor_scalar(out=out, in0=in0, scalar1=s1, scalar2=s2, op0=op0, op1=op1),
                         r=r, w=w)

    def stt(self, seq, out, in0, scalar, in1, op0, op1, r=(), w=()):
        return self.emit(seq, lambda: seq.eng.scalar_tensor_tensor(out=out, in0=in0, scalar=scalar, in1=in1,
                                                                   op0=op0, op1=op1), r=r, w=w)


class Phase:
    def __init__(self, k, name):
        self.k = k
        self.nc = k.nc
        self.name = name
        self.es = ExitStack()
        self.n = 0

    def __enter__(self):
        self.es.__enter__()
        return self

    def __exit__(self, *a):
        self.k.barrier()
        return self.es.__exit__(*a)

    def sb(self, shape, dt, nm="t"):
        self.n += 1
        h = self.es.enter_context(self.nc.sbuf_tensor("%s_%s%d" % (self.name, nm, self.n), list(shape), dt))
        return T(h)

    def ps(self, shape, dt, nm="p"):
        self.n += 1
        h = self.es.enter_context(self.nc.psum_tensor("%s_%s%d" % (self.name, nm, self.n), list(shape), dt))
        return T(h)

    def const(self, val):
        if not hasattr(self, "_consts"):
            self._consts = {}
        if val not in self._consts:
            t = self.sb([128, 1], F32, "c")
            self.k.emit(self.k.dve, lambda: self.nc.vector.memset(t[:], val), w=[t])
            self._consts[val] = t
        return self._consts[val]

    def sbring(self, n, shape, dt, nm="r"):
        return Ring([self.sb(shape, dt, nm) for _ in range(n)])

    def psring(self, n, shape, dt, nm="pr"):
        return Ring([self.ps(shape, dt, nm) for _ in range(n)])


def layer_norm_rows(k, ph, x, xn, stat_ring, eps=EPS):
    nc = k.nc
    st = stat_ring.next()
    for j in range(4):
        k.emit(k.dve, lambda: nc.vector.bn_stats(out=st[:, j * 6:(j + 1) * 6], in_=x[:, j * 512:(j + 1) * 512]),
               r=[x], w=[st])
    k.emit(k.dve, lambda: nc.vector.bn_aggr(out=st[:, 24:26], in_=st[:, 0:24]), r=[st], w=[st])
    epsT = ph.const(eps)
    k.actf(st[:, 26:27], st[:, 25:26], AF.Sqrt, r=[st, epsT], w=[st], bias=epsT[:, 0:1], scale=1.0)
    k.emit(k.dve, lambda: nc.vector.reciprocal(out=st[:, 26:27], in_=st[:, 26:27]), r=[st], w=[st])
    k.stt(k.dve, st[:, 27:28], st[:, 24:25], -1.0, st[:, 26:27], ALU.mult, ALU.mult, r=[st], w=[st])
    k.actf(xn[:, :], x[:, :], AF.Identity, r=[x, st], w=[xn], bias=st[:, 27:28], scale=st[:, 26:27])


def layer_norm_gen(k, ph, x, xn, stat_ring, eps):
    nc = k.nc
    st = stat_ring.next()
    for j in range(4):
        k.emit(k.dve, lambda: nc.vector.bn_stats(out=st[:, j * 6:(j + 1) * 6], in_=x[:, j * 512:(j + 1) * 512]),
               r=[x], w=[st])
        yield None
    k.emit(k.dve, lambda: nc.vector.bn_aggr(out=st[:, 24:26], in_=st[:, 0:24]), r=[st], w=[st])
    yield None
    epsT = ph.const(eps)
    k.actf(st[:, 26:27], st[:, 25:26], AF.Sqrt, r=[st, epsT], w=[st], bias=epsT[:, 0:1], scale=1.0)
    yield None
    k.emit(k.dve, lambda: nc.vector.reciprocal(out=st[:, 26:27], in_=st[:, 26:27]), r=[st], w=[st])
    yield None
    k.stt(k.dve, st[:, 27:28], st[:, 24:25], -1.0, st[:, 26:27], ALU.mult, ALU.mult, r=[st], w=[st])
    yield None
    k.actf(xn[:, :], x[:, :], AF.Identity, r=[x, st], w=[xn], bias=st[:, 27:28], scale=st[:, 26:27])
    yield None


def build(S, depth, dbg=False):
    assert S % 512 == 0
    NTL = S // 128
    NT = NTL + 2
    NTOK = S + L
    NQC = S // 512
    nc = bass.Bass("TRN2", target_bir_lowering=False)

    def din(name, shape, dt=F32):
        return nc.dram_tensor(name, list(shape), dt, kind="ExternalInput").ap()

    def dscr(name, shape, dt, out=False):
        return nc.dram_tensor(name, list(shape), dt, kind="ExternalOutput" if (out or dbg) else "Internal").ap()

    x_in = din("x", [S, D])
    ctx_in = din("ctx", [L, D])
    ccT = din("ccT", [128, KC, 2])
    w_mod = din("w_mod", [depth, D, 6 * D])
    b_mod = din("b_mod", [depth, 6 * D])
    w_in = din("w_in", [depth, D, INW])
    q_gain = din("q_gain_a", [depth, HD])
    k_gain = din("k_gain_a", [depth, HD])
    sink_b = din("sink_b", [depth, 4])
    w_f = din("w_fourier", [depth, 4, 128, 128])
    w_out = din("w_out", [depth, D, D])
    ln1_g = din("ln1_g", [depth, D])
    ln1_b = din("ln1_b", [depth, D])
    w_up = din("w_up", [depth, D, FF])
    w_gate = din("w_gate", [depth, D, FF])
    convp = din("convp", [depth, 128, 4, FC])
    w_down = din("w_down", [depth, FF, D])
    ln2_g = din("ln2_g", [depth, D])
    ln2_b = din("ln2_b", [depth, D])
    ropec = din("ropec", [S, 64])
    ropes = din("ropes", [S, 64])
    dftc = din("dftc", [S, S], BF16)
    dfts = din("dfts", [S, S], BF16)
    dftc_c = din("dftc_c", [L, L], BF16)
    dfts_c = din("dfts_c", [L, L], BF16)
    c128 = din("c128", [128, 128])
    ns128 = din("ns128", [128, 128])
    maskb_in = din("maskb", [128, 6, 512], BF16)
    identb_in = din("identb", [128, 128], BF16)
    identf_in = din("identf", [128, 128])

    y_out = nc.dram_tensor("y", [S, D], F32, kind="ExternalOutput").ap()

    xres = dscr("xres", [NTOK, D], F32)
    x1s = dscr("x1s", [NTOK, D], F32)
    fs = dscr("fs", [NTOK, D], F32)
    modv = dscr("modv", [depth, 2, 6 * D], F32)
    modT = dscr("modT", [depth, 128, 96, 2], F32)
    qT = dscr("qT", [12, 128, NTOK], BF16)
    kT = dscr("kT", [4, 128, NTOK], BF16)
    vS = dscr("vS", [NTOK, 512], BF16)
    uT = dscr("uT", [4, 128, NTOK], BF16)
    oT = dscr("oT", [KC, 128, NTOK], BF16)
    h2T = dscr("h2T", [KC, 128, NTOK], BF16)
    aT = dscr("aT", [FC, 128, NTOK], BF16)

    k = KB(nc)
    with k.stack:
        with Phase(k, "pro") as ph:
            k.dma(k.sp, xres[0:S, :], x_in[:, :])
            k.dma(k.sp, xres[S:NTOK, :], ctx_in[:, :])

        for l in range(depth):
            last = (l == depth - 1)
            NTa = NTL if last else NT
            with Phase(k, "p0_%d" % l) as ph:
                sc = ph.sb([128, KC, 2], F32)
                k.dma(k.sp, sc[:], ccT[:, :, :], w=[sc])
                scs = ph.sb([128, KC, 2], F32)
                k.actf(scs[:], sc[:], AF.Silu, r=[sc], w=[scs])
                bm = ph.sb([2, 6 * D], F32)
                k.dma(k.sp, bm[:], b_mod[l:l + 1, :].partition_broadcast(2), w=[bm])
                mo = ph.sb([2, 6 * D], F32)
                wring = ph.sbring(12, [128, 4, 512], F32, "wm")
                pring = ph.psring(2, [2, 512], F32)
                wv = w_mod[l].rearrange("(kc p) n -> p kc n", p=128)
                for cb in range(24):
                    wq = []
                    for q4 in range(4):
                        wt = wring.next()
                        k.dma(k.sp, wt[:, :, :], wv[:, q4 * 4:(q4 + 1) * 4, cb * 512:(cb + 1) * 512], w=[wt])
                        wq.append(wt)
                    p = pring.next()
                    for kc in range(KC):
                        wt = wq[kc // 4]
                        k.mm(p[:, :], scs[:, kc, :], wt[:, kc % 4, :], kc == 0, kc == KC - 1, r=[scs, wt], w=[p])
                    k.tt(k.dve, mo[:, cb * 512:(cb + 1) * 512], p[:, :], bm[:, cb * 512:(cb + 1) * 512], ALU.add,
                         r=[p, bm], w=[mo])
                for a in (1, 4):
                    k.ts(k.dve, mo[:, a * D:(a + 1) * D], mo[:, a * D:(a + 1) * D], 1.0, None, ALU.add, r=[mo], w=[mo])
                k.dma(k.sp, modv[l], mo[:], r=[mo])
                idf = ph.sb([2, 2], F32)
                k.dma(k.sp, idf[:], identf_in[0:2, 0:2], w=[idf])
                pT = ph.ps([128, 96 * 2], F32, "pT")
                for j in range(96):
                    k.tr(pT[:, 2 * j:2 * j + 2], mo[:, j * 128:(j + 1) * 128], idf[:], r=[mo, idf], w=[pT], sig=(j == 95))
                moT = ph.sb([128, 96 * 2], F32)
                k.actf(moT[:], pT[:], AF.Copy, r=[pT], w=[moT])
                k.dma(k.sp, modT[l].rearrange("p a b -> p (a b)"), moT[:], r=[moT])

            def modrow(idx, r):
                return modv[l, r:r + 1, idx * D:(idx + 1) * D].partition_broadcast(128)

            with Phase(k, "p1_%d" % l) as ph:
                wi = ph.sb([128, KC, INW], BF16, "wi")
                wiv = w_in[l].rearrange("(kc p) n -> p kc n", p=128)
                wiq = [{dc: T(wi.h) for dc in (0, 1536, 1792, 2048, 2304)} for _ in range(KC)]
                colmap = ((0, 0, 1536), (1536, 1536, 256), (1792, 2048, 256), (2048, 1792, 256), (2304, 2304, 768))
                for kc in range(KC):
                    for (dc, sc_, n_) in colmap:
                        hh = 0 if dc < 1536 else 1
                        k.dma(k.pool, wi[:, kc, dc:dc + n_], wiv[:, kc, sc_:sc_ + n_], w=[wiq[kc][dc]])
                identb = ph.sb([128, 128], BF16)
                k.dma(k.sp, identb[:], identb_in[:, :], w=[identb])
                mT = ph.sb([128, 96 * 2], F32, "mT")
                k.dma(k.sp, mT[:], modT[l].rearrange("p a b -> p (a b)"), w=[mT])
                qg = ph.sb([128, HD], F32)
                kg = ph.sb([128, HD], F32)
                k.dma(k.sp, qg[:], q_gain[l:l + 1, :].partition_broadcast(128), w=[qg])
                k.dma(k.sp, kg[:], k_gain[l:l + 1, :].partition_broadcast(128), w=[kg])
                k.ts(k.dve, qg[:], qg[:], QSCALE, None, ALU.mult, r=[qg], w=[qg])
                rcring = ph.sbring(2, [128, 64], F32, "rc")
                rsring = ph.sbring(2, [128, 64], F32, "rs")
                xring = ph.sbring(2, [128, D], F32, "x")
                string = ph.sbring(2, [128, 32], F32, "st")
                hbring = ph.sbring(2, [128, D], BF16, "hb")
                hTring = ph.sbring(2, [128, KC, 128], BF16, "hT")
                tp = [ph.ps([128, 1024], BF16, "tp") for _ in range(2)]
                pb = [ph.ps([128, 512], F32, "pb") for _ in range(6)]
                psb = ph.sb([128, INW], F32, "psb")
                psbq = T(psb.h)
                psbv = T(psb.h)
                sqring = ph.sbring(2, [128, 512], F32, "sq")
                ssring = ph.sbring(2, [128, 16], F32, "ss")
                qkrring = ph.sbring(2, [128, D], BF16, "qkr")
                tmpA = ph.sbring(1, [128, 16, 2, 32], F32, "ta")
                tmpB = ph.sbring(1, [128, 16, 2, 32], F32, "tb")
                vtring = ph.sbring(2, [128, 512], BF16, "vt")
                utring = ph.sbring(2, [128, 512], BF16, "ut")
                qTring = ph.sbring(2, [128, 16, 128], BF16, "qT")
                uTring = ph.sbring(2, [128, 4, 128], BF16, "uT")
                state = {"mod": None}

                def stageA1(t):
                    x = xring.next()
                    k.dma(k.sp, x[:], xres[t * 128:(t + 1) * 128, :], w=[x])
                    hb = hbring.next()
                    layer_norm_rows(k, ph, x, hb, string)
                    return hb

                def stageA2(t, hb):
                    isctx = t >= NTL
                    mr = 1 if isctx else 0
                    hT = hTring.next()
                    for half in range(2):
                        for j in range(8):
                            kc = half * 8 + j
                            k.tr(tp[half][:, j * 128:(j + 1) * 128], hb[:, kc * 128:(kc + 1) * 128], identb[:],
                                 r=[hb, identb], w=[tp[half]], sig=(j == 7))
                        for j in range(8):
                            kc = half * 8 + j
                            k.actf(hT[:, kc, :], tp[half][:, j * 128:(j + 1) * 128], AF.Identity, r=[tp[half], mT], w=[hT],
                                   scale=mT[:, (16 + kc) * 2 + mr:(16 + kc) * 2 + mr + 1],
                                   bias=mT[:, kc * 2 + mr:kc * 2 + mr + 1])
                    return hT

                def stageB(t, hT, half):
                    if True:
                        for kc in range(KC):
                            for cb in range(half * 3, half * 3 + 3):
                                k.mm(pb[cb][:, :], hT[:, kc, :], wi[:, kc, cb * 512:(cb + 1) * 512], kc == 0, kc == KC - 1,
                                     r=[hT] + ([wiq[kc][0]] if half == 0 else [wiq[kc][dc] for dc in (1536, 1792, 2048, 2304)]), w=[pb[cb]])

                def stageC0(t, lo, hi):
                    for cb in range(lo, hi):
                        dstb = psbq if cb < 4 else psbv
                        if cb % 2 == 0:
                            k.actf(psb[:, cb * 512:(cb + 1) * 512], pb[cb][:, :], AF.Copy, r=[pb[cb]], w=[dstb])
                        else:
                            k.emit(k.dve, lambda: nc.vector.tensor_copy(out=psb[:, cb * 512:(cb + 1) * 512], in_=pb[cb][:, :]),
                                   r=[pb[cb]], w=[dstb])

                def stageC(t):
                    isctx = t >= NTL
                    ss = ssring.next()
                    for (c0, nh, ofs) in ((0, 4, 0), (512, 4, 4), (1536, 2, 8)):
                        sq = sqring.next()
                        k.actf(sq[:, 0:nh * 128], psb[:, c0:c0 + nh * 128], AF.Square, r=[psbq], w=[sq])
                        k.emit(k.dve, lambda: nc.vector.tensor_reduce(
                            out=ss[:, ofs:ofs + nh], in_=sq[:, 0:nh * 128].rearrange("p (h d) -> p h d", d=128),
                            axis=AX.X, op=ALU.add), r=[sq], w=[ss])
                    epsT = ph.const(EPS)
                    k.actf(ss[:, 0:10], ss[:, 0:10], AF.Sqrt, r=[ss, epsT], w=[ss], bias=epsT[:, 0:1], scale=1.0 / 128.0)
                    k.emit(k.dve, lambda: nc.vector.reciprocal(out=ss[:, 0:10], in_=ss[:, 0:10]), r=[ss], w=[ss])
                    for h in range(8):
                        k.stt(k.dve, psb[:, h * 128:(h + 1) * 128], psb[:, h * 128:(h + 1) * 128],
                              ss[:, h:h + 1], qg[:, :], ALU.mult, ALU.mult, r=[psbq, ss, qg], w=[psbq])
                    k.actf(psb[:, 1024:1536], psb[:, 1024:1536], AF.Copy, r=[psbq], w=[psbq], scale=QSCALE)
                    for h in range(2):
                        c0 = 1536 + h * 128
                        k.stt(k.dve, psb[:, c0:c0 + 128], psb[:, c0:c0 + 128],
                              ss[:, 8 + h:9 + h], kg[:, :], ALU.mult, ALU.mult, r=[psbq, ss, kg], w=[psbq])
                    vt = vtring.next()
                    k.actf(vt[:, :], psb[:, 2048:2560], AF.Copy, r=[psbv], w=[vt])
                    ut = utring.next()
                    k.actf(ut[:, :], psb[:, 2560:3072], AF.Copy, r=[psbv], w=[ut])
                    qkr = qkrring.next()
                    if isctx:
                        k.actf(qkr[:, :], psb[:, 0:2048], AF.Copy, r=[psbq], w=[qkr])
                    else:
                        q5 = psb[:, 0:2048].rearrange("p (h x y f) -> p h x y f", h=16, x=2, y=2)
                        o5 = qkr[:, :].rearrange("p (h x y f) -> p h x y f", h=16, x=2, y=2)
                        a_ = q5[:, :, :, 0, :]
                        b_ = q5[:, :, :, 1, :]
                        rc = rcring.next()
                        rs_ = rsring.next()
                        k.dma(k.sp, rc[:], ropec[t * 128:(t + 1) * 128, :], w=[rc])
                        k.dma(k.sp, rs_[:], ropes[t * 128:(t + 1) * 128, :], w=[rs_])
                        cc = rc[:, :].rearrange("p (x f) -> p x f", x=2).unsqueeze(1).to_broadcast([128, 16, 2, 32])
                        sn = rs_[:, :].rearrange("p (x f) -> p x f", x=2).unsqueeze(1).to_broadcast([128, 16, 2, 32])
                        ta = tmpA.next()
                        tb = tmpB.next()
                        k.tt(k.dve, ta[:], a_, cc, ALU.mult, r=[psbq, rc], w=[ta])
                        k.tt(k.pool, tb[:], b_, sn, ALU.mult, r=[psbq, rs_], w=[tb])
                        k.tt(k.dve, o5[:, :, :, 0, :], ta[:], tb[:], ALU.subtract, r=[ta, tb], w=[qkr])
                        ta = tmpA.next()
                        tb = tmpB.next()
                        k.tt(k.pool, ta[:], a_, sn, ALU.mult, r=[psbq, rs_], w=[ta])
                        k.tt(k.dve, tb[:], b_, cc, ALU.mult, r=[psbq, rc], w=[tb])
                        k.tt(k.pool, o5[:, :, :, 1, :], ta[:], tb[:], ALU.add, r=[ta, tb], w=[qkr])
                    return (qkr, ut, vt)

                def stageD(t, qkr, ut, vt):
                    qTt = qTring.next()
                    for half in range(2):
                        for j in range(8):
                            c = half * 8 + j
                            k.tr(tp[half][:, j * 128:(j + 1) * 128], qkr[:, c * 128:(c + 1) * 128], identb[:],
                                 r=[qkr, identb], w=[tp[half]], sig=(j == 7))
                        k.emit(k.dve, lambda: nc.vector.tensor_copy(
                            out=qTt[:, half * 8:(half + 1) * 8, :],
                            in_=tp[half][:, :].rearrange("p (a b) -> p a b", b=128)), r=[tp[half]], w=[qTt])
                    uTt = uTring.next()
                    for j in range(4):
                        k.tr(tp[0][:, j * 128:(j + 1) * 128], ut[:, j * 128:(j + 1) * 128], identb[:],
                             r=[ut, identb], w=[tp[0]], sig=(j == 3))
                    k.emit(k.dve, lambda: nc.vector.tensor_copy(
                        out=uTt[:, :, :], in_=tp[0][:, 0:512].rearrange("p (a b) -> p a b", b=128)), r=[tp[0]], w=[uTt])
                    tsl = slice(t * 128, (t + 1) * 128)
                    k.dma(k.pool, qT[:, :, tsl].rearrange("h p n -> p h n"), qTt[:, 0:12, :], r=[qTt])
                    k.dma(k.pool, kT[:, :, tsl].rearrange("h p n -> p h n"), qTt[:, 12:16, :], r=[qTt])
                    k.dma(k.pool, uT[:, :, tsl].rearrange("h p n -> p h n"), uTt[:, :, :], r=[uTt])
                    k.dma(k.pool, vS[tsl, :], vt[:, :], r=[vt])

                hbs = {0: stageA1(0)}
                if NT > 1:
                    hbs[1] = stageA1(1)
                hT_cur = stageA2(0, hbs.pop(0))
                prevC = None
                for t in range(NT):
                    stageB(t, hT_cur, 0)
                    hT_nxt = stageA2(t + 1, hbs.pop(t + 1)) if t + 1 < NT else None
                    stageC0(t, 0, 3)
                    stageB(t, hT_cur, 1)
                    if prevC is not None:
                        stageD(t - 1, *prevC)
                    stageC0(t, 3, 6)
                    if t + 2 < NT:
                        hbs[t + 2] = stageA1(t + 2)
                    prevC = stageC(t)
                    hT_cur = hT_nxt
                stageD(NT - 1, *prevC)

            with Phase(k, "p2_%d" % l) as ph:
                kTs = ph.sb([128, 4, NTOK], BF16, "kTs")
                kTb = [T(kTs.h) for _ in range(4)]
                for h in range(4):
                    k.dma(k.sp, kTs[:, h, :], kT[h], w=[kTb[h]])
                vs = ph.sb([128, NT, 512], BF16, "vs")
                k.dma(k.sp, vs[:], vS.rearrange("(t p) c -> p t c", p=128), w=[vs])
                ones = ph.sb([128, 128], F32)
                k.emit(k.dve, lambda: nc.vector.memset(ones[:], 1.0), w=[ones])
                onesb = ph.sb([128, 128], BF16)
                k.emit(k.dve, lambda: nc.vector.memset(onesb[:], 1.0), w=[onesb])
                accsets = [[ph.sb([128, 512], F32, "acc") for _ in range(5)] for _ in range(2)]
                chunk_i = [0]
                identb = ph.sb([128, 128], BF16)
                k.dma(k.sp, identb[:], identb_in[:, :], w=[identb])
                maskb = ph.sb([128, 6, 512], BF16)
                k.dma(k.sp, maskb[:], maskb_in[:, :, :], w=[maskb])
                es = ph.sb([128, 4], F32)
                k.dma(k.sp, es[:], sink_b[l:l + 1, :].partition_broadcast(128), w=[es])
                k.actf(es[:], es[:], AF.Exp, r=[es], w=[es])
                qring = ph.sbring(2, [128, NTOK], BF16, "q")
                pring = ph.sbring(10, [128, 512], BF16, "pt")
                oring = ph.sbring(2, [128, 512], BF16, "ot")
                rdring = ph.sbring(2, [128, 512], F32, "rd")
                ps_s = ph.psring(3, [128, 512], F32, "s")
                ps_o = ph.psring(2, [128, 512], F32, "o")
                ps_d = ph.psring(2, [128, 512], F32, "d")
                for hq in range(12):
                    isB = hq >= 8
                    kv = (hq // 4) if not isB else (2 + (hq - 8) // 2)
                    q = qring.next()
                    k.dma(k.sp, q[:], qT[hq], w=[q])
                    chunks = [(qc * 512, 512, False) for qc in range(NQC)] + ([] if last else [(S, 256, True)])
                    for (q0, qn, isctx) in chunks:
                        if isctx:
                            tiles = [(NTL, None), (NTL + 1, None)]
                        elif not isB:
                            tiles = [(t, None) for t in range(NT)]
                        else:
                            n0 = q0 // 128
                            tiles = [(n0 + r, r + 1) for r in range(-1, 5) if 0 <= n0 + r < NTL]
                            tiles += [(NTL, None), (NTL + 1, None)]
                        po = ps_o.next()
                        pd = ps_d.next()
                        n = len(tiles)

                        def qk(i):
                            st, mi = tiles[i]
                            p = ps_s.next()
                            k.mm(p[:, 0:qn], kTs[:, kv, st * 128:(st + 1) * 128], q[:, q0:q0 + qn], True, mi is None,
                                 r=[kTb[kv], q], w=[p])
                            if mi is not None:
                                k.mm(p[:, 0:qn], identb[:], maskb[:, mi, 0:qn], False, True, r=[identb, maskb], w=[p])
                            return p

                        LA = 2
                        pend = [qk(i) for i in range(min(LA, n))]
                        accs = accsets[chunk_i[0] % 2]
                        chunk_i[0] += 1
                        used = [False] * 5
                        pd_started = [False]
                        plan = ((0, k.dve), (None, None), (3, k.pool), (1, k.dve), (None, None), (2, k.dve), (4, k.pool),
                                (None, None))
                        for i in range(n):
                            p = pend.pop(0)
                            pt = pring.next()
                            k.actf(pt[:, 0:qn], p[:, 0:qn], AF.Exp, r=[p], w=[pt])
                            if i + LA < n:
                                pend.append(qk(i + LA))
                            st = tiles[i][0]
                            k.mm(po[:, 0:qn], vs[:, st, kv * 128:(kv + 1) * 128], pt[:, 0:qn], i == 0, i == n - 1,
                                 r=[vs, pt], w=[po])
                            ai, seq_ = plan[i % 8]
                            if ai is None:
                                k.mm(pd[:, 0:qn], onesb[:], pt[:, 0:qn], not pd_started[0], False, r=[onesb, pt], w=[pd],
                                     sig=True)
                                pd_started[0] = True
                            else:
                                acc = accs[ai]
                                if not used[ai]:
                                    k.emit(seq_, lambda: seq_.eng.tensor_copy(out=acc[:, 0:qn], in_=pt[:, 0:qn]),
                                           r=[pt], w=[acc])
                                    used[ai] = True
                                else:
                                    k.tt(seq_, acc[:, 0:qn], acc[:, 0:qn], pt[:, 0:qn], ALU.add, r=[acc, pt], w=[acc])
                        ua = [a for a, u in zip(accs, used) if u]
                        for j, acc in enumerate(ua):
                            k.mm(pd[:, 0:qn], ones[:], acc[:, 0:qn], not pd_started[0], j == len(ua) - 1, r=[ones, acc], w=[pd])
                            pd_started[0] = True
                        rd = rdring.next()
                        if isB:
                            k.ts(k.dve, rd[:, 0:qn], pd[:, 0:qn], es[:, hq - 8:hq - 7], None, ALU.add, r=[pd, es], w=[rd])
                            k.emit(k.dve, lambda: nc.vector.reciprocal(out=rd[:, 0:qn], in_=rd[:, 0:qn]), r=[rd], w=[rd])
                        else:
                            k.emit(k.dve, lambda: nc.vector.reciprocal(out=rd[:, 0:qn], in_=pd[:, 0:qn]), r=[pd], w=[rd])
                        ot = oring.next()
                        k.tt(k.dve, ot[:, 0:qn], po[:, 0:qn], rd[:, 0:qn], ALU.mult, r=[po, rd], w=[ot])
                        k.dma(k.pool, oT[hq, :, q0:q0 + qn], ot[:, 0:qn], r=[ot])

            with Phase(k, "p3_%d" % l) as ph:
                c128t = ph.sb([128, 128], F32)
                ns128t = ph.sb([128, 128], F32)
                k.dma(k.sp, c128t[:], c128[:, :], w=[c128t])
                k.dma(k.sp, ns128t[:], ns128[:, :], w=[ns128t])
                wf = ph.sb([128, 4, 128], F32)
                k.dma(k.sp, wf[:], w_f[l].rearrange("g c e -> c g e"), w=[wf])
                AB = ph.sb([128, 4, 256], BF16)
                pw = ph.psring(2, [128, 512], F32, "pw")
                pacc = ph.psring(6, [128, 512], F32, "pa")
                for g in range(4):
                    p = pw.next()
                    k.mm(p[:, 0:128], c128t[:], wf[:, g, :], True, True, r=[c128t, wf], w=[p])
                    k.mm(p[:, 128:256], ns128t[:], wf[:, g, :], True, True, r=[ns128t, wf], w=[p])
                    k.actf(AB[:, g, :], p[:, 0:256], AF.Copy, r=[p], w=[AB])
                cring = ph.sbring(3, [128, 8, 512], BF16, "dc")
                sring = ph.sbring(3, [128, 8, 512], BF16, "ds")
                foring = ph.sbring(3, [128, 512], BF16, "fo")
                segs = [(S, 0, dftc, dfts, 512)] + ([] if last else [(L, S, dftc_c, dfts_c, 256)])
                for (N, off, Ct, St, CH) in segs:
                    MT = N // 128
                    uts = ph.sb([128, 4, N], BF16, "uts")
                    utb = [T(uts.h) for _ in range(4)]
                    for g in range(4):
                        k.dma(k.sp, uts[:, g, :], uT[g, :, off:off + N], w=[utb[g]])
                    UAB = ph.sb([128, MT, 4, 256], BF16, "uab")
                    uabb = [T(UAB.h) for _ in range(MT)]
                    for mt in range(MT):
                        for gp in range(2):
                            p = pw.next()
                            for gg in range(2):
                                g = gp * 2 + gg
                                k.mm(p[:, gg * 256:(gg + 1) * 256], uts[:, g, mt * 128:(mt + 1) * 128], AB[:, g, :], True, True,
                                     r=[utb[g], AB], w=[p])
                            if gp == 0:
                                k.actf(UAB[:, mt, 0:2, :], p[:, :].rearrange("p (a b) -> p a b", b=256), AF.Copy,
                                       r=[p], w=[uabb[mt]])
                            else:
                                k.emit(k.dve, lambda: nc.vector.tensor_copy(
                                    out=UAB[:, mt, 2:4, :], in_=p[:, :].rearrange("p (a b) -> p a b", b=256)),
                                    r=[p], w=[uabb[mt]])
                    for nci in range(N // CH):
                        banks = [pacc.next() for _ in range(4)]
                        for mg in range(0, MT, 8):
                            mcount = min(8, MT - mg)
                            ct = cring.next()
                            st_ = sring.next()
                            k.dma(k.sp, ct[:, 0:mcount, 0:CH],
                                  Ct[mg * 128:(mg + mcount) * 128, nci * CH:(nci + 1) * CH].rearrange("(m p) n -> p m n", p=128),
                                  w=[ct])
                            k.dma(k.sp, st_[:, 0:mcount, 0:CH],
                                  St[mg * 128:(mg + mcount) * 128, nci * CH:(nci + 1) * CH].rearrange("(m p) n -> p m n", p=128),
                                  w=[st_])
                            for mi in range(mcount):
                                mt = mg + mi
                                for g in range(4):
                                    k.mm(banks[g][:, 0:CH], UAB[:, mt, g, 0:128], ct[:, mi, 0:CH], mt == 0, False,
                                         r=[uabb[mt], ct], w=[banks[g]])
                                for g in range(4):
                                    k.mm(banks[g][:, 0:CH], UAB[:, mt, g, 128:256], st_[:, mi, 0:CH], False, mt == MT - 1,
                                         r=[uabb[mt], st_], w=[banks[g]],
                                         sig=(mt == MT - 1) or (mi == mcount - 1 and g == 3))
                        for g in range(4):
                            fo = foring.next()
                            k.actf(fo[:, 0:CH], banks[g][:, 0:CH], AF.Copy, r=[banks[g]], w=[fo])
                            k.dma(k.pool, oT[12 + g, :, off + nci * CH:off + (nci + 1) * CH], fo[:, 0:CH], r=[fo])

            with Phase(k, "p4_%d" % l) as ph:
                wo = ph.sb([128, KC, D], BF16, "wo")
                wob = [T(wo.h) for _ in range(KC)]
                wov = w_out[l].rearrange("(kc p) n -> p kc n", p=128)
                for kc in range(KC):
                    k.dma(k.pool, wo[:, kc, :], wov[:, kc, :], w=[wob[kc]])
                identb = ph.sb([128, 128], BF16)
                k.dma(k.sp, identb[:], identb_in[:, :], w=[identb])
                G1 = ph.sb([128, D], F32)
                LG = ph.sb([128, D], F32)
                LB = ph.sb([128, D], F32)
                mT = ph.sb([128, 96 * 2], F32, "mT")
                k.dma(k.sp, mT[:], modT[l].rearrange("p a b -> p (a b)"), w=[mT])
                k.dma(k.sp, LG[:], ln1_g[l:l + 1, :].partition_broadcast(128), w=[LG])
                k.dma(k.sp, LB[:], ln1_b[l:l + 1, :].partition_broadcast(128), w=[LB])
                ocring = ph.sbring(2, [128, KC, 128], BF16, "oc")
                xring = ph.sbring(3, [128, D], F32, "x")
                vring = ph.sbring(2, [128, D], F32, "v")
                xnring = ph.sbring(1, [128, D], F32, "xn")
                x1ring = ph.sbring(3, [128, D], F32, "x1")
                string = ph.sbring(4, [128, 32], F32, "st")
                hbring = ph.sbring(4, [128, D], BF16, "hb")
                hTring = ph.sbring(2, [128, KC, 128], BF16, "hT")
                tp = [ph.ps([128, 1024], BF16, "tp") for _ in range(2)]
                pb = [ph.ps([128, 512], F32, "pb") for _ in range(4)]
                state = {"mod": None}

                def stageA(t):
                    tsl = slice(t * 128, (t + 1) * 128)
                    oc = ocring.next()
                    k.dma(k.sp, oc[:], oT[:, :, tsl].rearrange("c p n -> p c n"), w=[oc])
                    x = xring.next()
                    k.dma(k.sp, x[:], xres[tsl, :], w=[x])
                    return (oc, x)

                def stageB(t, oc, half):
                    if True:
                        for kc in range(KC):
                            for cb in (2 * half, 2 * half + 1):
                                k.mm(pb[cb][:, :], oc[:, kc, :], wo[:, kc, cb * 512:(cb + 1) * 512], kc == 0, kc == KC - 1,
                                     r=[oc, wob[kc]], w=[pb[cb]])

                def stageC0(t):
                    mr = 1 if t >= NTL else 0
                    if state["mod"] != mr:
                        k.dma(k.sp, G1[:], modrow(2, mr), w=[G1])
                        k.ts(k.dve, G1[:, :], G1[:, :], 1.0 / ALPHA, None, ALU.mult, r=[G1], w=[G1])
                        state["mod"] = mr
                    tsl = slice(t * 128, (t + 1) * 128)
                    v = vring.next()
                    for cb in range(0, 2):
                        k.tt(k.dve, v[:, cb * 512:(cb + 1) * 512], pb[cb][:, :], G1[:, cb * 512:(cb + 1) * 512], ALU.mult,
                             r=[pb[cb], G1], w=[v])
                    return v

                def stageCa(t, x, v):
                    tsl = slice(t * 128, (t + 1) * 128)
                    for cb in range(2, 4):
                        k.tt(k.dve, v[:, cb * 512:(cb + 1) * 512], pb[cb][:, :], G1[:, cb * 512:(cb + 1) * 512], ALU.mult,
                             r=[pb[cb], G1], w=[v])
                        yield None
                    k.tt(k.pool, v[:, :], x[:, :], v[:, :], ALU.add, r=[x, v], w=[v])
                    yield None
                    xn = xnring.next()
                    for _ in layer_norm_gen(k, ph, v, xn, string, EPS / (ALPHA * ALPHA)):
                        yield None
                    k.tt(k.dve, xn[:, :], xn[:, :], LG[:, :], ALU.mult, r=[xn, LG], w=[xn])
                    yield None
                    x1 = x1ring.next()
                    k.tt(k.pool, x1[:, :], xn[:, :], LB[:, :], ALU.add, r=[xn, LB], w=[x1])
                    k.dma(k.pool, x1s[tsl, :], x1[:, :], r=[x1])
                    yield x1

                def stageCb(t, x1):
                    hb = hbring.next()
                    for _ in layer_norm_gen(k, ph, x1, hb, string, EPS):
                        yield None
                    yield hb

                def run_interleaved(ga, gb):
                    ra = rb = None
                    da = ga is None
                    db = gb is None
                    while not (da and db):
                        if not da:
                            try:
                                r_ = next(ga)
                                if r_ is not None:
                                    ra = r_
                            except StopIteration:
                                da = True
                        if not db:
                            try:
                                r_ = next(gb)
                                if r_ is not None:
                                    rb = r_
                            except StopIteration:
                                db = True
                    return ra, rb

                def stageD(t, hb):
                    mr = 1 if t >= NTL else 0
                    tsl = slice(t * 128, (t + 1) * 128)
                    hT = hTring.next()
                    for half in range(2):
                        for j in range(8):
                            kc = half * 8 + j
                            k.tr(tp[half][:, j * 128:(j + 1) * 128], hb[:, kc * 128:(kc + 1) * 128], identb[:],
                                 r=[hb, identb], w=[tp[half]], sig=(j == 7))
                        for j in range(8):
                            kc = half * 8 + j
                            k.actf(hT[:, kc, :], tp[half][:, j * 128:(j + 1) * 128], AF.Identity, r=[tp[half], mT], w=[hT],
                                   scale=mT[:, (64 + kc) * 2 + mr:(64 + kc) * 2 + mr + 1],
                                   bias=mT[:, (48 + kc) * 2 + mr:(48 + kc) * 2 + mr + 1])
                    k.dma(k.act, h2T[:, :, tsl].rearrange("c p n -> p c n"), hT[:, :, :], r=[hT])

                curA = stageA(0)
                hbs = {}
                x1prev = None
                for t in range(NTa):
                    stageB(t, curA[0], 0)
                    nxtA = stageA(t + 1) if t + 1 < NTa else None
                    v = stageC0(t)
                    stageB(t, curA[0], 1)
                    if t - 3 in hbs:
                        stageD(t - 3, hbs.pop(t - 3))
                    ga = stageCa(t, curA[1], v)
                    gb = stageCb(t - 1, x1prev) if x1prev is not None else None
                    x1cur, hbp = run_interleaved(ga, gb)
                    if hbp is not None:
                        hbs[t - 1] = hbp
                    x1prev = x1cur
                    curA = nxtA
                _, hbp = run_interleaved(None, stageCb(NTa - 1, x1prev))
                hbs[NTa - 1] = hbp
                for t in sorted(hbs):
                    stageD(t, hbs[t])

            tchunks = [(cq * 512, 512) for cq in range(NQC)] + ([] if last else [(S, 256)])
            NS5 = 11
            FS5 = FC // NS5
            with Phase(k, "p5a_%d" % l) as ph:
                wslots = []
                for i in range(2):
                    wu = ph.sb([128, KC, FS5 * 128], BF16, "wu")
                    wg = ph.sb([128, KC, FS5 * 128], BF16, "wg")
                    wslots.append((wu, wg, [T(wu.h) for _ in range(KC)], [T(wg.h) for _ in range(KC)]))
                wuv = w_up[l].rearrange("(kc p) n -> p kc n", p=128)
                wgv = w_gate[l].rearrange("(kc p) n -> p kc n", p=128)
                cp = ph.sb([128, 4, FC], F32)
                k.dma(k.sp, cp[:], convp[l], w=[cp])
                hring = ph.sbring(2, [128, KC, 514], BF16, "hc")
                gsring = ph.sbring(2, [128, 514], F32, "gs")
                accring = ph.sbring(2, [128, 512], F32, "acc")
                sgring = ph.sbring(2, [128, 512], F32, "sg")
                aring = ph.sbring(2, [128, FS5, 512], BF16, "at")
                pu = ph.psring(2, [128, 512], F32, "pu")
                pg = ph.psring(2, [128, 512], F32, "pg")
                phl = ph.psring(2, [128, 2], F32, "ph")

                def loadw(sl):
                    wu, wg, wub, wgb = wslots[sl % 2]
                    cs0 = sl * FS5 * 128
                    for kc in range(KC):
                        k.dma(k.pool, wg[:, kc, :], wgv[:, kc, cs0:cs0 + FS5 * 128], w=[wgb[kc]])
                        k.dma(k.pool, wu[:, kc, :], wuv[:, kc, cs0:cs0 + FS5 * 128], w=[wub[kc]])

                loadw(0)
                for sl in range(NS5):
                    if sl + 1 < NS5:
                        loadw(sl + 1)
                    wu, wg, wub, wgb = wslots[sl % 2]
                    for (t0, tn) in tchunks:
                        hc = hring.next()
                        lo_valid = (t0 > 0 and t0 < S)
                        hi_valid = (t0 + tn < S)
                        a0 = t0 - (1 if lo_valid else 0)
                        a1 = t0 + tn + (1 if hi_valid else 0)
                        d0 = 0 if lo_valid else 1
                        k.dma(k.sp, hc[:, :, d0:d0 + (a1 - a0)], h2T[:, :, a0:a1].rearrange("c p n -> p c n"), w=[hc])
                        if not lo_valid:
                            k.emit(k.dve, lambda: nc.vector.memset(hc[:, :, 0:1], 0.0), w=[hc])
                        if not hi_valid:
                            k.emit(k.dve, lambda: nc.vector.memset(hc[:, :, tn + 1:tn + 2], 0.0), w=[hc])
                        at = aring.next()
                        for fi in range(FS5):
                            fc = sl * FS5 + fi
                            pU = pu.next()
                            pG = pg.next()
                            pH = phl.next()
                            wsl = slice(fi * 128, (fi + 1) * 128)
                            for kc in range(KC):
                                k.mm(pG[:, 0:tn], wg[:, kc, wsl], hc[:, kc, 1:1 + tn], kc == 0, kc == KC - 1,
                                     r=[wgb[kc], hc], w=[pG])
                            for kc in range(KC):
                                k.mm(pH[:, 0:2], wg[:, kc, wsl], hc[:, kc, 0:tn + 2:tn + 1], kc == 0, kc == KC - 1,
                                     r=[wgb[kc], hc], w=[pH])
                            for kc in range(KC):
                                k.mm(pU[:, 0:tn], wu[:, kc, wsl], hc[:, kc, 1:1 + tn], kc == 0, kc == KC - 1,
                                     r=[wub[kc], hc], w=[pU])
                            gs = gsring.next()
                            k.actf(gs[:, 1:1 + tn], pG[:, 0:tn], AF.Copy, r=[pG], w=[gs])
                            k.actf(gs[:, 0:tn + 2:tn + 1], pH[:, 0:2], AF.Copy, r=[pH], w=[gs])
                            acc = accring.next()
                            k.ts(k.dve, acc[:, 0:tn], gs[:, 1:1 + tn], cp[:, 1, fc:fc + 1], cp[:, 3, fc:fc + 1],
                                 ALU.mult, ALU.add, r=[gs, cp], w=[acc])
                            k.stt(k.dve, acc[:, 0:tn], gs[:, 0:tn], cp[:, 0, fc:fc + 1], acc[:, 0:tn], ALU.mult, ALU.add,
                                  r=[gs, cp, acc], w=[acc])
                            k.stt(k.dve, acc[:, 0:tn], gs[:, 2:2 + tn], cp[:, 2, fc:fc + 1], acc[:, 0:tn], ALU.mult, ALU.add,
                                  r=[gs, cp, acc], w=[acc])
                            sg = sgring.next()
                            k.actf(sg[:, 0:tn], acc[:, 0:tn], AF.Silu, r=[acc], w=[sg])
                            k.tt(k.dve, at[:, fi, 0:tn], sg[:, 0:tn], pU[:, 0:tn], ALU.mult, r=[sg, pU], w=[at])
                        k.dma(k.pool, aT[sl * FS5:(sl + 1) * FS5, :, t0:t0 + tn].rearrange("f p n -> p f n"), at[:, :, 0:tn],
                              r=[at])

            with Phase(k, "p5b_%d" % l) as ph:
                wdh = [ph.sb([128, FC, 512], BF16, "wd") for _ in range(2)]
                wdq = [[T(h.h) for _ in range(4)] for h in wdh]
                ach = [ph.sb([128, FC, 512], BF16, "ac") for _ in range(2)]
                acq = [[T(h.h) for _ in range(4)] for h in ach]
                fring = ph.sbring(3, [128, 512], F32, "f")
                pf = ph.psring(4, [128, 512], F32, "pf")
                wdv = w_down[l].rearrange("(fc p) n -> p fc n", p=128)
                aci = 0
                for cs in range(4):
                    wd = wdh[cs % 2]
                    for q4 in range(4):
                        k.dma(k.pool, wd[:, q4 * 11:(q4 + 1) * 11, :], wdv[:, q4 * 11:(q4 + 1) * 11, cs * 512:(cs + 1) * 512],
                              w=[wdq[cs % 2][q4]])
                    for (t0, tn) in tchunks:
                        ac = ach[aci % 2]
                        aq = acq[aci % 2]
                        aci += 1
                        for q4 in range(4):
                            k.dma(k.sp, ac[:, q4 * 11:(q4 + 1) * 11, 0:tn],
                                  aT[q4 * 11:(q4 + 1) * 11, :, t0:t0 + tn].rearrange("f p n -> p f n"), w=[aq[q4]])
                        for ti in range(tn // 128):
                            p = pf.next()
                            for fc in range(FC):
                                k.mm(p[:, :], ac[:, fc, ti * 128:(ti + 1) * 128], wd[:, fc, :], fc == 0, fc == FC - 1,
                                     r=[aq[fc // 11], wdq[cs % 2][fc // 11]], w=[p])
                            f = fring.next()
                            k.actf(f[:, :], p[:, :], AF.Copy, r=[p], w=[f])
                            r0 = t0 + ti * 128
                            k.dma(k.act, fs[r0:r0 + 128, cs * 512:(cs + 1) * 512], f[:, :], r=[f])

            with Phase(k, "p5c_%d" % l) as ph:
                G2 = ph.sb([128, D], F32)
                LG = ph.sb([128, D], F32)
                LB = ph.sb([128, D], F32)
                k.dma(k.sp, LG[:], ln2_g[l:l + 1, :].partition_broadcast(128), w=[LG])
                k.dma(k.sp, LB[:], ln2_b[l:l + 1, :].partition_broadcast(128), w=[LB])
                x1ring = ph.sbring(2, [128, D], F32, "x1")
                fring = ph.sbring(2, [128, D], F32, "f")
                xnring = ph.sbring(2, [128, D], F32, "xn")
                oring = ph.sbring(2, [128, D], F32, "o")
                string = ph.sbring(2, [128, 32], F32, "st")
                cur_mod = None
                for t in range(NTa):
                    mr = 1 if t >= NTL else 0
                    if cur_mod != mr:
                        k.dma(k.sp, G2[:], modrow(5, mr), w=[G2])
                        cur_mod = mr
                    tsl = slice(t * 128, (t + 1) * 128)
                    x1 = x1ring.next()
                    f = fring.next()
                    k.dma(k.sp, x1[:], x1s[tsl, :], w=[x1])
                    k.dma(k.sp, f[:], fs[tsl, :], w=[f])
                    k.tt(k.dve, f[:, :], f[:, :], G2[:, :], ALU.mult, r=[f, G2], w=[f])
                    k.stt(k.dve, f[:, :], x1[:, :], ALPHA, f[:, :], ALU.mult, ALU.add, r=[x1, f], w=[f])
                    xn = xnring.next()
                    layer_norm_rows(k, ph, f, xn, string)
                    k.tt(k.dve, xn[:, :], xn[:, :], LG[:, :], ALU.mult, r=[xn, LG], w=[xn])
                    o = oring.next()
                    k.tt(k.pool, o[:, :], xn[:, :], LB[:, :], ALU.add, r=[xn, LB], w=[o])
                    dst = y_out[tsl, :] if last else xres[tsl, :]
                    k.dma(k.pool, dst, o[:, :], r=[o])
    return nc


_CACHE = {}


def _consts(S):
    if S in _CACHE:
        return _CACHE[S]
    bf = ml_dtypes.bfloat16
    t = np.arange(S)
    row = (t // 64).astype(np.float32)
    col = (t % 64).astype(np.float32)
    inv = (10000.0 ** (-np.arange(32, dtype=np.float32) / 32)).astype(np.float32)
    ang = np.concatenate([row[:, None] * inv[None, :], col[:, None] * inv[None, :]], axis=1).astype(np.float32)
    ropec = np.cos(ang).astype(np.float32)
    ropes = np.sin(ang).astype(np.float32)

    def dft(N, scale):
        i = np.arange(N, dtype=np.int64)
        m = (i[:, None] * i[None, :]) % N
        a = (2.0 * np.pi / N) * m.astype(np.float64)
        return (np.cos(a) * scale), (np.sin(a) * scale)

    c, s = dft(S, 1.0 / np.sqrt(S))
    dftc, dfts = c.astype(np.float32).astype(bf), s.astype(np.float32).astype(bf)
    c, s = dft(L, 1.0 / np.sqrt(L))
    dftc_c, dfts_c = c.astype(np.float32).astype(bf), s.astype(np.float32).astype(bf)
    c, s = dft(128, 1.0 / np.sqrt(128.0))
    c128 = c.astype(np.float32)
    ns128 = (-s).astype(np.float32)
    maskb = np.full((128, 6, 512), NEG, np.float32)
    sj = np.arange(128)[:, None]
    qi = np.arange(128)[None, :]
    for r in range(-1, 5):
        for cblk in range(4):
            d = r - cblk
            if d == -1:
                m = np.where(qi <= sj, 0.0, NEG)
            elif d == 0:
                m = np.zeros((128, 128))
            elif d == 1:
                m = np.where(sj <= qi, 0.0, NEG)
            else:
                continue
            maskb[:, r + 1, cblk * 128:(cblk + 1) * 128] = m
    out = dict(ropec=ropec, ropes=ropes, dftc=dftc, dfts=dfts, dftc_c=dftc_c, dfts_c=dfts_c, c128=c128, ns128=ns128,
               maskb=maskb.astype(bf), identb=np.eye(128, dtype=np.float32).astype(bf),
               identf=np.eye(128, dtype=np.float32))
    _CACHE[S] = out
    return out


_NC = {}


def kernel(x, c, ctx, c_ctx, w_mod, b_mod, w_in, q_gain_a, k_gain_a, sink_b, w_fourier, w_out, ln1_g, ln1_b,
           w_up, w_gate, conv_w, conv_b, w_down, ln2_g, ln2_b, _dbg=False):
    f = lambda a: np.ascontiguousarray(np.asarray(a, dtype=np.float32))
    x = f(x)
    B, S, _ = x.shape
    depth = w_mod.shape[0]
    key = (S, depth, _dbg)
    if key not in _NC:
        _NC[key] = build(S, depth, dbg=_dbg)
    nc = _NC[key]
    cs = _consts(S)
    c = f(c)
    ctx = f(ctx)
    c_ctx = f(c_ctx)
    conv_w = f(conv_w)
    conv_b = f(conv_b)
    cp = np.concatenate([conv_w, conv_b[:, None, :]], axis=1)
    convp = np.ascontiguousarray(cp.reshape(depth, 4, FC, 128).transpose(0, 3, 1, 2))
    shared = dict(w_mod=f(w_mod), b_mod=f(b_mod), w_in=f(w_in), q_gain_a=f(q_gain_a), k_gain_a=f(k_gain_a),
                  sink_b=f(sink_b), w_fourier=f(w_fourier), w_out=f(w_out), ln1_g=f(ln1_g), ln1_b=f(ln1_b),
                  w_up=f(w_up), w_gate=f(w_gate), convp=convp, w_down=f(w_down), ln2_g=f(ln2_g), ln2_b=f(ln2_b))
    shared.update(cs)
    in_maps = []
    for b in range(B):
        cc = np.stack([c[b], c_ctx], axis=0)
        ccT = np.ascontiguousarray(cc.reshape(2, KC, 128).transpose(2, 1, 0))
        m = dict(shared)
        m.update(x=x[b], ctx=ctx[b], ccT=ccT)
        in_maps.append(m)
    res = run_bass_kernel_spmd(nc, in_maps, core_ids=list(range(B)))
    if _dbg:
        return res
    return np.stack([np.asarray(r["y"], dtype=np.float32) for r in res.results], axis=0)
```

```python
from contextlib import ExitStack
import numpy as np
import ml_dtypes
import concourse.bass as bass
import concourse.mybir as mybir
from concourse.bass_utils import run_bass_kernel_spmd

F32 = mybir.dt.float32
BF16 = mybir.dt.bfloat16
AF = mybir.ActivationFunctionType
ALU = mybir.AluOpType
AX = mybir.AxisListType

D = 2048
KC = 16
L = 256
HD = 128
FF = 5632
FC = 44
INW = 3072
EPS = 1e-6
ALPHA = (2.0 * 4) ** 0.25
QSCALE = 128.0 ** -0.5
NEG = -30000.0
NSL = 4
FCS = FC // NSL


class Buf:
    __slots__ = ("w", "r")

    def __init__(self):
        self.w = {}
        self.r = {}


class Sem:
    __slots__ = ("h", "cnt", "step", "sid")

    def __init__(self, h, step, sid):
        self.h = h
        self.cnt = 0
        self.step = step
        self.sid = sid


class Seq:
    def __init__(self, eng, name, inorder=False):
        self.eng = eng
        self.name = name
        self.seen = {}
        self.csem = None
        self.dsems = []
        self.rr = 0
        self.inorder = inorder


class T:
    def __init__(self, h, b=None):
        self.h = h
        self.b = b if b is not None else Buf()

    def __getitem__(self, key):
        return self.h[key]


class Ring:
    def __init__(self, tiles):
        self.tiles = tiles
        self.i = 0

    def next(self):
        t = self.tiles[self.i % len(self.tiles)]
        self.i += 1
        return t


class KB:
    def __init__(self, nc):
        self.nc = nc
        self.stack = ExitStack()
        self.sems = []
        self.pe = Seq(nc.tensor, "pe", inorder=True)
        self.act = Seq(nc.scalar, "act")
        self.dve = Seq(nc.vector, "dve")
        self.pool = Seq(nc.gpsimd, "pool")
        self.sp = Seq(nc.sync, "sp")
        self.seqs = [self.pe, self.act, self.dve, self.pool, self.sp]
        for s in (self.pe, self.act, self.dve, self.pool):
            s.csem = self._sem("c_" + s.name, 1)
        for s, n in ((self.sp, 10), (self.pool, 8), (self.act, 4)):
            s.dsems = [self._sem("d_%s%d" % (s.name, i), 16) for i in range(n)]

    def _sem(self, name, step):
        h = self.stack.enter_context(self.nc.semaphore(name))
        s = Sem(h, step, len(self.sems))
        self.sems.append(s)
        return s

    def emit(self, seq, fn, r=(), w=(), dma=False, sig=True):
        need = {}

        def nd(d):
            for sid, v in d.items():
                if v > need.get(sid, 0):
                    need[sid] = v

        for b in r:
            nd(b.b.w if isinstance(b, T) else b.w)
        for b in w:
            bb = b.b if isinstance(b, T) else b
            nd(bb.r)
            nd(bb.w)
        if dma:
            sem = seq.dsems[seq.rr % len(seq.dsems)]
            seq.rr += 1
            if sem.cnt > 0:
                if sem.cnt > need.get(sem.sid, 0):
                    need[sem.sid] = sem.cnt
        else:
            sem = seq.csem
        for sid, v in need.items():
            if seq.inorder and not dma and sid == seq.csem.sid:
                continue
            if seq.seen.get(sid, 0) >= v:
                continue
            s = self.sems[sid]
            assert v <= s.cnt, "waiting on a pending ticket (%s sid=%d v=%d cnt=%d)" % (seq.name, sid, v, s.cnt)
            seq.eng.wait_ge(s.h, v)
            seq.seen[sid] = v
        ins = fn()
        if sig:
            sem.cnt += sem.step
            ins.then_inc(sem.h, sem.step)
            tk = sem.cnt
        else:
            tk = sem.cnt + sem.step
        for b in r:
            bb = b.b if isinstance(b, T) else b
            if tk > bb.r.get(sem.sid, 0):
                bb.r[sem.sid] = tk
        for b in w:
            bb = b.b if isinstance(b, T) else b
            bb.w = {sem.sid: tk}
            bb.r = {}
        return ins

    def barrier(self):
        for seq in self.seqs:
            for s in self.sems:
                if s.cnt > seq.seen.get(s.sid, 0):
                    seq.eng.wait_ge(s.h, s.cnt)
                    seq.seen[s.sid] = s.cnt

    def dma(self, seq, out, in_, r=(), w=(), **kw):
        return self.emit(seq, lambda: seq.eng.dma_start(out=out, in_=in_, **kw), r=r, w=w, dma=True)

    def mm(self, out, lhsT, rhs, start, stop, r=(), w=(), sig=None):
        if sig is None:
            sig = stop
        return self.emit(self.pe, lambda: self.nc.tensor.matmul(out, lhsT, rhs, start=start, stop=stop),
                         r=r, w=w, sig=sig)

    def tr(self, out, in_, ident, r=(), w=(), sig=True):
        return self.emit(self.pe, lambda: self.nc.tensor.transpose(out=out, in_=in_, identity=ident),
                         r=r, w=w, sig=sig)

    def actf(self, out, in_, func, r=(), w=(), **kw):
        return self.emit(self.act, lambda: self.nc.scalar.activation(out=out, in_=in_, func=func, **kw), r=r, w=w)

    def tt(self, seq, out, in0, in1, op, r=(), w=()):
        return self.emit(seq, lambda: seq.eng.tensor_tensor(out=out, in0=in0, in1=in1, op=op), r=r, w=w)

    def ts(self, seq, out, in0, s1, s2, op0, op1=None, r=(), w=()):
        if op1 is None:
            return self.emit(seq, lambda: seq.eng.tensor_scalar(out=out, in0=in0, scalar1=s1, scalar2=None, op0=op0),
                             r=r, w=w)
        return self.emit(seq, lambda: seq.eng.tensor_scalar(out=out, in0=in0, scalar1=s1, scalar2=s2, op0=op0, op1=op1),
                         r=r, w=w)

    def stt(self, seq, out, in0, scalar, in1, op0, op1, r=(), w=()):
        return self.emit(seq, lambda: seq.eng.scalar_tensor_tensor(out=out, in0=in0, scalar=scalar, in1=in1,
                                                                   op0=op0, op1=op1), r=r, w=w)


class Phase:
    def __init__(self, k, name):
        self.k = k
        self.nc = k.nc
        self.name = name
        self.es = ExitStack()
        self.n = 0

    def __enter__(self):
        self.es.__enter__()
        return self

    def __exit__(self, *a):
        self.k.barrier()
        return self.es.__exit__(*a)

    def sb(self, shape, dt, nm="t"):
        self.n += 1
        h = self.es.enter_context(self.nc.sbuf_tensor("%s_%s%d" % (self.name, nm, self.n), list(shape), dt))
        return T(h)

    def ps(self, shape, dt, nm="p"):
        self.n += 1
        h = self.es.enter_context(self.nc.psum_tensor("%s_%s%d" % (self.name, nm, self.n), list(shape), dt))
        return T(h)

    def const(self, val):
        if not hasattr(self, "_consts"):
            self._consts = {}
        if val not in self._consts:
            t = self.sb([128, 1], F32, "c")
            self.k.emit(self.k.dve, lambda: self.nc.vector.memset(t[:], val), w=[t])
            self._consts[val] = t
        return self._consts[val]

    def sbring(self, n, shape, dt, nm="r"):
        return Ring([self.sb(shape, dt, nm) for _ in range(n)])

    def psring(self, n, shape, dt, nm="pr"):
        return Ring([self.ps(shape, dt, nm) for _ in range(n)])


def layer_norm_rows(k, ph, x, xn, stat_ring, eps=EPS):
    nc = k.nc
    st = stat_ring.next()
    for j in range(4):
        k.emit(k.dve, lambda: nc.vector.bn_stats(out=st[:, j * 6:(j + 1) * 6], in_=x[:, j * 512:(j + 1) * 512]),
               r=[x], w=[st])
    k.emit(k.dve, lambda: nc.vector.bn_aggr(out=st[:, 24:26], in_=st[:, 0:24]), r=[st], w=[st])
    epsT = ph.const(eps)
    k.actf(st[:, 26:27], st[:, 25:26], AF.Sqrt, r=[st, epsT], w=[st], bias=epsT[:, 0:1], scale=1.0)
    k.emit(k.dve, lambda: nc.vector.reciprocal(out=st[:, 26:27], in_=st[:, 26:27]), r=[st], w=[st])
    k.stt(k.dve, st[:, 27:28], st[:, 24:25], -1.0, st[:, 26:27], ALU.mult, ALU.mult, r=[st], w=[st])
    k.actf(xn[:, :], x[:, :], AF.Identity, r=[x, st], w=[xn], bias=st[:, 27:28], scale=st[:, 26:27])


def layer_norm_gen(k, ph, x, xn, stat_ring, eps):
    nc = k.nc
    st = stat_ring.next()
    for j in range(4):
        k.emit(k.dve, lambda: nc.vector.bn_stats(out=st[:, j * 6:(j + 1) * 6], in_=x[:, j * 512:(j + 1) * 512]),
               r=[x], w=[st])
        yield None
    k.emit(k.dve, lambda: nc.vector.bn_aggr(out=st[:, 24:26], in_=st[:, 0:24]), r=[st], w=[st])
    yield None
    epsT = ph.const(eps)
    k.actf(st[:, 26:27], st[:, 25:26], AF.Sqrt, r=[st, epsT], w=[st], bias=epsT[:, 0:1], scale=1.0)
    yield None
    k.emit(k.dve, lambda: nc.vector.reciprocal(out=st[:, 26:27], in_=st[:, 26:27]), r=[st], w=[st])
    yield None
    k.stt(k.dve, st[:, 27:28], st[:, 24:25], -1.0, st[:, 26:27], ALU.mult, ALU.mult, r=[st], w=[st])
    yield None
    k.actf(xn[:, :], x[:, :], AF.Identity, r=[x, st], w=[xn], bias=st[:, 27:28], scale=st[:, 26:27])
    yield None


def build(S, depth, dbg=False):
    assert S % 512 == 0
    NTL = S // 128
    NT = NTL + 2
    NTOK = S + L
    NQC = S // 512
    nc = bass.Bass("TRN2", target_bir_lowering=False)

    def din(name, shape, dt=F32):
        return nc.dram_tensor(name, list(shape), dt, kind="ExternalInput").ap()

    def dscr(name, shape, dt, out=False):
        return nc.dram_tensor(name, list(shape), dt, kind="ExternalOutput" if (out or dbg) else "Internal").ap()

    x_in = din("x", [S, D])
    ctx_in = din("ctx", [L, D])
    ccT = din("ccT", [128, KC, 2])
    w_mod = din("w_mod", [depth, D, 6 * D])
    b_mod = din("b_mod", [depth, 6 * D])
    w_in = din("w_in", [depth, D, INW])
    q_gain = din("q_gain_a", [depth, HD])
    k_gain = din("k_gain_a", [depth, HD])
    sink_b = din("sink_b", [depth, 4])
    w_f = din("w_fourier", [depth, 4, 128, 128])
    w_out = din("w_out", [depth, D, D])
    ln1_g = din("ln1_g", [depth, D])
    ln1_b = din("ln1_b", [depth, D])
    w_up = din("w_up", [depth, D, FF])
    w_gate = din("w_gate", [depth, D, FF])
    convp = din("convp", [depth, 128, 4, FC])
    w_down = din("w_down", [depth, FF, D])
    ln2_g = din("ln2_g", [depth, D])
    ln2_b = din("ln2_b", [depth, D])
    ropec = din("ropec", [S, 64])
    ropes = din("ropes", [S, 64])
    dftc = din("dftc", [S, S], BF16)
    dfts = din("dfts", [S, S], BF16)
    dftc_c = din("dftc_c", [L, L], BF16)
    dfts_c = din("dfts_c", [L, L], BF16)
    c128 = din("c128", [128, 128])
    ns128 = din("ns128", [128, 128])
    maskb_in = din("maskb", [128, 6, 512], BF16)
    identb_in = din("identb", [128, 128], BF16)
    identf_in = din("identf", [128, 128])

    y_out = nc.dram_tensor("y", [S, D], F32, kind="ExternalOutput").ap()

    xres = dscr("xres", [NTOK, D], F32)
    x1s = dscr("x1s", [NTOK, D], F32)
    fs = dscr("fs", [NTOK, D], F32)
    modv = dscr("modv", [depth, 2, 6 * D], F32)
    modT = dscr("modT", [depth, 128, 96, 2], F32)
    qT = dscr("qT", [12, 128, NTOK], BF16)
    kT = dscr("kT", [4, 128, NTOK], BF16)
    vS = dscr("vS", [NTOK, 512], BF16)
    uT = dscr("uT", [4, 128, NTOK], BF16)
    oT = dscr("oT", [KC, 128, NTOK], BF16)
    h2T = dscr("h2T", [KC, 128, NTOK], BF16)
    aT = dscr("aT", [FC, 128, NTOK], BF16)

    k = KB(nc)
    with k.stack:
        with Phase(k, "pro") as ph:
            k.dma(k.sp, xres[0:S, :], x_in[:, :])
            k.dma(k.sp, xres[S:NTOK, :], ctx_in[:, :])

        for l in range(depth):
            last = (l == depth - 1)
            NTa = NTL if last else NT
            with Phase(k, "p0_%d" % l) as ph:
                sc = ph.sb([128, KC, 2], F32)
                k.dma(k.sp, sc[:], ccT[:, :, :], w=[sc])
                scs = ph.sb([128, KC, 2], F32)
                k.actf(scs[:], sc[:], AF.Silu, r=[sc], w=[scs])
                bm = ph.sb([2, 6 * D], F32)
                k.dma(k.sp, bm[:], b_mod[l:l + 1, :].partition_broadcast(2), w=[bm])
                mo = ph.sb([2, 6 * D], F32)
                wring = ph.sbring(12, [128, 4, 512], F32, "wm")
                pring = ph.psring(2, [2, 512], F32)
                wv = w_mod[l].rearrange("(kc p) n -> p kc n", p=128)
                for cb in range(24):
                    wq = []
                    for q4 in range(4):
                        wt = wring.next()
                        k.dma(k.sp, wt[:, :, :], wv[:, q4 * 4:(q4 + 1) * 4, cb * 512:(cb + 1) * 512], w=[wt])
                        wq.append(wt)
                    p = pring.next()
                    for kc in range(KC):
                        wt = wq[kc // 4]
                        k.mm(p[:, :], scs[:, kc, :], wt[:, kc % 4, :], kc == 0, kc == KC - 1, r=[scs, wt], w=[p])
                    k.tt(k.dve, mo[:, cb * 512:(cb + 1) * 512], p[:, :], bm[:, cb * 512:(cb + 1) * 512], ALU.add,
                         r=[p, bm], w=[mo])
                for a in (1, 4):
                    k.ts(k.dve, mo[:, a * D:(a + 1) * D], mo[:, a * D:(a + 1) * D], 1.0, None, ALU.add, r=[mo], w=[mo])
                k.dma(k.sp, modv[l], mo[:], r=[mo])
                idf = ph.sb([2, 2], F32)
                k.dma(k.sp, idf[:], identf_in[0:2, 0:2], w=[idf])
                pT = ph.ps([128, 96 * 2], F32, "pT")
                for j in range(96):
                    k.tr(pT[:, 2 * j:2 * j + 2], mo[:, j * 128:(j + 1) * 128], idf[:], r=[mo, idf], w=[pT], sig=(j == 95))
                moT = ph.sb([128, 96 * 2], F32)
                k.actf(moT[:], pT[:], AF.Copy, r=[pT], w=[moT])
                k.dma(k.sp, modT[l].rearrange("p a b -> p (a b)"), moT[:], r=[moT])

            def modrow(idx, r):
                return modv[l, r:r + 1, idx * D:(idx + 1) * D].partition_broadcast(128)

            with Phase(k, "p1_%d" % l) as ph:
                wi = ph.sb([128, KC, INW], BF16, "wi")
                wiv = w_in[l].rearrange("(kc p) n -> p kc n", p=128)
                wiq = [{dc: T(wi.h) for dc in (0, 1536, 1792, 2048, 2304)} for _ in range(KC)]
                colmap = ((0, 0, 1536), (1536, 1536, 256), (1792, 2048, 256), (2048, 1792, 256), (2304, 2304, 768))
                for kc in range(KC):
                    for (dc, sc_, n_) in colmap:
                        hh = 0 if dc < 1536 else 1
                        k.dma(k.pool, wi[:, kc, dc:dc + n_], wiv[:, kc, sc_:sc_ + n_], w=[wiq[kc][dc]])
                identb = ph.sb([128, 128], BF16)
                k.dma(k.sp, identb[:], identb_in[:, :], w=[identb])
                mT = ph.sb([128, 96 * 2], F32, "mT")
                k.dma(k.sp, mT[:], modT[l].rearrange("p a b -> p (a b)"), w=[mT])
                qg = ph.sb([128, HD], F32)
                kg = ph.sb([128, HD], F32)
                k.dma(k.sp, qg[:], q_gain[l:l + 1, :].partition_broadcast(128), w=[qg])
                k.dma(k.sp, kg[:], k_gain[l:l + 1, :].partition_broadcast(128), w=[kg])
                k.ts(k.dve, qg[:], qg[:], QSCALE, None, ALU.mult, r=[qg], w=[qg])
                rcring = ph.sbring(2, [128, 64], F32, "rc")
                rsring = ph.sbring(2, [128, 64], F32, "rs")
                xring = ph.sbring(2, [128, D], F32, "x")
                string = ph.sbring(2, [128, 32], F32, "st")
                hbring = ph.sbring(2, [128, D], BF16, "hb")
                hTring = ph.sbring(2, [128, KC, 128], BF16, "hT")
                tp = [ph.ps([128, 1024], BF16, "tp") for _ in range(2)]
                pb = [ph.ps([128, 512], F32, "pb") for _ in range(6)]
                psb = ph.sb([128, INW], F32, "psb")
                psbq = T(psb.h)
                psbv = T(psb.h)
                sqring = ph.sbring(2, [128, 512], F32, "sq")
                ssring = ph.sbring(2, [128, 16], F32, "ss")
                qkrring = ph.sbring(2, [128, D], BF16, "qkr")
                tmpA = ph.sbring(1, [128, 16, 2, 32], F32, "ta")
                tmpB = ph.sbring(1, [128, 16, 2, 32], F32, "tb")
                vtring = ph.sbring(2, [128, 512], BF16, "vt")
                utring = ph.sbring(2, [128, 512], BF16, "ut")
                qTring = ph.sbring(2, [128, 16, 128], BF16, "qT")
                uTring = ph.sbring(2, [128, 4, 128], BF16, "uT")
                state = {"mod": None}

                def stageA1(t):
                    x = xring.next()
                    k.dma(k.sp, x[:], xres[t * 128:(t + 1) * 128, :], w=[x])
                    hb = hbring.next()
                    layer_norm_rows(k, ph, x, hb, string)
                    return hb

                def stageA2(t, hb):
                    isctx = t >= NTL
                    mr = 1 if isctx else 0
                    hT = hTring.next()
                    for half in range(2):
                        for j in range(8):
                            kc = half * 8 + j
                            k.tr(tp[half][:, j * 128:(j + 1) * 128], hb[:, kc * 128:(kc + 1) * 128], identb[:],
                                 r=[hb, identb], w=[tp[half]], sig=(j == 7))
                        for j in range(8):
                            kc = half * 8 + j
                            k.actf(hT[:, kc, :], tp[half][:, j * 128:(j + 1) * 128], AF.Identity, r=[tp[half], mT], w=[hT],
                                   scale=mT[:, (16 + kc) * 2 + mr:(16 + kc) * 2 + mr + 1],
                                   bias=mT[:, kc * 2 + mr:kc * 2 + mr + 1])
                    return hT

                def stageB(t, hT, half):
                    if True:
                        for kc in range(KC):
                            for cb in range(half * 3, half * 3 + 3):
                                k.mm(pb[cb][:, :], hT[:, kc, :], wi[:, kc, cb * 512:(cb + 1) * 512], kc == 0, kc == KC - 1,
                                     r=[hT] + ([wiq[kc][0]] if half == 0 else [wiq[kc][dc] for dc in (1536, 1792, 2048, 2304)]), w=[pb[cb]])

                def stageC0(t, lo, hi):
                    for cb in range(lo, hi):
                        dstb = psbq if cb < 4 else psbv
                        if cb % 2 == 0:
                            k.actf(psb[:, cb * 512:(cb + 1) * 512], pb[cb][:, :], AF.Copy, r=[pb[cb]], w=[dstb])
                        else:
                            k.emit(k.dve, lambda: nc.vector.tensor_copy(out=psb[:, cb * 512:(cb + 1) * 512], in_=pb[cb][:, :]),
                                   r=[pb[cb]], w=[dstb])

                def stageC(t):
                    isctx = t >= NTL
                    ss = ssring.next()
                    for (c0, nh, ofs) in ((0, 4, 0), (512, 4, 4), (1536, 2, 8)):
                        sq = sqring.next()
                        k.actf(sq[:, 0:nh * 128], psb[:, c0:c0 + nh * 128], AF.Square, r=[psbq], w=[sq])
                        k.emit(k.dve, lambda: nc.vector.tensor_reduce(
                            out=ss[:, ofs:ofs + nh], in_=sq[:, 0:nh * 128].rearrange("p (h d) -> p h d", d=128),
                            axis=AX.X, op=ALU.add), r=[sq], w=[ss])
                    epsT = ph.const(EPS)
                    k.actf(ss[:, 0:10], ss[:, 0:10], AF.Sqrt, r=[ss, epsT], w=[ss], bias=epsT[:, 0:1], scale=1.0 / 128.0)
                    k.emit(k.dve, lambda: nc.vector.reciprocal(out=ss[:, 0:10], in_=ss[:, 0:10]), r=[ss], w=[ss])
                    for h in range(8):
                        k.stt(k.dve, psb[:, h * 128:(h + 1) * 128], psb[:, h * 128:(h + 1) * 128],
                              ss[:, h:h + 1], qg[:, :], ALU.mult, ALU.mult, r=[psbq, ss, qg], w=[psbq])
                    k.actf(psb[:, 1024:1536], psb[:, 1024:1536], AF.Copy, r=[psbq], w=[psbq], scale=QSCALE)
                    for h in range(2):
                        c0 = 1536 + h * 128
                        k.stt(k.dve, psb[:, c0:c0 + 128], psb[:, c0:c0 + 128],
                              ss[:, 8 + h:9 + h], kg[:, :], ALU.mult, ALU.mult, r=[psbq, ss, kg], w=[psbq])
                    vt = vtring.next()
                    k.actf(vt[:, :], psb[:, 2048:2560], AF.Copy, r=[psbv], w=[vt])
                    ut = utring.next()
                    k.actf(ut[:, :], psb[:, 2560:3072], AF.Copy, r=[psbv], w=[ut])
                    qkr = qkrring.next()
                    if isctx:
                        k.actf(qkr[:, :], psb[:, 0:2048], AF.Copy, r=[psbq], w=[qkr])
                    else:
                        q5 = psb[:, 0:2048].rearrange("p (h x y f) -> p h x y f", h=16, x=2, y=2)
                        o5 = qkr[:, :].rearrange("p (h x y f) -> p h x y f", h=16, x=2, y=2)
                        a_ = q5[:, :, :, 0, :]
                        b_ = q5[:, :, :, 1, :]
                        rc = rcring.next()
                        rs_ = rsring.next()
                        k.dma(k.sp, rc[:], ropec[t * 128:(t + 1) * 128, :], w=[rc])
                        k.dma(k.sp, rs_[:], ropes[t * 128:(t + 1) * 128, :], w=[rs_])
                        cc = rc[:, :].rearrange("p (x f) -> p x f", x=2).unsqueeze(1).to_broadcast([128, 16, 2, 32])
                        sn = rs_[:, :].rearrange("p (x f) -> p x f", x=2).unsqueeze(1).to_broadcast([128, 16, 2, 32])
                        ta = tmpA.next()
                        tb = tmpB.next()
                        k.tt(k.dve, ta[:], a_, cc, ALU.mult, r=[psbq, rc], w=[ta])
                        k.tt(k.pool, tb[:], b_, sn, ALU.mult, r=[psbq, rs_], w=[tb])
                        k.tt(k.dve, o5[:, :, :, 0, :], ta[:], tb[:], ALU.subtract, r=[ta, tb], w=[qkr])
                        ta = tmpA.next()
                        tb = tmpB.next()
                        k.tt(k.pool, ta[:], a_, sn, ALU.mult, r=[psbq, rs_], w=[ta])
                        k.tt(k.dve, tb[:], b_, cc, ALU.mult, r=[psbq, rc], w=[tb])
                        k.tt(k.pool, o5[:, :, :, 1, :], ta[:], tb[:], ALU.add, r=[ta, tb], w=[qkr])
                    return (qkr, ut, vt)

                def stageD(t, qkr, ut, vt):
                    qTt = qTring.next()
                    for half in range(2):
                        for j in range(8):
                            c = half * 8 + j
                            k.tr(tp[half][:, j * 128:(j + 1) * 128], qkr[:, c * 128:(c + 1) * 128], identb[:],
                                 r=[qkr, identb], w=[tp[half]], sig=(j == 7))
                        k.emit(k.dve, lambda: nc.vector.tensor_copy(
                            out=qTt[:, half * 8:(half + 1) * 8, :],
                            in_=tp[half][:, :].rearrange("p (a b) -> p a b", b=128)), r=[tp[half]], w=[qTt])
                    uTt = uTring.next()
                    for j in range(4):
                        k.tr(tp[0][:, j * 128:(j + 1) * 128], ut[:, j * 128:(j + 1) * 128], identb[:],
                             r=[ut, identb], w=[tp[0]], sig=(j == 3))
                    k.emit(k.dve, lambda: nc.vector.tensor_copy(
                        out=uTt[:, :, :], in_=tp[0][:, 0:512].rearrange("p (a b) -> p a b", b=128)), r=[tp[0]], w=[uTt])
                    tsl = slice(t * 128, (t + 1) * 128)
                    k.dma(k.pool, qT[:, :, tsl].rearrange("h p n -> p h n"), qTt[:, 0:12, :], r=[qTt])
                    k.dma(k.pool, kT[:, :, tsl].rearrange("h p n -> p h n"), qTt[:, 12:16, :], r=[qTt])
                    k.dma(k.pool, uT[:, :, tsl].rearrange("h p n -> p h n"), uTt[:, :, :], r=[uTt])
                    k.dma(k.pool, vS[tsl, :], vt[:, :], r=[vt])

                hbs = {0: stageA1(0)}
                if NT > 1:
                    hbs[1] = stageA1(1)
                hT_cur = stageA2(0, hbs.pop(0))
                prevC = None
                for t in range(NT):
                    stageB(t, hT_cur, 0)
                    hT_nxt = stageA2(t + 1, hbs.pop(t + 1)) if t + 1 < NT else None
                    stageC0(t, 0, 3)
                    stageB(t, hT_cur, 1)
                    if prevC is not None:
                        stageD(t - 1, *prevC)
                    stageC0(t, 3, 6)
                    if t + 2 < NT:
                        hbs[t + 2] = stageA1(t + 2)
                    prevC = stageC(t)
                    hT_cur = hT_nxt
                stageD(NT - 1, *prevC)

            with Phase(k, "p2_%d" % l) as ph:
                kTs = ph.sb([128, 4, NTOK], BF16, "kTs")
                kTb = [T(kTs.h) for _ in range(4)]
                for h in range(4):
                    k.dma(k.sp, kTs[:, h, :], kT[h], w=[kTb[h]])
                vs = ph.sb([128, NT, 512], BF16, "vs")
                k.dma(k.sp, vs[:], vS.rearrange("(t p) c -> p t c", p=128), w=[vs])
                ones = ph.sb([128, 128], BF16)
                k.emit(k.dve, lambda: nc.vector.memset(ones[:], 1.0), w=[ones])
                identb = ph.sb([128, 128], BF16)
                k.dma(k.sp, identb[:], identb_in[:, :], w=[identb])
                maskb = ph.sb([128, 6, 512], BF16)
                k.dma(k.sp, maskb[:], maskb_in[:, :, :], w=[maskb])
                es = ph.sb([128, 4], F32)
                k.dma(k.sp, es[:], sink_b[l:l + 1, :].partition_broadcast(128), w=[es])
                k.actf(es[:], es[:], AF.Exp, r=[es], w=[es])
                qring = ph.sbring(2, [128, NTOK], BF16, "q")
                pring = ph.sbring(4, [128, 512], BF16, "pt")
                oring = ph.sbring(2, [128, 512], BF16, "ot")
                rdring = ph.sbring(2, [128, 512], F32, "rd")
                ps_s = ph.psring(3, [128, 512], F32, "s")
                ps_o = ph.psring(2, [128, 512], F32, "o")
                ps_d = ph.psring(2, [128, 512], F32, "d")
                for hq in range(12):
                    isB = hq >= 8
                    kv = (hq // 4) if not isB else (2 + (hq - 8) // 2)
                    q = qring.next()
                    k.dma(k.sp, q[:], qT[hq], w=[q])
                    chunks = [(qc * 512, 512, False) for qc in range(NQC)] + ([] if last else [(S, 256, True)])
                    for (q0, qn, isctx) in chunks:
                        if isctx:
                            tiles = [(NTL, None), (NTL + 1, None)]
                        elif not isB:
                            tiles = [(t, None) for t in range(NT)]
                        else:
                            n0 = q0 // 128
                            tiles = [(n0 + r, r + 1) for r in range(-1, 5) if 0 <= n0 + r < NTL]
                            tiles += [(NTL, None), (NTL + 1, None)]
                        po = ps_o.next()
                        pd = ps_d.next()
                        n = len(tiles)

                        def qk(i):
                            st, mi = tiles[i]
                            p = ps_s.next()
                            k.mm(p[:, 0:qn], kTs[:, kv, st * 128:(st + 1) * 128], q[:, q0:q0 + qn], True, mi is None,
                                 r=[kTb[kv], q], w=[p])
                            if mi is not None:
                                k.mm(p[:, 0:qn], identb[:], maskb[:, mi, 0:qn], False, True, r=[identb, maskb], w=[p])
                            return p

                        LA = 2
                        pend = [qk(i) for i in range(min(LA, n))]
                        for i in range(n):
                            p = pend.pop(0)
                            pt = pring.next()
                            k.actf(pt[:, 0:qn], p[:, 0:qn], AF.Exp, r=[p], w=[pt])
                            if i + LA < n:
                                pend.append(qk(i + LA))
                            st = tiles[i][0]
                            k.mm(po[:, 0:qn], vs[:, st, kv * 128:(kv + 1) * 128], pt[:, 0:qn], i == 0, i == n - 1,
                                 r=[vs, pt], w=[po])
                            k.mm(pd[:, 0:qn], ones[:], pt[:, 0:qn], i == 0, i == n - 1, r=[ones, pt], w=[pd])
                        rd = rdring.next()
                        if isB:
                            k.ts(k.dve, rd[:, 0:qn], pd[:, 0:qn], es[:, hq - 8:hq - 7], None, ALU.add, r=[pd, es], w=[rd])
                            k.emit(k.dve, lambda: nc.vector.reciprocal(out=rd[:, 0:qn], in_=rd[:, 0:qn]), r=[rd], w=[rd])
                        else:
                            k.emit(k.dve, lambda: nc.vector.reciprocal(out=rd[:, 0:qn], in_=pd[:, 0:qn]), r=[pd], w=[rd])
                        ot = oring.next()
                        k.tt(k.dve, ot[:, 0:qn], po[:, 0:qn], rd[:, 0:qn], ALU.mult, r=[po, rd], w=[ot])
                        k.dma(k.pool, oT[hq, :, q0:q0 + qn], ot[:, 0:qn], r=[ot])

            with Phase(k, "p3_%d" % l) as ph:
                c128t = ph.sb([128, 128], F32)
                ns128t = ph.sb([128, 128], F32)
                k.dma(k.sp, c128t[:], c128[:, :], w=[c128t])
                k.dma(k.sp, ns128t[:], ns128[:, :], w=[ns128t])
                wf = ph.sb([128, 4, 128], F32)
                k.dma(k.sp, wf[:], w_f[l].rearrange("g c e -> c g e"), w=[wf])
                AB = ph.sb([128, 4, 256], BF16)
                pw = ph.psring(2, [128, 512], F32, "pw")
                pacc = ph.psring(6, [128, 512], F32, "pa")
                for g in range(4):
                    p = pw.next()
                    k.mm(p[:, 0:128], c128t[:], wf[:, g, :], True, True, r=[c128t, wf], w=[p])
                    k.mm(p[:, 128:256], ns128t[:], wf[:, g, :], True, True, r=[ns128t, wf], w=[p])
                    k.actf(AB[:, g, :], p[:, 0:256], AF.Copy, r=[p], w=[AB])
                cring = ph.sbring(3, [128, 8, 512], BF16, "dc")
                sring = ph.sbring(3, [128, 8, 512], BF16, "ds")
                foring = ph.sbring(3, [128, 512], BF16, "fo")
                segs = [(S, 0, dftc, dfts, 512)] + ([] if last else [(L, S, dftc_c, dfts_c, 256)])
                for (N, off, Ct, St, CH) in segs:
                    MT = N // 128
                    uts = ph.sb([128, 4, N], BF16, "uts")
                    utb = [T(uts.h) for _ in range(4)]
                    for g in range(4):
                        k.dma(k.sp, uts[:, g, :], uT[g, :, off:off + N], w=[utb[g]])
                    UAB = ph.sb([128, MT, 4, 256], BF16, "uab")
                    uabb = [T(UAB.h) for _ in range(MT)]
                    for mt in range(MT):
                        for gp in range(2):
                            p = pw.next()
                            for gg in range(2):
                                g = gp * 2 + gg
                                k.mm(p[:, gg * 256:(gg + 1) * 256], uts[:, g, mt * 128:(mt + 1) * 128], AB[:, g, :], True, True,
                                     r=[utb[g], AB], w=[p])
                            if gp == 0:
                                k.actf(UAB[:, mt, 0:2, :], p[:, :].rearrange("p (a b) -> p a b", b=256), AF.Copy,
                                       r=[p], w=[uabb[mt]])
                            else:
                                k.emit(k.dve, lambda: nc.vector.tensor_copy(
                                    out=UAB[:, mt, 2:4, :], in_=p[:, :].rearrange("p (a b) -> p a b", b=256)),
                                    r=[p], w=[uabb[mt]])
                    for nci in range(N // CH):
                        banks = [pacc.next() for _ in range(4)]
                        for mg in range(0, MT, 8):
                            mcount = min(8, MT - mg)
                            ct = cring.next()
                            st_ = sring.next()
                            k.dma(k.sp, ct[:, 0:mcount, 0:CH],
                                  Ct[mg * 128:(mg + mcount) * 128, nci * CH:(nci + 1) * CH].rearrange("(m p) n -> p m n", p=128),
                                  w=[ct])
                            k.dma(k.sp, st_[:, 0:mcount, 0:CH],
                                  St[mg * 128:(mg + mcount) * 128, nci * CH:(nci + 1) * CH].rearrange("(m p) n -> p m n", p=128),
                                  w=[st_])
                            for mi in range(mcount):
                                mt = mg + mi
                                for g in range(4):
                                    k.mm(banks[g][:, 0:CH], UAB[:, mt, g, 0:128], ct[:, mi, 0:CH], mt == 0, False,
                                         r=[uabb[mt], ct], w=[banks[g]])
                                for g in range(4):
                                    k.mm(banks[g][:, 0:CH], UAB[:, mt, g, 128:256], st_[:, mi, 0:CH], False, mt == MT - 1,
                                         r=[uabb[mt], st_], w=[banks[g]],
                                         sig=(mt == MT - 1) or (mi == mcount - 1 and g == 3))
                        for g in range(4):
                            fo = foring.next()
                            k.actf(fo[:, 0:CH], banks[g][:, 0:CH], AF.Copy, r=[banks[g]], w=[fo])
                            k.dma(k.pool, oT[12 + g, :, off + nci * CH:off + (nci + 1) * CH], fo[:, 0:CH], r=[fo])

            with Phase(k, "p4_%d" % l) as ph:
                wo = ph.sb([128, KC, D], BF16, "wo")
                wob = [T(wo.h) for _ in range(KC)]
                wov = w_out[l].rearrange("(kc p) n -> p kc n", p=128)
                for kc in range(KC):
                    k.dma(k.pool, wo[:, kc, :], wov[:, kc, :], w=[wob[kc]])
                identb = ph.sb([128, 128], BF16)
                k.dma(k.sp, identb[:], identb_in[:, :], w=[identb])
                G1 = ph.sb([128, D], F32)
                LG = ph.sb([128, D], F32)
                LB = ph.sb([128, D], F32)
                mT = ph.sb([128, 96 * 2], F32, "mT")
                k.dma(k.sp, mT[:], modT[l].rearrange("p a b -> p (a b)"), w=[mT])
                k.dma(k.sp, LG[:], ln1_g[l:l + 1, :].partition_broadcast(128), w=[LG])
                k.dma(k.sp, LB[:], ln1_b[l:l + 1, :].partition_broadcast(128), w=[LB])
                ocring = ph.sbring(2, [128, KC, 128], BF16, "oc")
                xring = ph.sbring(3, [128, D], F32, "x")
                vring = ph.sbring(2, [128, D], F32, "v")
                xnring = ph.sbring(1, [128, D], F32, "xn")
                x1ring = ph.sbring(3, [128, D], F32, "x1")
                string = ph.sbring(4, [128, 32], F32, "st")
                hbring = ph.sbring(4, [128, D], BF16, "hb")
                hTring = ph.sbring(2, [128, KC, 128], BF16, "hT")
                tp = [ph.ps([128, 1024], BF16, "tp") for _ in range(2)]
                pb = [ph.ps([128, 512], F32, "pb") for _ in range(4)]
                state = {"mod": None}

                def stageA(t):
                    tsl = slice(t * 128, (t + 1) * 128)
                    oc = ocring.next()
                    k.dma(k.sp, oc[:], oT[:, :, tsl].rearrange("c p n -> p c n"), w=[oc])
                    x = xring.next()
                    k.dma(k.sp, x[:], xres[tsl, :], w=[x])
                    return (oc, x)

                def stageB(t, oc, half):
                    if True:
                        for kc in range(KC):
                            for cb in (2 * half, 2 * half + 1):
                                k.mm(pb[cb][:, :], oc[:, kc, :], wo[:, kc, cb * 512:(cb + 1) * 512], kc == 0, kc == KC - 1,
                                     r=[oc, wob[kc]], w=[pb[cb]])

                def stageC0(t):
                    mr = 1 if t >= NTL else 0
                    if state["mod"] != mr:
                        k.dma(k.sp, G1[:], modrow(2, mr), w=[G1])
                        k.ts(k.dve, G1[:, :], G1[:, :], 1.0 / ALPHA, None, ALU.mult, r=[G1], w=[G1])
                        state["mod"] = mr
                    tsl = slice(t * 128, (t + 1) * 128)
                    v = vring.next()
                    for cb in range(0, 2):
                        k.tt(k.dve, v[:, cb * 512:(cb + 1) * 512], pb[cb][:, :], G1[:, cb * 512:(cb + 1) * 512], ALU.mult,
                             r=[pb[cb], G1], w=[v])
                    return v

                def stageCa(t, x, v):
                    tsl = slice(t * 128, (t + 1) * 128)
                    for cb in range(2, 4):
                        k.tt(k.dve, v[:, cb * 512:(cb + 1) * 512], pb[cb][:, :], G1[:, cb * 512:(cb + 1) * 512], ALU.mult,
                             r=[pb[cb], G1], w=[v])
                        yield None
                    k.tt(k.pool, v[:, :], x[:, :], v[:, :], ALU.add, r=[x, v], w=[v])
                    yield None
                    xn = xnring.next()
                    for _ in layer_norm_gen(k, ph, v, xn, string, EPS / (ALPHA * ALPHA)):
                        yield None
                    k.tt(k.dve, xn[:, :], xn[:, :], LG[:, :], ALU.mult, r=[xn, LG], w=[xn])
                    yield None
                    x1 = x1ring.next()
                    k.tt(k.pool, x1[:, :], xn[:, :], LB[:, :], ALU.add, r=[xn, LB], w=[x1])
                    k.dma(k.pool, x1s[tsl, :], x1[:, :], r=[x1])
                    yield x1

                def stageCb(t, x1):
                    hb = hbring.next()
                    for _ in layer_norm_gen(k, ph, x1, hb, string, EPS):
                        yield None
                    yield hb

                def run_interleaved(ga, gb):
                    ra = rb = None
                    da = ga is None
                    db = gb is None
                    while not (da and db):
                        if not da:
                            try:
                                r_ = next(ga)
                                if r_ is not None:
                                    ra = r_
                            except StopIteration:
                                da = True
                        if not db:
                            try:
                                r_ = next(gb)
                                if r_ is not None:
                                    rb = r_
                            except StopIteration:
                                db = True
                    return ra, rb

                def stageD(t, hb):
                    mr = 1 if t >= NTL else 0
                    tsl = slice(t * 128, (t + 1) * 128)
                    hT = hTring.next()
                    for half in range(2):
                        for j in range(8):
                            kc = half * 8 + j
                            k.tr(tp[half][:, j * 128:(j + 1) * 128], hb[:, kc * 128:(kc + 1) * 128], identb[:],
                                 r=[hb, identb], w=[tp[half]], sig=(j == 7))
                        for j in range(8):
                            kc = half * 8 + j
                            k.actf(hT[:, kc, :], tp[half][:, j * 128:(j + 1) * 128], AF.Identity, r=[tp[half], mT], w=[hT],
                                   scale=mT[:, (64 + kc) * 2 + mr:(64 + kc) * 2 + mr + 1],
                                   bias=mT[:, (48 + kc) * 2 + mr:(48 + kc) * 2 + mr + 1])
                    k.dma(k.act, h2T[:, :, tsl].rearrange("c p n -> p c n"), hT[:, :, :], r=[hT])

                curA = stageA(0)
                hbs = {}
                x1prev = None
                for t in range(NTa):
                    stageB(t, curA[0], 0)
                    nxtA = stageA(t + 1) if t + 1 < NTa else None
                    v = stageC0(t)
                    stageB(t, curA[0], 1)
                    if t - 3 in hbs:
                        stageD(t - 3, hbs.pop(t - 3))
                    ga = stageCa(t, curA[1], v)
                    gb = stageCb(t - 1, x1prev) if x1prev is not None else None
                    x1cur, hbp = run_interleaved(ga, gb)
                    if hbp is not None:
                        hbs[t - 1] = hbp
                    x1prev = x1cur
                    curA = nxtA
                _, hbp = run_interleaved(None, stageCb(NTa - 1, x1prev))
                hbs[NTa - 1] = hbp
                for t in sorted(hbs):
                    stageD(t, hbs[t])

            tchunks = [(cq * 512, 512) for cq in range(NQC)] + ([] if last else [(S, 256)])
            NS5 = 11
            FS5 = FC // NS5
            with Phase(k, "p5a_%d" % l) as ph:
                wslots = []
                for i in range(2):
                    wu = ph.sb([128, KC, FS5 * 128], BF16, "wu")
                    wg = ph.sb([128, KC, FS5 * 128], BF16, "wg")
                    wslots.append((wu, wg, [T(wu.h) for _ in range(KC)], [T(wg.h) for _ in range(KC)]))
                wuv = w_up[l].rearrange("(kc p) n -> p kc n", p=128)
                wgv = w_gate[l].rearrange("(kc p) n -> p kc n", p=128)
                cp = ph.sb([128, 4, FC], F32)
                k.dma(k.sp, cp[:], convp[l], w=[cp])
                hring = ph.sbring(2, [128, KC, 514], BF16, "hc")
                gsring = ph.sbring(2, [128, 514], F32, "gs")
                accring = ph.sbring(2, [128, 512], F32, "acc")
                sgring = ph.sbring(2, [128, 512], F32, "sg")
                aring = ph.sbring(2, [128, FS5, 512], BF16, "at")
                pu = ph.psring(2, [128, 512], F32, "pu")
                pg = ph.psring(2, [128, 512], F32, "pg")
                phl = ph.psring(2, [128, 2], F32, "ph")

                def loadw(sl):
                    wu, wg, wub, wgb = wslots[sl % 2]
                    cs0 = sl * FS5 * 128
                    for kc in range(KC):
                        k.dma(k.pool, wg[:, kc, :], wgv[:, kc, cs0:cs0 + FS5 * 128], w=[wgb[kc]])
                        k.dma(k.pool, wu[:, kc, :], wuv[:, kc, cs0:cs0 + FS5 * 128], w=[wub[kc]])

                loadw(0)
                for sl in range(NS5):
                    if sl + 1 < NS5:
                        loadw(sl + 1)
                    wu, wg, wub, wgb = wslots[sl % 2]
                    for (t0, tn) in tchunks:
                        hc = hring.next()
                        lo_valid = (t0 > 0 and t0 < S)
                        hi_valid = (t0 + tn < S)
                        a0 = t0 - (1 if lo_valid else 0)
                        a1 = t0 + tn + (1 if hi_valid else 0)
                        d0 = 0 if lo_valid else 1
                        k.dma(k.sp, hc[:, :, d0:d0 + (a1 - a0)], h2T[:, :, a0:a1].rearrange("c p n -> p c n"), w=[hc])
                        if not lo_valid:
                            k.emit(k.dve, lambda: nc.vector.memset(hc[:, :, 0:1], 0.0), w=[hc])
                        if not hi_valid:
                            k.emit(k.dve, lambda: nc.vector.memset(hc[:, :, tn + 1:tn + 2], 0.0), w=[hc])
                        at = aring.next()
                        for fi in range(FS5):
                            fc = sl * FS5 + fi
                            pU = pu.next()
                            pG = pg.next()
                            pH = phl.next()
                            wsl = slice(fi * 128, (fi + 1) * 128)
                            for kc in range(KC):
                                k.mm(pG[:, 0:tn], wg[:, kc, wsl], hc[:, kc, 1:1 + tn], kc == 0, kc == KC - 1,
                                     r=[wgb[kc], hc], w=[pG])
                            for kc in range(KC):
                                k.mm(pH[:, 0:2], wg[:, kc, wsl], hc[:, kc, 0:tn + 2:tn + 1], kc == 0, kc == KC - 1,
                                     r=[wgb[kc], hc], w=[pH])
                            for kc in range(KC):
                                k.mm(pU[:, 0:tn], wu[:, kc, wsl], hc[:, kc, 1:1 + tn], kc == 0, kc == KC - 1,
                                     r=[wub[kc], hc], w=[pU])
                            gs = gsring.next()
                            k.actf(gs[:, 1:1 + tn], pG[:, 0:tn], AF.Copy, r=[pG], w=[gs])
                            k.actf(gs[:, 0:tn + 2:tn + 1], pH[:, 0:2], AF.Copy, r=[pH], w=[gs])
                            acc = accring.next()
                            k.ts(k.dve, acc[:, 0:tn], gs[:, 1:1 + tn], cp[:, 1, fc:fc + 1], cp[:, 3, fc:fc + 1],
                                 ALU.mult, ALU.add, r=[gs, cp], w=[acc])
                            k.stt(k.dve, acc[:, 0:tn], gs[:, 0:tn], cp[:, 0, fc:fc + 1], acc[:, 0:tn], ALU.mult, ALU.add,
                                  r=[gs, cp, acc], w=[acc])
                            k.stt(k.dve, acc[:, 0:tn], gs[:, 2:2 + tn], cp[:, 2, fc:fc + 1], acc[:, 0:tn], ALU.mult, ALU.add,
                                  r=[gs, cp, acc], w=[acc])
                            sg = sgring.next()
                            k.actf(sg[:, 0:tn], acc[:, 0:tn], AF.Silu, r=[acc], w=[sg])
                            k.tt(k.dve, at[:, fi, 0:tn], sg[:, 0:tn], pU[:, 0:tn], ALU.mult, r=[sg, pU], w=[at])
                        k.dma(k.pool, aT[sl * FS5:(sl + 1) * FS5, :, t0:t0 + tn].rearrange("f p n -> p f n"), at[:, :, 0:tn],
                              r=[at])

            with Phase(k, "p5b_%d" % l) as ph:
                wdh = [ph.sb([128, FC, 512], BF16, "wd") for _ in range(2)]
                wdq = [[T(h.h) for _ in range(4)] for h in wdh]
                ach = [ph.sb([128, FC, 512], BF16, "ac") for _ in range(2)]
                acq = [[T(h.h) for _ in range(4)] for h in ach]
                fring = ph.sbring(3, [128, 512], F32, "f")
                pf = ph.psring(4, [128, 512], F32, "pf")
                wdv = w_down[l].rearrange("(fc p) n -> p fc n", p=128)
                aci = 0
                for cs in range(4):
                    wd = wdh[cs % 2]
                    for q4 in range(4):
                        k.dma(k.pool, wd[:, q4 * 11:(q4 + 1) * 11, :], wdv[:, q4 * 11:(q4 + 1) * 11, cs * 512:(cs + 1) * 512],
                              w=[wdq[cs % 2][q4]])
                    for (t0, tn) in tchunks:
                        ac = ach[aci % 2]
                        aq = acq[aci % 2]
                        aci += 1
                        for q4 in range(4):
                            k.dma(k.sp, ac[:, q4 * 11:(q4 + 1) * 11, 0:tn],
                                  aT[q4 * 11:(q4 + 1) * 11, :, t0:t0 + tn].rearrange("f p n -> p f n"), w=[aq[q4]])
                        for ti in range(tn // 128):
                            p = pf.next()
                            for fc in range(FC):
                                k.mm(p[:, :], ac[:, fc, ti * 128:(ti + 1) * 128], wd[:, fc, :], fc == 0, fc == FC - 1,
                                     r=[aq[fc // 11], wdq[cs % 2][fc // 11]], w=[p])
                            f = fring.next()
                            k.actf(f[:, :], p[:, :], AF.Copy, r=[p], w=[f])
                            r0 = t0 + ti * 128
                            k.dma(k.act, fs[r0:r0 + 128, cs * 512:(cs + 1) * 512], f[:, :], r=[f])

            with Phase(k, "p5c_%d" % l) as ph:
                G2 = ph.sb([128, D], F32)
                LG = ph.sb([128, D], F32)
                LB = ph.sb([128, D], F32)
                k.dma(k.sp, LG[:], ln2_g[l:l + 1, :].partition_broadcast(128), w=[LG])
                k.dma(k.sp, LB[:], ln2_b[l:l + 1, :].partition_broadcast(128), w=[LB])
                x1ring = ph.sbring(2, [128, D], F32, "x1")
                fring = ph.sbring(2, [128, D], F32, "f")
                xnring = ph.sbring(2, [128, D], F32, "xn")
                oring = ph.sbring(2, [128, D], F32, "o")
                string = ph.sbring(2, [128, 32], F32, "st")
                cur_mod = None
                for t in range(NTa):
                    mr = 1 if t >= NTL else 0
                    if cur_mod != mr:
                        k.dma(k.sp, G2[:], modrow(5, mr), w=[G2])
                        cur_mod = mr
                    tsl = slice(t * 128, (t + 1) * 128)
                    x1 = x1ring.next()
                    f = fring.next()
                    k.dma(k.sp, x1[:], x1s[tsl, :], w=[x1])
                    k.dma(k.sp, f[:], fs[tsl, :], w=[f])
                    k.tt(k.dve, f[:, :], f[:, :], G2[:, :], ALU.mult, r=[f, G2], w=[f])
                    k.stt(k.dve, f[:, :], x1[:, :], ALPHA, f[:, :], ALU.mult, ALU.add, r=[x1, f], w=[f])
                    xn = xnring.next()
                    layer_norm_rows(k, ph, f, xn, string)
                    k.tt(k.dve, xn[:, :], xn[:, :], LG[:, :], ALU.mult, r=[xn, LG], w=[xn])
                    o = oring.next()
                    k.tt(k.pool, o[:, :], xn[:, :], LB[:, :], ALU.add, r=[xn, LB], w=[o])
                    dst = y_out[tsl, :] if last else xres[tsl, :]
                    k.dma(k.pool, dst, o[:, :], r=[o])
    return nc


_CACHE = {}


def _consts(S):
    if S in _CACHE:
        return _CACHE[S]
    bf = ml_dtypes.bfloat16
    t = np.arange(S)
    row = (t // 64).astype(np.float32)
    col = (t % 64).astype(np.float32)
    inv = (10000.0 ** (-np.arange(32, dtype=np.float32) / 32)).astype(np.float32)
    ang = np.concatenate([row[:, None] * inv[None, :], col[:, None] * inv[None, :]], axis=1).astype(np.float32)
    ropec = np.cos(ang).astype(np.float32)
    ropes = np.sin(ang).astype(np.float32)

    def dft(N, scale):
        i = np.arange(N, dtype=np.int64)
        m = (i[:, None] * i[None, :]) % N
        a = (2.0 * np.pi / N) * m.astype(np.float64)
        return (np.cos(a) * scale), (np.sin(a) * scale)

    c, s = dft(S, 1.0 / np.sqrt(S))
    dftc, dfts = c.astype(np.float32).astype(bf), s.astype(np.float32).astype(bf)
    c, s = dft(L, 1.0 / np.sqrt(L))
    dftc_c, dfts_c = c.astype(np.float32).astype(bf), s.astype(np.float32).astype(bf)
    c, s = dft(128, 1.0 / np.sqrt(128.0))
    c128 = c.astype(np.float32)
    ns128 = (-s).astype(np.float32)
    maskb = np.full((128, 6, 512), NEG, np.float32)
    sj = np.arange(128)[:, None]
    qi = np.arange(128)[None, :]
    for r in range(-1, 5):
        for cblk in range(4):
            d = r - cblk
            if d == -1:
                m = np.where(qi <= sj, 0.0, NEG)
            elif d == 0:
                m = np.zeros((128, 128))
            elif d == 1:
                m = np.where(sj <= qi, 0.0, NEG)
            else:
                continue
            maskb[:, r + 1, cblk * 128:(cblk + 1) * 128] = m
    out = dict(ropec=ropec, ropes=ropes, dftc=dftc, dfts=dfts, dftc_c=dftc_c, dfts_c=dfts_c, c128=c128, ns128=ns128,
               maskb=maskb.astype(bf), identb=np.eye(128, dtype=np.float32).astype(bf),
               identf=np.eye(128, dtype=np.float32))
    _CACHE[S] = out
    return out


_NC = {}


def kernel(x, c, ctx, c_ctx, w_mod, b_mod, w_in, q_gain_a, k_gain_a, sink_b, w_fourier, w_out, ln1_g, ln1_b,
           w_up, w_gate, conv_w, conv_b, w_down, ln2_g, ln2_b, _dbg=False):
    f = lambda a: np.ascontiguousarray(np.asarray(a, dtype=np.float32))
    x = f(x)
    B, S, _ = x.shape
    depth = w_mod.shape[0]
    key = (S, depth, _dbg)
    if key not in _NC:
        _NC[key] = build(S, depth, dbg=_dbg)
    nc = _NC[key]
    cs = _consts(S)
    c = f(c)
    ctx = f(ctx)
    c_ctx = f(c_ctx)
    conv_w = f(conv_w)
    conv_b = f(conv_b)
    cp = np.concatenate([conv_w, conv_b[:, None, :]], axis=1)
    convp = np.ascontiguousarray(cp.reshape(depth, 4, FC, 128).transpose(0, 3, 1, 2))
    shared = dict(w_mod=f(w_mod), b_mod=f(b_mod), w_in=f(w_in), q_gain_a=f(q_gain_a), k_gain_a=f(k_gain_a),
                  sink_b=f(sink_b), w_fourier=f(w_fourier), w_out=f(w_out), ln1_g=f(ln1_g), ln1_b=f(ln1_b),
                  w_up=f(w_up), w_gate=f(w_gate), convp=convp, w_down=f(w_down), ln2_g=f(ln2_g), ln2_b=f(ln2_b))
    shared.update(cs)
    in_maps = []
    for b in range(B):
        cc = np.stack([c[b], c_ctx], axis=0)
        ccT = np.ascontiguousarray(cc.reshape(2, KC, 128).transpose(2, 1, 0))
        m = dict(shared)
        m.update(x=x[b], ctx=ctx[b], ccT=ccT)
        in_maps.append(m)
    res = run_bass_kernel_spmd(nc, in_maps, core_ids=list(range(B)))
    if _dbg:
        return res
    return np.stack([np.asarray(r["y"], dtype=np.float32) for r in res.results], axis=0)
```

```python
from contextlib import ExitStack
import numpy as np
import ml_dtypes
import concourse.bass as bass
import concourse.mybir as mybir
from concourse.bass_utils import run_bass_kernel_spmd

F32 = mybir.dt.float32
BF16 = mybir.dt.bfloat16
AF = mybir.ActivationFunctionType
ALU = mybir.AluOpType
AX = mybir.AxisListType

D = 2048
KC = 16
L = 256
HD = 128
FF = 5632
FC = 44
INW = 3072
EPS = 1e-6
ALPHA = (2.0 * 4) ** 0.25
QSCALE = 128.0 ** -0.5
NEG = -30000.0
NSL = 4
FCS = FC // NSL


class Buf:
    __slots__ = ("w", "r")

    def __init__(self):
        self.w = {}
        self.r = {}


class Sem:
    __slots__ = ("h", "cnt", "step", "sid")

    def __init__(self, h, step, sid):
        self.h = h
        self.cnt = 0
        self.step = step
        self.sid = sid


class Seq:
    def __init__(self, eng, name, inorder=False):
        self.eng = eng
        self.name = name
        self.seen = {}
        self.csem = None
        self.dsems = []
        self.rr = 0
        self.inorder = inorder


class T:
    def __init__(self, h, b=None):
        self.h = h
        self.b = b if b is not None else Buf()

    def __getitem__(self, key):
        return self.h[key]


class Ring:
    def __init__(self, tiles):
        self.tiles = tiles
        self.i = 0

    def next(self):
        t = self.tiles[self.i % len(self.tiles)]
        self.i += 1
        return t


class KB:
    def __init__(self, nc):
        self.nc = nc
        self.stack = ExitStack()
        self.sems = []
        self.pe = Seq(nc.tensor, "pe", inorder=True)
        self.act = Seq(nc.scalar, "act")
        self.dve = Seq(nc.vector, "dve")
        self.pool = Seq(nc.gpsimd, "pool")
        self.sp = Seq(nc.sync, "sp")
        self.seqs = [self.pe, self.act, self.dve, self.pool, self.sp]
        for s in (self.pe, self.act, self.dve, self.pool):
            s.csem = self._sem("c_" + s.name, 1)
        for s, n in ((self.sp, 10), (self.pool, 8), (self.act, 4)):
            s.dsems = [self._sem("d_%s%d" % (s.name, i), 16) for i in range(n)]

    def _sem(self, name, step):
        h = self.stack.enter_context(self.nc.semaphore(name))
        s = Sem(h, step, len(self.sems))
        self.sems.append(s)
        return s

    def emit(self, seq, fn, r=(), w=(), dma=False, sig=True):
        need = {}

        def nd(d):
            for sid, v in d.items():
                if v > need.get(sid, 0):
                    need[sid] = v

        for b in r:
            nd(b.b.w if isinstance(b, T) else b.w)
        for b in w:
            bb = b.b if isinstance(b, T) else b
            nd(bb.r)
            nd(bb.w)
        if dma:
            sem = seq.dsems[seq.rr % len(seq.dsems)]
            seq.rr += 1
            if sem.cnt > 0:
                if sem.cnt > need.get(sem.sid, 0):
                    need[sem.sid] = sem.cnt
        else:
            sem = seq.csem
        for sid, v in need.items():
            if seq.inorder and not dma and sid == seq.csem.sid:
                continue
            if seq.seen.get(sid, 0) >= v:
                continue
            s = self.sems[sid]
            assert v <= s.cnt, "waiting on a pending ticket (%s sid=%d v=%d cnt=%d)" % (seq.name, sid, v, s.cnt)
            seq.eng.wait_ge(s.h, v)
            seq.seen[sid] = v
        ins = fn()
        if sig:
            sem.cnt += sem.step
            ins.then_inc(sem.h, sem.step)
            tk = sem.cnt
        else:
            tk = sem.cnt + sem.step
        for b in r:
            bb = b.b if isinstance(b, T) else b
            if tk > bb.r.get(sem.sid, 0):
                bb.r[sem.sid] = tk
        for b in w:
            bb = b.b if isinstance(b, T) else b
            bb.w = {sem.sid: tk}
            bb.r = {}
        return ins

    def barrier(self):
        for seq in self.seqs:
            for s in self.sems:
                if s.cnt > seq.seen.get(s.sid, 0):
                    seq.eng.wait_ge(s.h, s.cnt)
                    seq.seen[s.sid] = s.cnt

    def dma(self, seq, out, in_, r=(), w=(), **kw):
        return self.emit(seq, lambda: seq.eng.dma_start(out=out, in_=in_, **kw), r=r, w=w, dma=True)

    def mm(self, out, lhsT, rhs, start, stop, r=(), w=(), sig=None):
        if sig is None:
            sig = stop
        return self.emit(self.pe, lambda: self.nc.tensor.matmul(out, lhsT, rhs, start=start, stop=stop),
                         r=r, w=w, sig=sig)

    def tr(self, out, in_, ident, r=(), w=(), sig=True):
        return self.emit(self.pe, lambda: self.nc.tensor.transpose(out=out, in_=in_, identity=ident),
                         r=r, w=w, sig=sig)

    def actf(self, out, in_, func, r=(), w=(), **kw):
        return self.emit(self.act, lambda: self.nc.scalar.activation(out=out, in_=in_, func=func, **kw), r=r, w=w)

    def tt(self, seq, out, in0, in1, op, r=(), w=()):
        return self.emit(seq, lambda: seq.eng.tensor_tensor(out=out, in0=in0, in1=in1, op=op), r=r, w=w)

    def ts(self, seq, out, in0, s1, s2, op0, op1=None, r=(), w=()):
        if op1 is None:
            return self.emit(seq, lambda: seq.eng.tensor_scalar(out=out, in0=in0, scalar1=s1, scalar2=None, op0=op0),
                             r=r, w=w)
        return self.emit(seq, lambda: seq.eng.tensor_scalar(out=out, in0=in0, scalar1=s1, scalar2=s2, op0=op0, op1=op1),
                         r=r, w=w)

    def stt(self, seq, out, in0, scalar, in1, op0, op1, r=(), w=()):
        return self.emit(seq, lambda: seq.eng.scalar_tensor_tensor(out=out, in0=in0, scalar=scalar, in1=in1,
                                                                   op0=op0, op1=op1), r=r, w=w)


class Phase:
    def __init__(self, k, name):
        self.k = k
        self.nc = k.nc
        self.name = name
        self.es = ExitStack()
        self.n = 0

    def __enter__(self):
        self.es.__enter__()
        return self

    def __exit__(self, *a):
        self.k.barrier()
        return self.es.__exit__(*a)

    def sb(self, shape, dt, nm="t"):
        self.n += 1
        h = self.es.enter_context(self.nc.sbuf_tensor("%s_%s%d" % (self.name, nm, self.n), list(shape), dt))
        return T(h)

    def ps(self, shape, dt, nm="p"):
        self.n += 1
        h = self.es.enter_context(self.nc.psum_tensor("%s_%s%d" % (self.name, nm, self.n), list(shape), dt))
        return T(h)

    def const(self, val):
        if not hasattr(self, "_consts"):
            self._consts = {}
        if val not in self._consts:
            t = self.sb([128, 1], F32, "c")
            self.k.emit(self.k.dve, lambda: self.nc.vector.memset(t[:], val), w=[t])
            self._consts[val] = t
        return self._consts[val]

    def sbring(self, n, shape, dt, nm="r"):
        return Ring([self.sb(shape, dt, nm) for _ in range(n)])

    def psring(self, n, shape, dt, nm="pr"):
        return Ring([self.ps(shape, dt, nm) for _ in range(n)])


def layer_norm_rows(k, ph, x, xn, stat_ring, eps=EPS):
    nc = k.nc
    st = stat_ring.next()
    for j in range(4):
        k.emit(k.dve, lambda: nc.vector.bn_stats(out=st[:, j * 6:(j + 1) * 6], in_=x[:, j * 512:(j + 1) * 512]),
               r=[x], w=[st])
    k.emit(k.dve, lambda: nc.vector.bn_aggr(out=st[:, 24:26], in_=st[:, 0:24]), r=[st], w=[st])
    epsT = ph.const(eps)
    k.actf(st[:, 26:27], st[:, 25:26], AF.Sqrt, r=[st, epsT], w=[st], bias=epsT[:, 0:1], scale=1.0)
    k.emit(k.dve, lambda: nc.vector.reciprocal(out=st[:, 26:27], in_=st[:, 26:27]), r=[st], w=[st])
    k.stt(k.dve, st[:, 27:28], st[:, 24:25], -1.0, st[:, 26:27], ALU.mult, ALU.mult, r=[st], w=[st])
    k.actf(xn[:, :], x[:, :], AF.Identity, r=[x, st], w=[xn], bias=st[:, 27:28], scale=st[:, 26:27])


def layer_norm_gen(k, ph, x, xn, stat_ring, eps):
    nc = k.nc
    st = stat_ring.next()
    for j in range(4):
        k.emit(k.dve, lambda: nc.vector.bn_stats(out=st[:, j * 6:(j + 1) * 6], in_=x[:, j * 512:(j + 1) * 512]),
               r=[x], w=[st])
        yield None
    k.emit(k.dve, lambda: nc.vector.bn_aggr(out=st[:, 24:26], in_=st[:, 0:24]), r=[st], w=[st])
    yield None
    epsT = ph.const(eps)
    k.actf(st[:, 26:27], st[:, 25:26], AF.Sqrt, r=[st, epsT], w=[st], bias=epsT[:, 0:1], scale=1.0)
    yield None
    k.emit(k.dve, lambda: nc.vector.reciprocal(out=st[:, 26:27], in_=st[:, 26:27]), r=[st], w=[st])
    yield None
    k.stt(k.dve, st[:, 27:28], st[:, 24:25], -1.0, st[:, 26:27], ALU.mult, ALU.mult, r=[st], w=[st])
    yield None
    k.actf(xn[:, :], x[:, :], AF.Identity, r=[x, st], w=[xn], bias=st[:, 27:28], scale=st[:, 26:27])
    yield None


def build(S, depth, dbg=False):
    assert S % 512 == 0
    NTL = S // 128
    NT = NTL + 2
    NTOK = S + L
    NQC = S // 512
    nc = bass.Bass("TRN2", target_bir_lowering=False)

    def din(name, shape, dt=F32):
        return nc.dram_tensor(name, list(shape), dt, kind="ExternalInput").ap()

    def dscr(name, shape, dt, out=False):
        return nc.dram_tensor(name, list(shape), dt, kind="ExternalOutput" if (out or dbg) else "Internal").ap()

    x_in = din("x", [S, D])
    ctx_in = din("ctx", [L, D])
    ccT = din("ccT", [128, KC, 2])
    w_mod = din("w_mod", [depth, D, 6 * D])
    b_mod = din("b_mod", [depth, 6 * D])
    w_in = din("w_in", [depth, D, INW])
    q_gain = din("q_gain_a", [depth, HD])
    k_gain = din("k_gain_a", [depth, HD])
    sink_b = din("sink_b", [depth, 4])
    w_f = din("w_fourier", [depth, 4, 128, 128])
    w_out = din("w_out", [depth, D, D])
    ln1_g = din("ln1_g", [depth, D])
    ln1_b = din("ln1_b", [depth, D])
    w_up = din("w_up", [depth, D, FF])
    w_gate = din("w_gate", [depth, D, FF])
    convp = din("convp", [depth, 128, 4, FC])
    w_down = din("w_down", [depth, FF, D])
    ln2_g = din("ln2_g", [depth, D])
    ln2_b = din("ln2_b", [depth, D])
    ropec = din("ropec", [S, 64])
    ropes = din("ropes", [S, 64])
    dftc = din("dftc", [S, S], BF16)
    dfts = din("dfts", [S, S], BF16)
    dftc_c = din("dftc_c", [L, L], BF16)
    dfts_c = din("dfts_c", [L, L], BF16)
    c128 = din("c128", [128, 128])
    ns128 = din("ns128", [128, 128])
    maskb_in = din("maskb", [128, 6, 512], BF16)
    identb_in = din("identb", [128, 128], BF16)
    identf_in = din("identf", [128, 128])

    y_out = nc.dram_tensor("y", [S, D], F32, kind="ExternalOutput").ap()

    xres = dscr("xres", [NTOK, D], F32)
    x1s = dscr("x1s", [NTOK, D], F32)
    fs = dscr("fs", [NTOK, D], F32)
    modv = dscr("modv", [depth, 2, 6 * D], F32)
    modT = dscr("modT", [depth, 128, 96, 2], F32)
    qT = dscr("qT", [12, 128, NTOK], BF16)
    kT = dscr("kT", [4, 128, NTOK], BF16)
    vS = dscr("vS", [NTOK, 512], BF16)
    uT = dscr("uT", [4, 128, NTOK], BF16)
    oT = dscr("oT", [KC, 128, NTOK], BF16)
    h2T = dscr("h2T", [KC, 128, NTOK], BF16)
    aT = dscr("aT", [FC, 128, NTOK], BF16)

    k = KB(nc)
    with k.stack:
        with Phase(k, "pro") as ph:
            k.dma(k.sp, xres[0:S, :], x_in[:, :])
            k.dma(k.sp, xres[S:NTOK, :], ctx_in[:, :])

        for l in range(depth):
            last = (l == depth - 1)
            NTa = NTL if last else NT
            with Phase(k, "p0_%d" % l) as ph:
                sc = ph.sb([128, KC, 2], F32)
                k.dma(k.sp, sc[:], ccT[:, :, :], w=[sc])
                scs = ph.sb([128, KC, 2], F32)
                k.actf(scs[:], sc[:], AF.Silu, r=[sc], w=[scs])
                bm = ph.sb([2, 6 * D], F32)
                k.dma(k.sp, bm[:], b_mod[l:l + 1, :].partition_broadcast(2), w=[bm])
                mo = ph.sb([2, 6 * D], F32)
                wring = ph.sbring(12, [128, 4, 512], F32, "wm")
                pring = ph.psring(2, [2, 512], F32)
                wv = w_mod[l].rearrange("(kc p) n -> p kc n", p=128)
                for cb in range(24):
                    wq = []
                    for q4 in range(4):
                        wt = wring.next()
                        k.dma(k.sp, wt[:, :, :], wv[:, q4 * 4:(q4 + 1) * 4, cb * 512:(cb + 1) * 512], w=[wt])
                        wq.append(wt)
                    p = pring.next()
                    for kc in range(KC):
                        wt = wq[kc // 4]
                        k.mm(p[:, :], scs[:, kc, :], wt[:, kc % 4, :], kc == 0, kc == KC - 1, r=[scs, wt], w=[p])
                    k.tt(k.dve, mo[:, cb * 512:(cb + 1) * 512], p[:, :], bm[:, cb * 512:(cb + 1) * 512], ALU.add,
                         r=[p, bm], w=[mo])
                for a in (1, 4):
                    k.ts(k.dve, mo[:, a * D:(a + 1) * D], mo[:, a * D:(a + 1) * D], 1.0, None, ALU.add, r=[mo], w=[mo])
                k.dma(k.sp, modv[l], mo[:], r=[mo])
                idf = ph.sb([2, 2], F32)
                k.dma(k.sp, idf[:], identf_in[0:2, 0:2], w=[idf])
                pT = ph.ps([128, 96 * 2], F32, "pT")
                for j in range(96):
                    k.tr(pT[:, 2 * j:2 * j + 2], mo[:, j * 128:(j + 1) * 128], idf[:], r=[mo, idf], w=[pT], sig=(j == 95))
                moT = ph.sb([128, 96 * 2], F32)
                k.actf(moT[:], pT[:], AF.Copy, r=[pT], w=[moT])
                k.dma(k.sp, modT[l].rearrange("p a b -> p (a b)"), moT[:], r=[moT])

            def modrow(idx, r):
                return modv[l, r:r + 1, idx * D:(idx + 1) * D].partition_broadcast(128)

            with Phase(k, "p1_%d" % l) as ph:
                wi = ph.sb([128, KC, INW], BF16, "wi")
                wiv = w_in[l].rearrange("(kc p) n -> p kc n", p=128)
                wiq = [{dc: T(wi.h) for dc in (0, 1536, 1792, 2048, 2304)} for _ in range(KC)]
                colmap = ((0, 0, 1536), (1536, 1536, 256), (1792, 2048, 256), (2048, 1792, 256), (2304, 2304, 768))
                for kc in range(KC):
                    for (dc, sc_, n_) in colmap:
                        hh = 0 if dc < 1536 else 1
                        k.dma(k.pool, wi[:, kc, dc:dc + n_], wiv[:, kc, sc_:sc_ + n_], w=[wiq[kc][dc]])
                identb = ph.sb([128, 128], BF16)
                k.dma(k.sp, identb[:], identb_in[:, :], w=[identb])
                mT = ph.sb([128, 96 * 2], F32, "mT")
                k.dma(k.sp, mT[:], modT[l].rearrange("p a b -> p (a b)"), w=[mT])
                qg = ph.sb([128, HD], F32)
                kg = ph.sb([128, HD], F32)
                k.dma(k.sp, qg[:], q_gain[l:l + 1, :].partition_broadcast(128), w=[qg])
                k.dma(k.sp, kg[:], k_gain[l:l + 1, :].partition_broadcast(128), w=[kg])
                k.ts(k.dve, qg[:], qg[:], QSCALE, None, ALU.mult, r=[qg], w=[qg])
                rcring = ph.sbring(2, [128, 64], F32, "rc")
                rsring = ph.sbring(2, [128, 64], F32, "rs")
                xring = ph.sbring(2, [128, D], F32, "x")
                string = ph.sbring(2, [128, 32], F32, "st")
                hbring = ph.sbring(2, [128, D], BF16, "hb")
                hTring = ph.sbring(2, [128, KC, 128], BF16, "hT")
                tp = [ph.ps([128, 1024], BF16, "tp") for _ in range(2)]
                pb = [ph.ps([128, 512], F32, "pb") for _ in range(6)]
                psb = ph.sb([128, INW], F32, "psb")
                psbq = T(psb.h)
                psbv = T(psb.h)
                sqring = ph.sbring(2, [128, 512], F32, "sq")
                ssring = ph.sbring(2, [128, 16], F32, "ss")
                qkrring = ph.sbring(2, [128, D], BF16, "qkr")
                tmpA = ph.sbring(1, [128, 16, 2, 32], F32, "ta")
                tmpB = ph.sbring(1, [128, 16, 2, 32], F32, "tb")
                vtring = ph.sbring(2, [128, 512], BF16, "vt")
                utring = ph.sbring(2, [128, 512], BF16, "ut")
                qTring = ph.sbring(2, [128, 16, 128], BF16, "qT")
                uTring = ph.sbring(2, [128, 4, 128], BF16, "uT")
                state = {"mod": None}

                def stageA1(t):
                    x = xring.next()
                    k.dma(k.sp, x[:], xres[t * 128:(t + 1) * 128, :], w=[x])
                    hb = hbring.next()
                    layer_norm_rows(k, ph, x, hb, string)
                    return hb

                def stageA2(t, hb):
                    isctx = t >= NTL
                    mr = 1 if isctx else 0
                    hT = hTring.next()
                    for half in range(2):
                        for j in range(8):
                            kc = half * 8 + j
                            k.tr(tp[half][:, j * 128:(j + 1) * 128], hb[:, kc * 128:(kc + 1) * 128], identb[:],
                                 r=[hb, identb], w=[tp[half]], sig=(j == 7))
                        for j in range(8):
                            kc = half * 8 + j
                            k.actf(hT[:, kc, :], tp[half][:, j * 128:(j + 1) * 128], AF.Identity, r=[tp[half], mT], w=[hT],
                                   scale=mT[:, (16 + kc) * 2 + mr:(16 + kc) * 2 + mr + 1],
                                   bias=mT[:, kc * 2 + mr:kc * 2 + mr + 1])
                    return hT

                def stageB(t, hT, half):
                    if True:
                        for kc in range(KC):
                            for cb in range(half * 3, half * 3 + 3):
                                k.mm(pb[cb][:, :], hT[:, kc, :], wi[:, kc, cb * 512:(cb + 1) * 512], kc == 0, kc == KC - 1,
                                     r=[hT] + ([wiq[kc][0]] if half == 0 else [wiq[kc][dc] for dc in (1536, 1792, 2048, 2304)]), w=[pb[cb]])

                def stageC0(t, lo, hi):
                    for cb in range(lo, hi):
                        dstb = psbq if cb < 4 else psbv
                        if cb % 2 == 0:
                            k.actf(psb[:, cb * 512:(cb + 1) * 512], pb[cb][:, :], AF.Copy, r=[pb[cb]], w=[dstb])
                        else:
                            k.emit(k.dve, lambda: nc.vector.tensor_copy(out=psb[:, cb * 512:(cb + 1) * 512], in_=pb[cb][:, :]),
                                   r=[pb[cb]], w=[dstb])

                def stageC(t):
                    isctx = t >= NTL
                    ss = ssring.next()
                    for (c0, nh, ofs) in ((0, 4, 0), (512, 4, 4), (1536, 2, 8)):
                        sq = sqring.next()
                        k.actf(sq[:, 0:nh * 128], psb[:, c0:c0 + nh * 128], AF.Square, r=[psbq], w=[sq])
                        k.emit(k.dve, lambda: nc.vector.tensor_reduce(
                            out=ss[:, ofs:ofs + nh], in_=sq[:, 0:nh * 128].rearrange("p (h d) -> p h d", d=128),
                            axis=AX.X, op=ALU.add), r=[sq], w=[ss])
                    epsT = ph.const(EPS)
                    k.actf(ss[:, 0:10], ss[:, 0:10], AF.Sqrt, r=[ss, epsT], w=[ss], bias=epsT[:, 0:1], scale=1.0 / 128.0)
                    k.emit(k.dve, lambda: nc.vector.reciprocal(out=ss[:, 0:10], in_=ss[:, 0:10]), r=[ss], w=[ss])
                    for h in range(8):
                        k.stt(k.dve, psb[:, h * 128:(h + 1) * 128], psb[:, h * 128:(h + 1) * 128],
                              ss[:, h:h + 1], qg[:, :], ALU.mult, ALU.mult, r=[psbq, ss, qg], w=[psbq])
                    k.actf(psb[:, 1024:1536], psb[:, 1024:1536], AF.Copy, r=[psbq], w=[psbq], scale=QSCALE)
                    for h in range(2):
                        c0 = 1536 + h * 128
                        k.stt(k.dve, psb[:, c0:c0 + 128], psb[:, c0:c0 + 128],
                              ss[:, 8 + h:9 + h], kg[:, :], ALU.mult, ALU.mult, r=[psbq, ss, kg], w=[psbq])
                    vt = vtring.next()
                    k.actf(vt[:, :], psb[:, 2048:2560], AF.Copy, r=[psbv], w=[vt])
                    ut = utring.next()
                    k.actf(ut[:, :], psb[:, 2560:3072], AF.Copy, r=[psbv], w=[ut])
                    qkr = qkrring.next()
                    if isctx:
                        k.actf(qkr[:, :], psb[:, 0:2048], AF.Copy, r=[psbq], w=[qkr])
                    else:
                        q5 = psb[:, 0:2048].rearrange("p (h x y f) -> p h x y f", h=16, x=2, y=2)
                        o5 = qkr[:, :].rearrange("p (h x y f) -> p h x y f", h=16, x=2, y=2)
                        a_ = q5[:, :, :, 0, :]
                        b_ = q5[:, :, :, 1, :]
                        rc = rcring.next()
                        rs_ = rsring.next()
                        k.dma(k.sp, rc[:], ropec[t * 128:(t + 1) * 128, :], w=[rc])
                        k.dma(k.sp, rs_[:], ropes[t * 128:(t + 1) * 128, :], w=[rs_])
                        cc = rc[:, :].rearrange("p (x f) -> p x f", x=2).unsqueeze(1).to_broadcast([128, 16, 2, 32])
                        sn = rs_[:, :].rearrange("p (x f) -> p x f", x=2).unsqueeze(1).to_broadcast([128, 16, 2, 32])
                        ta = tmpA.next()
                        tb = tmpB.next()
                        k.tt(k.dve, ta[:], a_, cc, ALU.mult, r=[psbq, rc], w=[ta])
                        k.tt(k.pool, tb[:], b_, sn, ALU.mult, r=[psbq, rs_], w=[tb])
                        k.tt(k.dve, o5[:, :, :, 0, :], ta[:], tb[:], ALU.subtract, r=[ta, tb], w=[qkr])
                        ta = tmpA.next()
                        tb = tmpB.next()
                        k.tt(k.pool, ta[:], a_, sn, ALU.mult, r=[psbq, rs_], w=[ta])
                        k.tt(k.dve, tb[:], b_, cc, ALU.mult, r=[psbq, rc], w=[tb])
                        k.tt(k.pool, o5[:, :, :, 1, :], ta[:], tb[:], ALU.add, r=[ta, tb], w=[qkr])
                    return (qkr, ut, vt)

                def stageD(t, qkr, ut, vt):
                    qTt = qTring.next()
                    for half in range(2):
                        for j in range(8):
                            c = half * 8 + j
                            k.tr(tp[half][:, j * 128:(j + 1) * 128], qkr[:, c * 128:(c + 1) * 128], identb[:],
                                 r=[qkr, identb], w=[tp[half]], sig=(j == 7))
                        k.emit(k.dve, lambda: nc.vector.tensor_copy(
                            out=qTt[:, half * 8:(half + 1) * 8, :],
                            in_=tp[half][:, :].rearrange("p (a b) -> p a b", b=128)), r=[tp[half]], w=[qTt])
                    uTt = uTring.next()
                    for j in range(4):
                        k.tr(tp[0][:, j * 128:(j + 1) * 128], ut[:, j * 128:(j + 1) * 128], identb[:],
                             r=[ut, identb], w=[tp[0]], sig=(j == 3))
                    k.emit(k.dve, lambda: nc.vector.tensor_copy(
                        out=uTt[:, :, :], in_=tp[0][:, 0:512].rearrange("p (a b) -> p a b", b=128)), r=[tp[0]], w=[uTt])
                    tsl = slice(t * 128, (t + 1) * 128)
                    k.dma(k.pool, qT[:, :, tsl].rearrange("h p n -> p h n"), qTt[:, 0:12, :], r=[qTt])
                    k.dma(k.pool, kT[:, :, tsl].rearrange("h p n -> p h n"), qTt[:, 12:16, :], r=[qTt])
                    k.dma(k.pool, uT[:, :, tsl].rearrange("h p n -> p h n"), uTt[:, :, :], r=[uTt])
                    k.dma(k.pool, vS[tsl, :], vt[:, :], r=[vt])

                hbs = {0: stageA1(0)}
                if NT > 1:
                    hbs[1] = stageA1(1)
                hT_cur = stageA2(0, hbs.pop(0))
                prevC = None
                for t in range(NT):
                    stageB(t, hT_cur, 0)
                    hT_nxt = stageA2(t + 1, hbs.pop(t + 1)) if t + 1 < NT else None
                    stageC0(t, 0, 3)
                    stageB(t, hT_cur, 1)
                    if prevC is not None:
                        stageD(t - 1, *prevC)
                    stageC0(t, 3, 6)
                    if t + 2 < NT:
                        hbs[t + 2] = stageA1(t + 2)
                    prevC = stageC(t)
                    hT_cur = hT_nxt
                stageD(NT - 1, *prevC)

            with Phase(k, "p2_%d" % l) as ph:
                kTs = ph.sb([128, 4, NTOK], BF16, "kTs")
                kTb = [T(kTs.h) for _ in range(4)]
                for h in range(4):
                    k.dma(k.sp, kTs[:, h, :], kT[h], w=[kTb[h]])
                vs = ph.sb([128, NT, 512], BF16, "vs")
                k.dma(k.sp, vs[:], vS.rearrange("(t p) c -> p t c", p=128), w=[vs])
                ones = ph.sb([128, 128], BF16)
                k.emit(k.dve, lambda: nc.vector.memset(ones[:], 1.0), w=[ones])
                identb = ph.sb([128, 128], BF16)
                k.dma(k.sp, identb[:], identb_in[:, :], w=[identb])
                maskb = ph.sb([128, 6, 512], BF16)
                k.dma(k.sp, maskb[:], maskb_in[:, :, :], w=[maskb])
                es = ph.sb([128, 4], F32)
                k.dma(k.sp, es[:], sink_b[l:l + 1, :].partition_broadcast(128), w=[es])
                k.actf(es[:], es[:], AF.Exp, r=[es], w=[es])
                qring = ph.sbring(2, [128, NTOK], BF16, "q")
                pring = ph.sbring(4, [128, 512], BF16, "pt")
                oring = ph.sbring(2, [128, 512], BF16, "ot")
                rdring = ph.sbring(2, [128, 512], F32, "rd")
                ps_s = ph.psring(3, [128, 512], F32, "s")
                ps_o = ph.psring(2, [128, 512], F32, "o")
                ps_d = ph.psring(2, [128, 512], F32, "d")
                for hq in range(12):
                    isB = hq >= 8
                    kv = (hq // 4) if not isB else (2 + (hq - 8) // 2)
                    q = qring.next()
                    k.dma(k.sp, q[:], qT[hq], w=[q])
                    chunks = [(qc * 512, 512, False) for qc in range(NQC)] + ([] if last else [(S, 256, True)])
                    for (q0, qn, isctx) in chunks:
                        if isctx:
                            tiles = [(NTL, None), (NTL + 1, None)]
                        elif not isB:
                            tiles = [(t, None) for t in range(NT)]
                        else:
                            n0 = q0 // 128
                            tiles = [(n0 + r, r + 1) for r in range(-1, 5) if 0 <= n0 + r < NTL]
                            tiles += [(NTL, None), (NTL + 1, None)]
                        po = ps_o.next()
                        pd = ps_d.next()
                        n = len(tiles)

                        def qk(i):
                            st, mi = tiles[i]
                            p = ps_s.next()
                            k.mm(p[:, 0:qn], kTs[:, kv, st * 128:(st + 1) * 128], q[:, q0:q0 + qn], True, mi is None,
                                 r=[kTb[kv], q], w=[p])
                            if mi is not None:
                                k.mm(p[:, 0:qn], identb[:], maskb[:, mi, 0:qn], False, True, r=[identb, maskb], w=[p])
                            return p

                        LA = 2
                        pend = [qk(i) for i in range(min(LA, n))]
                        for i in range(n):
                            p = pend.pop(0)
                            pt = pring.next()
                            k.actf(pt[:, 0:qn], p[:, 0:qn], AF.Exp, r=[p], w=[pt])
                            if i + LA < n:
                                pend.append(qk(i + LA))
                            st = tiles[i][0]
                            k.mm(po[:, 0:qn], vs[:, st, kv * 128:(kv + 1) * 128], pt[:, 0:qn], i == 0, i == n - 1,
                                 r=[vs, pt], w=[po])
                            k.mm(pd[:, 0:qn], ones[:], pt[:, 0:qn], i == 0, i == n - 1, r=[ones, pt], w=[pd])
                        rd = rdring.next()
                        if isB:
                            k.ts(k.dve, rd[:, 0:qn], pd[:, 0:qn], es[:, hq - 8:hq - 7], None, ALU.add, r=[pd, es], w=[rd])
                            k.emit(k.dve, lambda: nc.vector.reciprocal(out=rd[:, 0:qn], in_=rd[:, 0:qn]), r=[rd], w=[rd])
                        else:
                            k.emit(k.dve, lambda: nc.vector.reciprocal(out=rd[:, 0:qn], in_=pd[:, 0:qn]), r=[pd], w=[rd])
                        ot = oring.next()
                        k.tt(k.dve, ot[:, 0:qn], po[:, 0:qn], rd[:, 0:qn], ALU.mult, r=[po, rd], w=[ot])
                        k.dma(k.pool, oT[hq, :, q0:q0 + qn], ot[:, 0:qn], r=[ot])

            with Phase(k, "p3_%d" % l) as ph:
                c128t = ph.sb([128, 128], F32)
                ns128t = ph.sb([128, 128], F32)
                k.dma(k.sp, c128t[:], c128[:, :], w=[c128t])
                k.dma(k.sp, ns128t[:], ns128[:, :], w=[ns128t])
                wf = ph.sb([128, 4, 128], F32)
                k.dma(k.sp, wf[:], w_f[l].rearrange("g c e -> c g e"), w=[wf])
                AB = ph.sb([128, 4, 256], BF16)
                pw = ph.psring(2, [128, 512], F32, "pw")
                pacc = ph.psring(6, [128, 512], F32, "pa")
                for g in range(4):
                    p = pw.next()
                    k.mm(p[:, 0:128], c128t[:], wf[:, g, :], True, True, r=[c128t, wf], w=[p])
                    k.mm(p[:, 128:256], ns128t[:], wf[:, g, :], True, True, r=[ns128t, wf], w=[p])
                    k.actf(AB[:, g, :], p[:, 0:256], AF.Copy, r=[p], w=[AB])
                cring = ph.sbring(3, [128, 8, 512], BF16, "dc")
                sring = ph.sbring(3, [128, 8, 512], BF16, "ds")
                foring = ph.sbring(3, [128, 512], BF16, "fo")
                segs = [(S, 0, dftc, dfts, 512)] + ([] if last else [(L, S, dftc_c, dfts_c, 256)])
                for (N, off, Ct, St, CH) in segs:
                    MT = N // 128
                    uts = ph.sb([128, 4, N], BF16, "uts")
                    utb = [T(uts.h) for _ in range(4)]
                    for g in range(4):
                        k.dma(k.sp, uts[:, g, :], uT[g, :, off:off + N], w=[utb[g]])
                    UAB = ph.sb([128, MT, 4, 256], BF16, "uab")
                    uabb = [T(UAB.h) for _ in range(MT)]
                    for mt in range(MT):
                        for gp in range(2):
                            p = pw.next()
                            for gg in range(2):
                                g = gp * 2 + gg
                                k.mm(p[:, gg * 256:(gg + 1) * 256], uts[:, g, mt * 128:(mt + 1) * 128], AB[:, g, :], True, True,
                                     r=[utb[g], AB], w=[p])
                            if gp == 0:
                                k.actf(UAB[:, mt, 0:2, :], p[:, :].rearrange("p (a b) -> p a b", b=256), AF.Copy,
                                       r=[p], w=[uabb[mt]])
                            else:
                                k.emit(k.dve, lambda: nc.vector.tensor_copy(
                                    out=UAB[:, mt, 2:4, :], in_=p[:, :].rearrange("p (a b) -> p a b", b=256)),
                                    r=[p], w=[uabb[mt]])
                    for nci in range(N // CH):
                        banks = [pacc.next() for _ in range(4)]
                        for mg in range(0, MT, 8):
                            mcount = min(8, MT - mg)
                            ct = cring.next()
                            st_ = sring.next()
                            k.dma(k.sp, ct[:, 0:mcount, 0:CH],
                                  Ct[mg * 128:(mg + mcount) * 128, nci * CH:(nci + 1) * CH].rearrange("(m p) n -> p m n", p=128),
                                  w=[ct])
                            k.dma(k.sp, st_[:, 0:mcount, 0:CH],
                                  St[mg * 128:(mg + mcount) * 128, nci * CH:(nci + 1) * CH].rearrange("(m p) n -> p m n", p=128),
                                  w=[st_])
                            for mi in range(mcount):
                                mt = mg + mi
                                for g in range(4):
                                    k.mm(banks[g][:, 0:CH], UAB[:, mt, g, 0:128], ct[:, mi, 0:CH], mt == 0, False,
                                         r=[uabb[mt], ct], w=[banks[g]])
                                for g in range(4):
                                    k.mm(banks[g][:, 0:CH], UAB[:, mt, g, 128:256], st_[:, mi, 0:CH], False, mt == MT - 1,
                                         r=[uabb[mt], st_], w=[banks[g]],
                                         sig=(mt == MT - 1) or (mi == mcount - 1 and g == 3))
                        for g in range(4):
                            fo = foring.next()
                            k.actf(fo[:, 0:CH], banks[g][:, 0:CH], AF.Copy, r=[banks[g]], w=[fo])
                            k.dma(k.pool, oT[12 + g, :, off + nci * CH:off + (nci + 1) * CH], fo[:, 0:CH], r=[fo])

            with Phase(k, "p4_%d" % l) as ph:
                wo = ph.sb([128, KC, D], BF16, "wo")
                wob = [T(wo.h) for _ in range(KC)]
                wov = w_out[l].rearrange("(kc p) n -> p kc n", p=128)
                for kc in range(KC):
                    k.dma(k.pool, wo[:, kc, :], wov[:, kc, :], w=[wob[kc]])
                identb = ph.sb([128, 128], BF16)
                k.dma(k.sp, identb[:], identb_in[:, :], w=[identb])
                G1 = ph.sb([128, D], F32)
                LG = ph.sb([128, D], F32)
                LB = ph.sb([128, D], F32)
                mT = ph.sb([128, 96 * 2], F32, "mT")
                k.dma(k.sp, mT[:], modT[l].rearrange("p a b -> p (a b)"), w=[mT])
                k.dma(k.sp, LG[:], ln1_g[l:l + 1, :].partition_broadcast(128), w=[LG])
                k.dma(k.sp, LB[:], ln1_b[l:l + 1, :].partition_broadcast(128), w=[LB])
                ocring = ph.sbring(2, [128, KC, 128], BF16, "oc")
                xring = ph.sbring(3, [128, D], F32, "x")
                vring = ph.sbring(2, [128, D], F32, "v")
                xnring = ph.sbring(1, [128, D], F32, "xn")
                x1ring = ph.sbring(3, [128, D], F32, "x1")
                string = ph.sbring(4, [128, 32], F32, "st")
                hbring = ph.sbring(4, [128, D], BF16, "hb")
                hTring = ph.sbring(2, [128, KC, 128], BF16, "hT")
                tp = [ph.ps([128, 1024], BF16, "tp") for _ in range(2)]
                pb = [ph.ps([128, 512], F32, "pb") for _ in range(4)]
                state = {"mod": None}

                def stageA(t):
                    tsl = slice(t * 128, (t + 1) * 128)
                    oc = ocring.next()
                    k.dma(k.sp, oc[:], oT[:, :, tsl].rearrange("c p n -> p c n"), w=[oc])
                    x = xring.next()
                    k.dma(k.sp, x[:], xres[tsl, :], w=[x])
                    return (oc, x)

                def stageB(t, oc, half):
                    if True:
                        for kc in range(KC):
                            for cb in (2 * half, 2 * half + 1):
                                k.mm(pb[cb][:, :], oc[:, kc, :], wo[:, kc, cb * 512:(cb + 1) * 512], kc == 0, kc == KC - 1,
                                     r=[oc, wob[kc]], w=[pb[cb]])

                def stageC0(t):
                    mr = 1 if t >= NTL else 0
                    if state["mod"] != mr:
                        k.dma(k.sp, G1[:], modrow(2, mr), w=[G1])
                        k.ts(k.dve, G1[:, :], G1[:, :], 1.0 / ALPHA, None, ALU.mult, r=[G1], w=[G1])
                        state["mod"] = mr
                    tsl = slice(t * 128, (t + 1) * 128)
                    v = vring.next()
                    for cb in range(0, 2):
                        k.tt(k.dve, v[:, cb * 512:(cb + 1) * 512], pb[cb][:, :], G1[:, cb * 512:(cb + 1) * 512], ALU.mult,
                             r=[pb[cb], G1], w=[v])
                    return v

                def stageCa(t, x, v):
                    tsl = slice(t * 128, (t + 1) * 128)
                    for cb in range(2, 4):
                        k.tt(k.dve, v[:, cb * 512:(cb + 1) * 512], pb[cb][:, :], G1[:, cb * 512:(cb + 1) * 512], ALU.mult,
                             r=[pb[cb], G1], w=[v])
                        yield None
                    k.tt(k.pool, v[:, :], x[:, :], v[:, :], ALU.add, r=[x, v], w=[v])
                    yield None
                    xn = xnring.next()
                    for _ in layer_norm_gen(k, ph, v, xn, string, EPS / (ALPHA * ALPHA)):
                        yield None
                    k.tt(k.dve, xn[:, :], xn[:, :], LG[:, :], ALU.mult, r=[xn, LG], w=[xn])
                    yield None
                    x1 = x1ring.next()
                    k.tt(k.pool, x1[:, :], xn[:, :], LB[:, :], ALU.add, r=[xn, LB], w=[x1])
                    k.dma(k.pool, x1s[tsl, :], x1[:, :], r=[x1])
                    yield x1

                def stageCb(t, x1):
                    hb = hbring.next()
                    for _ in layer_norm_gen(k, ph, x1, hb, string, EPS):
                        yield None
                    yield hb

                def run_interleaved(ga, gb):
                    ra = rb = None
                    da = ga is None
                    db = gb is None
                    while not (da and db):
                        if not da:
                            try:
                                r_ = next(ga)
                                if r_ is not None:
                                    ra = r_
                            except StopIteration:
                                da = True
                        if not db:
                            try:
                                r_ = next(gb)
                                if r_ is not None:
                                    rb = r_
                            except StopIteration:
                                db = True
                    return ra, rb

                def stageD(t, hb):
                    mr = 1 if t >= NTL else 0
                    tsl = slice(t * 128, (t + 1) * 128)
                    hT = hTring.next()
                    for half in range(2):
                        for j in range(8):
                            kc = half * 8 + j
                            k.tr(tp[half][:, j * 128:(j + 1) * 128], hb[:, kc * 128:(kc + 1) * 128], identb[:],
                                 r=[hb, identb], w=[tp[half]], sig=(j == 7))
                        for j in range(8):
                            kc = half * 8 + j
                            k.actf(hT[:, kc, :], tp[half][:, j * 128:(j + 1) * 128], AF.Identity, r=[tp[half], mT], w=[hT],
                                   scale=mT[:, (64 + kc) * 2 + mr:(64 + kc) * 2 + mr + 1],
                                   bias=mT[:, (48 + kc) * 2 + mr:(48 + kc) * 2 + mr + 1])
                    k.dma(k.act, h2T[:, :, tsl].rearrange("c p n -> p c n"), hT[:, :, :], r=[hT])

                curA = stageA(0)
                hbs = {}
                x1prev = None
                for t in range(NTa):
                    stageB(t, curA[0], 0)
                    nxtA = stageA(t + 1) if t + 1 < NTa else None
                    v = stageC0(t)
                    stageB(t, curA[0], 1)
                    if t - 3 in hbs:
                        stageD(t - 3, hbs.pop(t - 3))
                    ga = stageCa(t, curA[1], v)
                    gb = stageCb(t - 1, x1prev) if x1prev is not None else None
                    x1cur, hbp = run_interleaved(ga, gb)
                    if hbp is not None:
                        hbs[t - 1] = hbp
                    x1prev = x1cur
                    curA = nxtA
                _, hbp = run_interleaved(None, stageCb(NTa - 1, x1prev))
                hbs[NTa - 1] = hbp
                for t in sorted(hbs):
                    stageD(t, hbs[t])

            tchunks = [(cq * 512, 512) for cq in range(NQC)] + ([] if last else [(S, 256)])
            NS5 = 11
            FS5 = FC // NS5
            with Phase(k, "p5a_%d" % l) as ph:
                wslots = []
                for i in range(2):
                    wu = ph.sb([128, KC, FS5 * 128], BF16, "wu")
                    wg = ph.sb([128, KC, FS5 * 128], BF16, "wg")
                    wslots.append((wu, wg, [T(wu.h) for _ in range(KC)], [T(wg.h) for _ in range(KC)]))
                wuv = w_up[l].rearrange("(kc p) n -> p kc n", p=128)
                wgv = w_gate[l].rearrange("(kc p) n -> p kc n", p=128)
                cp = ph.sb([128, 4, FC], F32)
                k.dma(k.sp, cp[:], convp[l], w=[cp])
                hring = ph.sbring(2, [128, KC, 514], BF16, "hc")
                gsring = ph.sbring(2, [128, 514], F32, "gs")
                accring = ph.sbring(2, [128, 512], F32, "acc")
                sgring = ph.sbring(2, [128, 512], F32, "sg")
                aring = ph.sbring(2, [128, FS5, 512], BF16, "at")
                pu = ph.psring(2, [128, 512], F32, "pu")
                pg = ph.psring(2, [128, 512], F32, "pg")
                phl = ph.psring(2, [128, 2], F32, "ph")

                def loadw(sl):
                    wu, wg, wub, wgb = wslots[sl % 2]
                    cs0 = sl * FS5 * 128
                    for kc in range(KC):
                        k.dma(k.pool, wg[:, kc, :], wgv[:, kc, cs0:cs0 + FS5 * 128], w=[wgb[kc]])
                        k.dma(k.pool, wu[:, kc, :], wuv[:, kc, cs0:cs0 + FS5 * 128], w=[wub[kc]])

                loadw(0)
                for sl in range(NS5):
                    if sl + 1 < NS5:
                        loadw(sl + 1)
                    wu, wg, wub, wgb = wslots[sl % 2]
                    for (t0, tn) in tchunks:
                        hc = hring.next()
                        lo_valid = (t0 > 0 and t0 < S)
                        hi_valid = (t0 + tn < S)
                        a0 = t0 - (1 if lo_valid else 0)
                        a1 = t0 + tn + (1 if hi_valid else 0)
                        d0 = 0 if lo_valid else 1
                        k.dma(k.sp, hc[:, :, d0:d0 + (a1 - a0)], h2T[:, :, a0:a1].rearrange("c p n -> p c n"), w=[hc])
                        if not lo_valid:
                            k.emit(k.dve, lambda: nc.vector.memset(hc[:, :, 0:1], 0.0), w=[hc])
                        if not hi_valid:
                            k.emit(k.dve, lambda: nc.vector.memset(hc[:, :, tn + 1:tn + 2], 0.0), w=[hc])
                        at = aring.next()
                        for fi in range(FS5):
                            fc = sl * FS5 + fi
                            pU = pu.next()
                            pG = pg.next()
                            pH = phl.next()
                            wsl = slice(fi * 128, (fi + 1) * 128)
                            for kc in range(KC):
                                k.mm(pG[:, 0:tn], wg[:, kc, wsl], hc[:, kc, 1:1 + tn], kc == 0, kc == KC - 1,
                                     r=[wgb[kc], hc], w=[pG])
                            for kc in range(KC):
                                k.mm(pH[:, 0:2], wg[:, kc, wsl], hc[:, kc, 0:tn + 2:tn + 1], kc == 0, kc == KC - 1,
                                     r=[wgb[kc], hc], w=[pH])
                            for kc in range(KC):
                                k.mm(pU[:, 0:tn], wu[:, kc, wsl], hc[:, kc, 1:1 + tn], kc == 0, kc == KC - 1,
                                     r=[wub[kc], hc], w=[pU])
                            gs = gsring.next()
                            k.actf(gs[:, 1:1 + tn], pG[:, 0:tn], AF.Copy, r=[pG], w=[gs])
                            k.actf(gs[:, 0:tn + 2:tn + 1], pH[:, 0:2], AF.Copy, r=[pH], w=[gs])
                            acc = accring.next()
                            k.ts(k.dve, acc[:, 0:tn], gs[:, 1:1 + tn], cp[:, 1, fc:fc + 1], cp[:, 3, fc:fc + 1],
                                 ALU.mult, ALU.add, r=[gs, cp], w=[acc])
                            k.stt(k.dve, acc[:, 0:tn], gs[:, 0:tn], cp[:, 0, fc:fc + 1], acc[:, 0:tn], ALU.mult, ALU.add,
                                  r=[gs, cp, acc], w=[acc])
                            k.stt(k.dve, acc[:, 0:tn], gs[:, 2:2 + tn], cp[:, 2, fc:fc + 1], acc[:, 0:tn], ALU.mult, ALU.add,
                                  r=[gs, cp, acc], w=[acc])
                            sg = sgring.next()
                            k.actf(sg[:, 0:tn], acc[:, 0:tn], AF.Silu, r=[acc], w=[sg])
                            k.tt(k.dve, at[:, fi, 0:tn], sg[:, 0:tn], pU[:, 0:tn], ALU.mult, r=[sg, pU], w=[at])
                        k.dma(k.pool, aT[sl * FS5:(sl + 1) * FS5, :, t0:t0 + tn].rearrange("f p n -> p f n"), at[:, :, 0:tn],
                              r=[at])

            with Phase(k, "p5b_%d" % l) as ph:
                wdh = [ph.sb([128, FC, 512], BF16, "wd") for _ in range(2)]
                wdq = [[T(h.h) for _ in range(4)] for h in wdh]
                ach = [ph.sb([128, FC, 512], BF16, "ac") for _ in range(2)]
                acq = [[T(h.h) for _ in range(4)] for h in ach]
                fring = ph.sbring(3, [128, 512], F32, "f")
                pf = ph.psring(4, [128, 512], F32, "pf")
                wdv = w_down[l].rearrange("(fc p) n -> p fc n", p=128)
                aci = 0
                for cs in range(4):
                    wd = wdh[cs % 2]
                    for q4 in range(4):
                        k.dma(k.pool, wd[:, q4 * 11:(q4 + 1) * 11, :], wdv[:, q4 * 11:(q4 + 1) * 11, cs * 512:(cs + 1) * 512],
                              w=[wdq[cs % 2][q4]])
                    for (t0, tn) in tchunks:
                        ac = ach[aci % 2]
                        aq = acq[aci % 2]
                        aci += 1
                        for q4 in range(4):
                            k.dma(k.sp, ac[:, q4 * 11:(q4 + 1) * 11, 0:tn],
                                  aT[q4 * 11:(q4 + 1) * 11, :, t0:t0 + tn].rearrange("f p n -> p f n"), w=[aq[q4]])
                        for ti in range(tn // 128):
                            p = pf.next()
                            for fc in range(FC):
                                k.mm(p[:, :], ac[:, fc, ti * 128:(ti + 1) * 128], wd[:, fc, :], fc == 0, fc == FC - 1,
                                     r=[aq[fc // 11], wdq[cs % 2][fc // 11]], w=[p])
                            f = fring.next()
                            k.actf(f[:, :], p[:, :], AF.Copy, r=[p], w=[f])
                            r0 = t0 + ti * 128
                            k.dma(k.act, fs[r0:r0 + 128, cs * 512:(cs + 1) * 512], f[:, :], r=[f])

            with Phase(k, "p5c_%d" % l) as ph:
                G2 = ph.sb([128, D], F32)
                LG = ph.sb([128, D], F32)
                LB = ph.sb([128, D], F32)
                k.dma(k.sp, LG[:], ln2_g[l:l + 1, :].partition_broadcast(128), w=[LG])
                k.dma(k.sp, LB[:], ln2_b[l:l + 1, :].partition_broadcast(128), w=[LB])
                x1ring = ph.sbring(2, [128, D], F32, "x1")
                fring = ph.sbring(2, [128, D], F32, "f")
                xnring = ph.sbring(2, [128, D], F32, "xn")
                oring = ph.sbring(2, [128, D], F32, "o")
                string = ph.sbring(2, [128, 32], F32, "st")
                cur_mod = None
                for t in range(NTa):
                    mr = 1 if t >= NTL else 0
                    if cur_mod != mr:
                        k.dma(k.sp, G2[:], modrow(5, mr), w=[G2])
                        k.ts(k.dve, G2[:, :], G2[:, :], 1.0 / ALPHA, None, ALU.mult, r=[G2], w=[G2])
                        cur_mod = mr
                    tsl = slice(t * 128, (t + 1) * 128)
                    x1 = x1ring.next()
                    f = fring.next()
                    k.dma(k.sp, x1[:], x1s[tsl, :], w=[x1])
                    k.dma(k.sp, f[:], fs[tsl, :], w=[f])
                    k.tt(k.dve, f[:, :], f[:, :], G2[:, :], ALU.mult, r=[f, G2], w=[f])
                    k.tt(k.pool, f[:, :], x1[:, :], f[:, :], ALU.add, r=[x1, f], w=[f])
                    xn = xnring.next()
                    layer_norm_rows(k, ph, f, xn, string, eps=EPS / (ALPHA * ALPHA))
                    k.tt(k.dve, xn[:, :], xn[:, :], LG[:, :], ALU.mult, r=[xn, LG], w=[xn])
                    o = oring.next()
                    oa = T(o.h)
                    ob = T(o.h)
                    k.tt(k.dve, o[:, 0:1024], xn[:, 0:1024], LB[:, 0:1024], ALU.add, r=[xn, LB, o], w=[oa])
                    k.tt(k.pool, o[:, 1024:2048], xn[:, 1024:2048], LB[:, 1024:2048], ALU.add, r=[xn, LB, o], w=[ob])
                    o = [oa, ob, o]
                    dst = y_out[tsl, :] if last else xres[tsl, :]
                    k.dma(k.pool, dst, o[2][:, :], r=[o[0], o[1]], w=[o[2]])
    return nc


_CACHE = {}


def _consts(S):
    if S in _CACHE:
        return _CACHE[S]
    bf = ml_dtypes.bfloat16
    t = np.arange(S)
    row = (t // 64).astype(np.float32)
    col = (t % 64).astype(np.float32)
    inv = (10000.0 ** (-np.arange(32, dtype=np.float32) / 32)).astype(np.float32)
    ang = np.concatenate([row[:, None] * inv[None, :], col[:, None] * inv[None, :]], axis=1).astype(np.float32)
    ropec = np.cos(ang).astype(np.float32)
    ropes = np.sin(ang).astype(np.float32)

    def dft(N, scale):
        i = np.arange(N, dtype=np.int64)
        m = (i[:, None] * i[None, :]) % N
        a = (2.0 * np.pi / N) * m.astype(np.float64)
        return (np.cos(a) * scale), (np.sin(a) * scale)

    c, s = dft(S, 1.0 / np.sqrt(S))
    dftc, dfts = c.astype(np.float32).astype(bf), s.astype(np.float32).astype(bf)
    c, s = dft(L, 1.0 / np.sqrt(L))
    dftc_c, dfts_c = c.astype(np.float32).astype(bf), s.astype(np.float32).astype(bf)
    c, s = dft(128, 1.0 / np.sqrt(128.0))
    c128 = c.astype(np.float32)
    ns128 = (-s).astype(np.float32)
    maskb = np.full((128, 6, 512), NEG, np.float32)
    sj = np.arange(128)[:, None]
    qi = np.arange(128)[None, :]
    for r in range(-1, 5):
        for cblk in range(4):
            d = r - cblk
            if d == -1:
                m = np.where(qi <= sj, 0.0, NEG)
            elif d == 0:
                m = np.zeros((128, 128))
            elif d == 1:
                m = np.where(sj <= qi, 0.0, NEG)
            else:
                continue
            maskb[:, r + 1, cblk * 128:(cblk + 1) * 128] = m
    out = dict(ropec=ropec, ropes=ropes, dftc=dftc, dfts=dfts, dftc_c=dftc_c, dfts_c=dfts_c, c128=c128, ns128=ns128,
               maskb=maskb.astype(bf), identb=np.eye(128, dtype=np.float32).astype(bf),
               identf=np.eye(128, dtype=np.float32))
    _CACHE[S] = out
    return out


_NC = {}


def kernel(x, c, ctx, c_ctx, w_mod, b_mod, w_in, q_gain_a, k_gain_a, sink_b, w_fourier, w_out, ln1_g, ln1_b,
           w_up, w_gate, conv_w, conv_b, w_down, ln2_g, ln2_b, _dbg=False):
    f = lambda a: np.ascontiguousarray(np.asarray(a, dtype=np.float32))
    x = f(x)
    B, S, _ = x.shape
    depth = w_mod.shape[0]
    key = (S, depth, _dbg)
    if key not in _NC:
        _NC[key] = build(S, depth, dbg=_dbg)
    nc = _NC[key]
    cs = _consts(S)
    c = f(c)
    ctx = f(ctx)
    c_ctx = f(c_ctx)
    conv_w = f(conv_w)
    conv_b = f(conv_b)
    cp = np.concatenate([conv_w, conv_b[:, None, :]], axis=1)
    convp = np.ascontiguousarray(cp.reshape(depth, 4, FC, 128).transpose(0, 3, 1, 2))
    shared = dict(w_mod=f(w_mod), b_mod=f(b_mod), w_in=f(w_in), q_gain_a=f(q_gain_a), k_gain_a=f(k_gain_a),
                  sink_b=f(sink_b), w_fourier=f(w_fourier), w_out=f(w_out), ln1_g=f(ln1_g), ln1_b=f(ln1_b),
                  w_up=f(w_up), w_gate=f(w_gate), convp=convp, w_down=f(w_down), ln2_g=f(ln2_g), ln2_b=f(ln2_b))
    shared.update(cs)
    in_maps = []
    for b in range(B):
        cc = np.stack([c[b], c_ctx], axis=0)
        ccT = np.ascontiguousarray(cc.reshape(2, KC, 128).transpose(2, 1, 0))
        m = dict(shared)
        m.update(x=x[b], ctx=ctx[b], ccT=ccT)
        in_maps.append(m)
    res = run_bass_kernel_spmd(nc, in_maps, core_ids=list(range(B)))
    if _dbg:
        return res
    return np.stack([np.asarray(r["y"], dtype=np.float32) for r in res.results], axis=0)
```
